# Optimizing a Trainium2 kernel written in Bass

```python
import jax, jax.numpy as jnp
from jax import lax
import numpy as np

D_MODEL = 2048
BATCH = 8
SEQ = 2048
DEPTH = 2

CHUNK = 64
Q_BLOCK = 128
NORM_EPS = 1e-6
ROPE_THETA = 10000.0
MASK_VALUE = -1e30

HG_HEADS = 4
HG_DK = 128
HG_DV = 128
HG_WIDTH = HG_HEADS * HG_DV

SB_HEADS = 6
SB_DH = 128
SB_WIDTH = SB_HEADS * SB_DH

MLA_HEADS = 6
MLA_NOPE = 128
MLA_ROPE = 64
MLA_V = 128
MLA_Q_RANK = 512
MLA_KV_RANK = 256
MLA_WIDTH = MLA_HEADS * MLA_V

MIX_WIDTH = HG_WIDTH + SB_WIDTH + MLA_WIDTH
IN_SIZES = (HG_HEADS * HG_DK, HG_HEADS * HG_DK, HG_WIDTH, HG_WIDTH,
            SB_WIDTH, SB_WIDTH, SB_WIDTH, MLA_Q_RANK, MLA_KV_RANK + MLA_ROPE)
IN_COLS = 4 * HG_WIDTH + 3 * SB_WIDTH + MLA_Q_RANK + MLA_KV_RANK + MLA_ROPE
D_FF = ((8 * D_MODEL + 3 * 256 - 1) // (3 * 256)) * 256

kernel_name = 'hybrid_hgrn2_stickbreak_mla_block'


def rmsnorm(x, g):
    xf = x.astype(jnp.float32)
    y = xf * lax.rsqrt(jnp.mean(xf * xf, axis=-1, keepdims=True) + NORM_EPS)
    return (y * g.astype(jnp.float32)).astype(x.dtype)


def group_rmsnorm(x, g, n_heads):
    b, s, w = x.shape
    xf = x.astype(jnp.float32).reshape(b, s, n_heads, w // n_heads)
    y = xf * lax.rsqrt(jnp.mean(xf * xf, axis=-1, keepdims=True) + NORM_EPS)
    return (y.reshape(b, s, w) * g.astype(jnp.float32)).astype(x.dtype)


def rope_angles(positions):
    inv_freq = ROPE_THETA ** (-jnp.arange(0, MLA_ROPE, 2, dtype=jnp.float32) / MLA_ROPE)
    ang = positions.astype(jnp.float32)[..., None] * inv_freq
    return jnp.cos(ang), jnp.sin(ang)


def apply_rope(x, cos, sin):
    xf = x.astype(jnp.float32)
    x1, x2 = xf[..., : MLA_ROPE // 2], xf[..., MLA_ROPE // 2:]
    return jnp.concatenate([x1 * cos - x2 * sin, x2 * cos + x1 * sin], axis=-1).astype(x.dtype)


def hgrn2_mixer(q, f_pre, v, lower_bound):
    b, s, _ = q.shape
    n_chunks = s // CHUNK
    lb = lower_bound.astype(jnp.float32)
    fp = f_pre.astype(jnp.float32)
    f = lb + (1.0 - lb) * jax.nn.sigmoid(fp)
    log_f = jnp.log(f)
    k = (1.0 - lb) * jax.nn.sigmoid(-fp)

    def to_chunks(t, d):
        return t.astype(jnp.float32).reshape(b, n_chunks, CHUNK, HG_HEADS, d).transpose(1, 0, 3, 2, 4)

    qc, kc, vc = to_chunks(q, HG_DK), to_chunks(k, HG_DK), to_chunks(v, HG_DV)
    gc = jnp.cumsum(to_chunks(log_f, HG_DK), axis=3)
    causal = jnp.tril(jnp.ones((CHUNK, CHUNK), dtype=bool))[:, :, None]

    def step(state, inp):
        q_c, k_c, v_c, g_c = inp
        diff = g_c[:, :, :, None, :] - g_c[:, :, None, :, :]
        decay = jnp.where(causal, jnp.exp(jnp.where(causal, diff, 0.0)), 0.0)
        scores = jnp.einsum('bhtk,bhsk,bhtsk->bhts', q_c, k_c, decay)
        o = jnp.einsum('bhts,bhsv->bhtv', scores, v_c) + jnp.einsum('bhtk,bhkv->bhtv', q_c * jnp.exp(g_c), state)
        g_last = g_c[:, :, -1:, :]
        state = jnp.exp(g_last)[:, :, 0, :, None] * state + jnp.einsum('bhsk,bhsv->bhkv', k_c * jnp.exp(g_last - g_c), v_c)
        return state, o

    s0 = jnp.zeros((b, HG_HEADS, HG_DK, HG_DV), jnp.float32)
    _, o = lax.scan(step, s0, (qc, kc, vc, gc))
    return o.transpose(1, 0, 3, 2, 4).reshape(b, s, HG_WIDTH).astype(q.dtype)


def stick_breaking_mixer(q, k, v):
    b, s, h, d = q.shape
    scale = d ** -0.5
    outs = []
    for blk in range(s // Q_BLOCK):
        t0, t1 = blk * Q_BLOCK, (blk + 1) * Q_BLOCK
        z = jnp.einsum('bthd,bshd->bhts', q[:, t0:t1], k[:, :t1]).astype(jnp.float32) * scale
        strict = jnp.arange(t1)[None, :] < jnp.arange(t0, t1)[:, None]
        log_keep = jnp.where(strict, jax.nn.log_sigmoid(-z), 0.0)
        between = lax.cumsum(log_keep, axis=3, reverse=True) - log_keep
        a = jnp.where(strict, jnp.exp(jnp.where(strict, jax.nn.log_sigmoid(z) + between, 0.0)), 0.0)
        outs.append(jnp.einsum('bhts,bshd->bthd', a, v[:, :t1].astype(jnp.float32)))
    return jnp.concatenate(outs, axis=1).reshape(b, s, h * d).astype(q.dtype)


def mla_mixer(c_q, c_kv_rope, positions, q_norm_g, w_uq, kv_norm_g, w_ukv):
    b, s, _ = c_q.shape
    q = (rmsnorm(c_q, q_norm_g) @ w_uq).reshape(b, s, MLA_HEADS, MLA_NOPE + MLA_ROPE)
    q_nope, q_rope = q[..., :MLA_NOPE], q[..., MLA_NOPE:]
    c_kv, k_rope = c_kv_rope[..., :MLA_KV_RANK], c_kv_rope[..., MLA_KV_RANK:]
    kv = (rmsnorm(c_kv, kv_norm_g) @ w_ukv).reshape(b, s, MLA_HEADS, MLA_NOPE + MLA_V)
    k_nope, v = kv[..., :MLA_NOPE], kv[..., MLA_NOPE:]
    cos, sin = rope_angles(positions)
    q_rope = apply_rope(q_rope, cos[:, :, None, :], sin[:, :, None, :])
    k_rope = apply_rope(k_rope, cos, sin)
    scale = (MLA_NOPE + MLA_ROPE) ** -0.5
    chunk_id = jnp.arange(s) // CHUNK
    outs = []
    for blk in range(s // Q_BLOCK):
        t0, t1 = blk * Q_BLOCK, (blk + 1) * Q_BLOCK
        sc = (jnp.einsum('bthd,bshd->bhts', q_nope[:, t0:t1], k_nope[:, :t1])
              + jnp.einsum('bthr,bsr->bhts', q_rope[:, t0:t1], k_rope[:, :t1])).astype(jnp.float32) * scale
        allowed = chunk_id[None, :t1] <= chunk_id[t0:t1, None]
        p = jax.nn.softmax(jnp.where(allowed, sc, MASK_VALUE), axis=-1)
        outs.append(jnp.einsum('bhts,bshd->bthd', p, v[:, :t1].astype(jnp.float32)))
    return jnp.concatenate(outs, axis=1).reshape(b, s, MLA_WIDTH).astype(c_q.dtype)


def setup_inputs(seed: int = 0) -> dict:
    key = jax.random.key(seed)
    ks = jax.random.split(key, 20)
    f32 = jnp.float32

    def w(k, shape, fan_in):
        return jax.random.normal(k, shape, f32) * (fan_in ** -0.5)

    def gain(k, shape):
        return 1.0 + 0.02 * jax.random.normal(k, shape, f32)

    x = jax.random.normal(ks[0], (BATCH, SEQ, D_MODEL), f32)
    offsets = jax.random.randint(ks[1], (BATCH, 1), 0, 64, dtype=jnp.int32) * CHUNK
    positions = (offsets + jnp.arange(SEQ, dtype=jnp.int32)[None, :]).astype(jnp.int32)
    return {
        'x': x,
        'positions': positions,
        'attn_norm_g': gain(ks[2], (DEPTH, D_MODEL)),
        'w_in': w(ks[3], (DEPTH, D_MODEL, IN_COLS), D_MODEL),
        'hg_lower_bounds': 1.0 + 0.1 * jax.random.normal(ks[4], (DEPTH, HG_HEADS * HG_DK), f32),
        'hg_norm_g': gain(ks[5], (DEPTH, HG_WIDTH)),
        'sb_norm_g': gain(ks[6], (DEPTH, SB_WIDTH)),
        'mla_q_norm_g': gain(ks[7], (DEPTH, MLA_Q_RANK)),
        'mla_w_uq': w(ks[8], (DEPTH, MLA_Q_RANK, MLA_HEADS * (MLA_NOPE + MLA_ROPE)), MLA_Q_RANK),
        'mla_kv_norm_g': gain(ks[9], (DEPTH, MLA_KV_RANK)),
        'mla_w_ukv': w(ks[10], (DEPTH, MLA_KV_RANK, MLA_HEADS * (MLA_NOPE + MLA_V)), MLA_KV_RANK),
        'mla_out_norm_g': gain(ks[11], (DEPTH, MLA_WIDTH)),
        'w_o': w(ks[12], (DEPTH, MIX_WIDTH, D_MODEL), MIX_WIDTH),
        'ffn_norm_g': gain(ks[13], (DEPTH, D_MODEL)),
        'w_gate': w(ks[14], (DEPTH, D_MODEL, D_FF), D_MODEL),
        'w_up': w(ks[15], (DEPTH, D_MODEL, D_FF), D_MODEL),
        'w_down': w(ks[16], (DEPTH, D_FF, D_MODEL), D_FF),
        'final_norm_g': gain(ks[17], (D_MODEL,)),
    }


def reference(x, positions, attn_norm_g, w_in, hg_lower_bounds, hg_norm_g, sb_norm_g,
              mla_q_norm_g, mla_w_uq, mla_kv_norm_g, mla_w_ukv, mla_out_norm_g, w_o,
              ffn_norm_g, w_gate, w_up, w_down, final_norm_g):
    b, s, _ = x.shape
    sm = jax.nn.softmax(hg_lower_bounds.astype(jnp.float32), axis=0)
    lower_bounds = jnp.cumsum(sm, axis=0) - sm[0:1]
    split_at = np.cumsum(IN_SIZES)[:-1].tolist()
    for layer in range(DEPTH):
        h = rmsnorm(x, attn_norm_g[layer])
        hq, hf, hi, hg, sq, sk, sv, c_q, c_kv_rope = jnp.split(h @ w_in[layer], split_at, axis=-1)

        hg_out = hgrn2_mixer(hq, hf, hi, lower_bounds[layer])
        hg_out = group_rmsnorm(hg_out, hg_norm_g[layer], HG_HEADS) * jax.nn.silu(hg)

        sb_out = stick_breaking_mixer(sq.reshape(b, s, SB_HEADS, SB_DH), sk.reshape(b, s, SB_HEADS, SB_DH),
                                      sv.reshape(b, s, SB_HEADS, SB_DH))
        sb_out = group_rmsnorm(sb_out, sb_norm_g[layer], SB_HEADS)

        mla_out = mla_mixer(c_q, c_kv_rope, positions, mla_q_norm_g[layer], mla_w_uq[layer],
                            mla_kv_norm_g[layer], mla_w_ukv[layer])
        mla_out = group_rmsnorm(mla_out, mla_out_norm_g[layer], MLA_HEADS)

        x = x + jnp.concatenate([hg_out, sb_out, mla_out], axis=-1) @ w_o[layer]

        h = rmsnorm(x, ffn_norm_g[layer])
        x = x + (jax.nn.silu(h @ w_gate[layer]) * (h @ w_up[layer])) @ w_down[layer]
    return rmsnorm(x, final_norm_g)
```

```python
import math
import numpy as np
import concourse.bass as bass
import concourse.mybir as mybir
from concourse.bass_utils import run_bass_kernel_spmd

F32 = mybir.dt.float32
BF16 = mybir.dt.bfloat16
I32 = mybir.dt.int32
AF = mybir.ActivationFunctionType
ALU = mybir.AluOpType

S = 2048
D = 2048
DEPTH = 2
NCORES = 8
EPS = 1e-6
IN_COLS = 5184
D_FF = 5632
NFC = D_FF // 128
P = 128

_ESZ = {str(F32): 4, str(BF16): 2, str(I32): 4}


def _esz(dt):
    return _ESZ[str(dt)]


def _box(ap):
    t = ap.tensor
    dims = [(int(s), int(c)) for s, c in ap.ap]
    es = _esz(ap.dtype)
    off = int(ap.offset)
    if type(t).__name__.startswith("DRam"):
        ext = sum((c - 1) * abs(s) for s, c in dims)
        return (t.name, 0, 1, off * es, (off + ext + 1) * es)
    pstep, pc = dims[0]
    if pstep == 0:
        pstep = 1 << 40
    p0 = off // pstep
    f0 = off % pstep
    ext = sum((c - 1) * abs(s) for s, c in dims[1:])
    return (t.name, p0, p0 + pc, f0 * es, (f0 + ext + 1) * es)


class Sched:
    ENG = ("pe", "dve", "act", "pool", "sp")

    def __init__(self, nc):
        self.nc = nc
        self.ops = []
        self.recs = {}
        self.psum_last = {}
        self.out_dmas = []

    def _access(self, box, idx, w, eng, dma, deps):
        name, p0, p1, f0, f1 = box
        recs = self.recs.get(name)
        if recs is None:
            recs = self.recs[name] = []
        new = []
        for r in recs:
            rp0, rp1, rf0, rf1, ridx, rw, reng, rdma = r
            ov = rp0 < p1 and p0 < rp1 and rf0 < f1 and f0 < rf1
            if ov and (w or rw) and ridx != idx:
                deps.add(ridx)
            if ridx == idx:
                new.append(r)
                continue
            if w and rp0 >= p0 and rp1 <= p1 and rf0 >= f0 and rf1 <= f1:
                continue
            if (not w) and (not rw) and reng == eng and (not dma) and (not rdma) \
                    and rp0 == p0 and rp1 == p1 and rf0 == f0 and rf1 == f1:
                continue
            new.append(r)
        new.append((p0, p1, f0, f1, idx, w, eng, dma))
        self.recs[name] = new

    def _psum_access(self, name, idx, w, eng, deps):
        r = self.psum_last.get(name)
        if r is not None and r[0] != idx:
            ridx, rw, reng = r
            if not (reng == eng and not w and not rw):
                deps.add(ridx)
        if r is not None and r[0] == idx:
            self.psum_last[name] = (idx, w or r[1], eng)
        else:
            self.psum_last[name] = (idx, w, eng)

    def add(self, eng, emit, reads, writes, dma=False, is_out=False):
        idx = len(self.ops)
        deps = set()
        for ap in reads:
            if type(ap.tensor).__name__.startswith("PSum"):
                self._psum_access(ap.tensor.name, idx, False, eng, deps)
            else:
                self._access(_box(ap), idx, False, eng, dma, deps)
        for ap in writes:
            if type(ap.tensor).__name__.startswith("PSum"):
                self._psum_access(ap.tensor.name, idx, True, eng, deps)
            else:
                self._access(_box(ap), idx, True, eng, dma, deps)
        op = dict(idx=idx, eng=eng, emit=emit, deps=deps, dma=dma, signal=False, sem=None, val=0)
        self.ops.append(op)
        if is_out:
            self.out_dmas.append(idx)
        return idx

    def mm(self, out, lhsT, rhs, start=True, stop=True):
        self.add("pe", lambda e: e.matmul(out, lhsT, rhs, start=start, stop=stop), [lhsT, rhs], [out])

    def tr(self, out, in_, ident):
        self.add("pe", lambda e: e.transpose(out, in_, ident), [in_, ident], [out])

    def act(self, out, in_, func, scale=None, bias=None):
        kw = {}
        rd = [in_]
        if scale is not None:
            kw["scale"] = scale
            if not isinstance(scale, (int, float)):
                rd.append(scale)
        if bias is not None:
            kw["bias"] = bias
            if not isinstance(bias, (int, float)):
                rd.append(bias)
        self.add("act", lambda e: e.activation(out, in_, func, **kw), rd, [out])

    def _veng(self, eng):
        return eng

    def tt(self, eng, out, in0, in1, op):
        self.add(eng, lambda e: e.tensor_tensor(out, in0, in1, op), [in0, in1], [out])

    def ts(self, eng, out, in0, s1, s2, op0, op1=None):
        rd = [in0]
        if not isinstance(s1, (int, float)):
            rd.append(s1)
        if s2 is not None and not isinstance(s2, (int, float)):
            rd.append(s2)
        if op1 is None:
            self.add(eng, lambda e: e.tensor_scalar(out, in0, s1, None, op0), rd, [out])
        else:
            self.add(eng, lambda e: e.tensor_scalar(out, in0, s1, s2, op0, op1), rd, [out])

    def stt(self, out, in0, scalar, in1, op0, op1):
        rd = [in0, in1]
        if not isinstance(scalar, (int, float)):
            rd.append(scalar)
        self.add("dve", lambda e: e.scalar_tensor_tensor(out, in0, scalar, in1, op0, op1), rd, [out])

    def scan(self, out, d0, d1, initial, op0, op1):
        self.add("dve", lambda e: e.tensor_tensor_scan(out, d0, d1, initial, op0, op1), [d0, d1], [out])

    def copy(self, eng, out, in_):
        if eng == "act":
            self.add("act", lambda e: e.copy(out, in_), [in_], [out])
        else:
            self.add(eng, lambda e: e.tensor_copy(out, in_), [in_], [out])

    def memset(self, eng, ap, val):
        self.add(eng, lambda e: e.memset(ap, val), [], [ap])

    def recip(self, out, in_):
        self.add("dve", lambda e: e.reciprocal(out, in_), [in_], [out])

    def dma(self, eng, out, in_, is_out=False):
        self.add(eng, lambda e: e.dma_start(out=out, in_=in_), [in_], [out], dma=True, is_out=is_out)

    def emit_all(self):
        nc = self.nc
        ops = self.ops
        NPOOL = 24
        dsem = [nc.alloc_semaphore("dq%d" % i) for i in range(NPOOL)]
        dcnt = [0] * NPOOL
        dlast = [None] * NPOOL
        nd = 0
        for op in ops:
            if op["dma"]:
                k = nd % NPOOL
                nd += 1
                if dlast[k] is not None:
                    op["deps"].add(dlast[k])
                dcnt[k] += 16
                op["sem"] = dsem[k]
                op["val"] = dcnt[k]
                op["semid"] = ("d", k)
                dlast[k] = op["idx"]
        for op in ops:
            for d in op["deps"]:
                dop = ops[d]
                if dop["dma"]:
                    continue
                if dop["eng"] == "pe" and op["eng"] == "pe" and not op["dma"]:
                    continue
                dop["signal"] = True
        MAXC = 30000
        cur = {}
        for op in ops:
            if op["dma"] or not op["signal"]:
                continue
            e = op["eng"]
            if e not in cur or cur[e][1] >= MAXC:
                gen = 0 if e not in cur else cur[e][2] + 1
                cur[e] = [nc.alloc_semaphore("c_%s_%d" % (e, gen)), 0, gen]
            cur[e][1] += 1
            op["sem"] = cur[e][0]
            op["val"] = cur[e][1]
            op["semid"] = ("c", e, cur[e][2])

        handles = {"pe": "tensor", "dve": "vector", "act": "scalar", "pool": "gpsimd", "sp": "sync"}

        def emit_engine(ename, e):
            waited = {}
            last = None
            for op in ops:
                if op["eng"] != ename:
                    continue
                need = {}
                for d in op["deps"]:
                    dop = ops[d]
                    if (not dop["dma"]) and dop["eng"] == "pe" and ename == "pe" and not op["dma"]:
                        continue
                    sid = dop["semid"]
                    if waited.get(sid, 0) >= dop["val"]:
                        continue
                    if sid not in need or need[sid][1] < dop["val"]:
                        need[sid] = (dop["sem"], dop["val"])
                for sid, (sem, val) in need.items():
                    e.wait_ge(sem, val)
                    waited[sid] = val
                ins = op["emit"](e)
                if op["dma"]:
                    ins.then_inc(op["sem"], 16)
                elif op["signal"]:
                    ins.then_inc(op["sem"], 1)
            if ename == "sp":
                fin = {}
                for d in self.out_dmas:
                    dop = ops[d]
                    sid = dop["semid"]
                    if sid not in fin or fin[sid][1] < dop["val"]:
                        fin[sid] = (dop["sem"], dop["val"])
                for sid, (sem, val) in fin.items():
                    if waited.get(sid, 0) < val:
                        e.wait_ge(sem, val)

        with nc.Block() as block:
            @block.tensor
            def _(e):
                emit_engine("pe", e)

            @block.vector
            def _(e):
                emit_engine("dve", e)

            @block.scalar
            def _(e):
                emit_engine("act", e)

            @block.gpsimd
            def _(e):
                emit_engine("pool", e)

            @block.sync
            def _(e):
                emit_engine("sp", e)


class Arena:
    def __init__(self, t, nbytes):
        self.t = t
        self.nbytes = nbytes
        self.off = 0

    def reset(self):
        self.off = 0

    def alloc(self, shape, dt):
        es = _esz(dt)
        n = 1
        for s in shape[1:]:
            n *= s
        nb = (n * es + 63) // 64 * 64
        assert self.off + nb <= self.nbytes, ("arena overflow", self.off, nb, self.nbytes)
        a = self.t[0:shape[0], self.off // 2:(self.off + n * es) // 2]
        self.off += nb
        if dt != BF16:
            a = a.bitcast(dt)
        if len(shape) == 3:
            a = a.rearrange("p (a b) -> p a b", a=shape[1])
        elif len(shape) == 4:
            a = a.rearrange("p (a b c) -> p a b c", a=shape[1], b=shape[2])
        return a


CST_IDENT = 0
CST_ONES = 128
CST_NEGTRI = 256
CST_STRICT = 384
CST_NEGONES = 512
CST_NB = 640
CST_HGMASK = 640
CST_FOLD = 768
CST_CMASK = 896
CST_INVF = 900
CST_PHASE = 901
CST_SIGN = 902
CST_N = 904

VPL = 58
V_ATTN, V_FFN, V_HGG, V_SBG, V_MLAOG, V_MLAQG, V_MLAKVG, V_LB = 0, 16, 32, 36, 42, 48, 52, 54
V_FINAL = 2 * VPL
V_N = 2 * VPL + 16


def _make_cst():
    c = np.zeros((128, CST_N), np.float32)
    i = np.arange(128)
    c[:, CST_IDENT:CST_IDENT + 128] = np.eye(128, dtype=np.float32)
    c[:, CST_ONES:CST_ONES + 128] = 1.0
    c[:, CST_NEGTRI:CST_NEGTRI + 128] = -(i[:, None] >= i[None, :]).astype(np.float32)
    c[:, CST_STRICT:CST_STRICT + 128] = (i[:, None] < i[None, :]).astype(np.float32)
    c[:, CST_NEGONES:CST_NEGONES + 128] = -1.0
    c[:, CST_HGMASK:CST_HGMASK + 128] = ((i[:, None] // 32 == i[None, :] // 32) & (i[:, None] <= i[None, :])).astype(np.float32)
    c[:, CST_FOLD:CST_FOLD + 128] = (i[:, None] % 64 == i[None, :] % 64).astype(np.float32)
    for k in range(4):
        c[:, CST_CMASK + k] = (i // 32 == k).astype(np.float32)
    inv_freq = (np.float32(10000.0) ** (-np.arange(0, 64, 2, dtype=np.float32) / np.float32(64))).astype(np.float32)
    c[:, CST_INVF] = inv_freq[i % 32]
    c[:, CST_PHASE] = np.where(i < 64, np.float32(math.pi / 2), np.float32(0.0))
    c[:, CST_SIGN] = np.where((i >= 64) & (i < 96), -1.0, 1.0)
    return c


def build_program(debug=False, nlayers=DEPTH, stages=None):
    nc = bass.Bass("TRN2", target_bir_lowering=False)
    sc = Sched(nc)

    def dram_in(name, shape, dt=F32):
        return nc.dram_tensor(name, list(shape), dt, kind="ExternalInput").ap()

    need = set(stages or ("norm", "hg", "sb", "mla", "wo", "ffn", "final"))
    xT_in = dram_in("xT", [D, S])
    pos_in = dram_in("pos", [1, S], I32)
    vec_in = dram_in("vec", [128, V_N])
    cst_in = dram_in("cst", [128, CST_N])
    w_in = dram_in("w_in", [DEPTH, D, IN_COLS]) if need & {"hg", "sb", "mla"} else None
    w_uq = dram_in("mla_w_uq", [DEPTH, 512, 1152]) if "mla" in need else None
    w_ukv = dram_in("mla_w_ukv", [DEPTH, 256, 1536]) if "mla" in need else None
    w_o = dram_in("w_o", [DEPTH, D, D]) if "wo" in need else None
    w_gate = dram_in("w_gate", [DEPTH, D, D_FF]) if "ffn" in need else None
    w_up = dram_in("w_up", [DEPTH, D, D_FF]) if "ffn" in need else None
    w_down = dram_in("w_down", [DEPTH, D_FF, D]) if "ffn" in need else None
    outT = nc.dram_tensor("outT", [D, S], F32, kind="ExternalOutput").ap()
    xres = nc.dram_tensor("xres", [D, S], F32, kind="Internal").ap()
    mixd = nc.dram_tensor("mixd", [D, S], BF16, kind="Internal").ap()
    dbg = {}
    if debug:
        st_ = set(stages or ())
        if "norm" in st_:
            dbg["hT"] = nc.dram_tensor("dbg_hT", [D, S], BF16, kind="ExternalOutput").ap()
        if st_ & {"hg", "sb", "mla"}:
            dbg["mix"] = nc.dram_tensor("dbg_mix", [D, S], BF16, kind="ExternalOutput").ap()
        if "wo" in st_:
            dbg["x1"] = nc.dram_tensor("dbg_x1", [D, S], F32, kind="ExternalOutput").ap()
        if "ffn" in st_:
            dbg["x2"] = nc.dram_tensor("dbg_x2", [D, S], F32, kind="ExternalOutput").ap()
        dbg["cs"] = nc.dram_tensor("dbg_cs", [128, S], F32, kind="ExternalOutput").ap()
        dbg["lbt"] = nc.dram_tensor("dbg_lbt", [128, 32], F32, kind="ExternalOutput").ap()

    HT = nc.alloc_sbuf_tensor("HT", [128, 16, S], BF16)
    ATT_BYTES = 68 * 1024
    ATTt = nc.alloc_sbuf_tensor("ATT", [128, ATT_BYTES // 2], BF16)
    att = Arena(ATTt, ATT_BYTES)
    NSLOT = 3
    WS = [nc.alloc_sbuf_tensor("WS%d" % i, [128, 8192], BF16) for i in range(NSLOT)]
    CS = nc.alloc_sbuf_tensor("CS", [128, S], F32)
    VEC = nc.alloc_sbuf_tensor("VEC", [128, V_N], F32)
    CSTF = nc.alloc_sbuf_tensor("CSTF", [128, CST_N], F32)
    CSTB = nc.alloc_sbuf_tensor("CSTB", [128, CST_NB], BF16)
    LBT = nc.alloc_sbuf_tensor("LBT", [128, 32], F32)

    PS = [nc.alloc_psum_tensor("PS%d" % i, [128, 512], F32) for i in range(7)]
    PSB = nc.alloc_psum_tensor("PSB", [128, 1024], BF16)

    ident_b = CSTB[:, 0:128]
    ones_b = CSTB[:, 128:256]
    negtri_b = CSTB[:, 256:384]
    strict_b = CSTB[:, 384:512]
    negones_b = CSTB[:, 512:640]
    ones_f = CSTF[:, CST_ONES:CST_ONES + 128]
    hgmask_f = CSTF[:, CST_HGMASK:CST_HGMASK + 128]
    fold_f = CSTF[:, CST_FOLD:CST_FOLD + 128]

    def vcol(layer, base, j):
        c = layer * VPL + base + j
        return VEC[:, c:c + 1]

    sc.dma("sp", VEC[:, :], vec_in)
    sc.dma("sp", CSTF[:, :], cst_in)
    sc.dma("pool", CSTB[:, :], cst_in[:, 0:CST_NB])

    r0 = VEC[:, V_LB:V_LB + 4]
    r1 = VEC[:, VPL + V_LB:VPL + V_LB + 4]
    sc.tt("dve", LBT[:, 24:28], r0, r1, ALU.subtract)
    sc.act(LBT[:, 0:4], LBT[:, 24:28], AF.Sigmoid)
    sc.act(LBT[:, 4:8], LBT[:, 24:28], AF.Sigmoid, scale=-1.0)
    sc.tt("dve", LBT[:, 8:12], LBT[:, 0:4], LBT[:, 0:4], ALU.subtract)
    sc.tt("dve", LBT[:, 28:32], LBT[:, 0:4], LBT[:, 4:8], ALU.add)
    sc.tt("dve", LBT[:, 12:16], LBT[:, 28:32], LBT[:, 0:4], ALU.subtract)
    sc.ts("dve", LBT[:, 16:24], LBT[:, 8:16], -1.0, 1.0, ALU.mult, ALU.add)

    def lbcol(layer, h):
        return LBT[:, 8 + 4 * layer + h:9 + 4 * layer + h]

    def omlcol(layer, h):
        return LBT[:, 16 + 4 * layer + h:17 + 4 * layer + h]

    att.reset()
    posi = att.alloc([128, S], I32)
    ang = att.alloc([128, S], F32)
    kq = att.alloc([128, S], F32)
    kqi = att.alloc([128, S], I32)
    sc.dma("sp", posi, pos_in.partition_broadcast(128))
    sc.copy("dve", ang, posi)
    sc.ts("dve", ang, ang, CSTF[:, CST_INVF:CST_INVF + 1], CSTF[:, CST_PHASE:CST_PHASE + 1], ALU.mult, ALU.add)
    sc.ts("dve", kq, ang, float(1.0 / (2 * math.pi)), None, ALU.mult)
    sc.copy("dve", kqi, kq)
    sc.copy("dve", kq, kqi)
    C1 = 6.28125
    C2 = float(2 * math.pi - 6.28125)
    sc.stt(ang, kq, -C1, ang, ALU.mult, ALU.add)
    sc.stt(ang, kq, -C2, ang, ALU.mult, ALU.add)
    sc.ts("dve", kq, ang, float(math.pi), float(-2 * math.pi), ALU.is_gt, ALU.mult)
    sc.tt("dve", ang, ang, kq, ALU.add)
    sc.ts("dve", kq, ang, float(-math.pi), float(2 * math.pi), ALU.is_lt, ALU.mult)
    sc.tt("dve", ang, ang, kq, ALU.add)
    sc.ts("dve", ang, ang, float(math.pi), float(-math.pi), ALU.min, ALU.max)
    sc.act(CS[:, :], ang, AF.Sin)
    sc.ts("dve", CS[:, :], CS[:, :], CSTF[:, CST_SIGN:CST_SIGN + 1], None, ALU.mult)

    if debug:
        sc.dma("sp", dbg["cs"], CS[:, :], is_out=True)
        sc.dma("sp", dbg["lbt"], LBT[:, :], is_out=True)

    jobs = []

    def job(load, compute):
        jobs.append((load, compute))

    def wview(slot, shape):
        n = 1
        for s in shape:
            n *= s
        a = slot[:, 0:n]
        if len(shape) == 2:
            return a.rearrange("p (a b) -> p a b", a=shape[0])
        if len(shape) == 3:
            return a.rearrange("p (a b c) -> p a b c", a=shape[0], b=shape[1])
        return a

    def w_in_cols(layer, a, b):
        return w_in[layer].rearrange("(c p) n -> p c n", p=128)[:, :, a:b]

    def proj_f(ps, wfn, tg, nk=16, rhs_fn=None):
        for c in range(nk):
            rhs = HT[:, c, tg * 512:(tg + 1) * 512] if rhs_fn is None else rhs_fn(c)
            sc.mm(ps, wfn(c), rhs, start=(c == 0), stop=(c == nk - 1))

    def norm_stage(src, gcol_fn, dst_dram=None):
        att.reset()
        XB = [att.alloc([128, 512], F32) for _ in range(4)]
        SQ = [att.alloc([128, 512], F32) for _ in range(2)]
        RS = att.alloc([128, 512], F32)
        RINV = att.alloc([128, 512], F32)
        k = 0
        for tg in range(4):
            cols = slice(tg * 512, (tg + 1) * 512)
            st = PS[6]
            for c in range(16):
                xb = XB[k % 4]
                k += 1
                sc.dma("sp", xb, src[c * 128:(c + 1) * 128, cols])
                sq = SQ[c % 2]
                sc.act(sq, xb, AF.Square)
                sc.mm(st[:, :], ones_f, sq, start=(c == 0), stop=(c == 15))
            sc.act(RS, st[:, :], AF.Sqrt, scale=1.0 / D, bias=EPS)
            sc.recip(RINV, RS)
            xbs = {}

            def ld(c):
                nonlocal k
                xbs[c] = XB[k % 4]
                k += 1
                sc.dma("sp", xbs[c], src[c * 128:(c + 1) * 128, cols])

            ld(0)
            ld(1)
            for c in range(16):
                xb = xbs[c]
                if dst_dram is None:
                    sc.stt(HT[:, c, cols], xb, gcol_fn(c), RINV, ALU.mult, ALU.mult)
                else:
                    sc.stt(xb, xb, gcol_fn(c), RINV, ALU.mult, ALU.mult)
                    sc.dma("sp", dst_dram[c * 128:(c + 1) * 128, cols], xb, is_out=True)
                if c + 2 < 16:
                    ld(c + 2)

    def group_norm_out(o_ap, width, gcol, dst, tmp_sq, tmp_rs, tmp_rinv, st_ps, extra_mul=None, tmp2=None):
        sc.act(tmp_sq[:, 0:width], o_ap, AF.Square)
        sc.mm(st_ps[:, 0:width], ones_f, tmp_sq[:, 0:width], start=True, stop=True)
        sc.act(tmp_rs[:, 0:width], st_ps[:, 0:width], AF.Sqrt, scale=1.0 / 128.0, bias=EPS)
        sc.recip(tmp_rinv[:, 0:width], tmp_rs[:, 0:width])
        if extra_mul is None:
            sc.stt(dst, o_ap, gcol, tmp_rinv[:, 0:width], ALU.mult, ALU.mult)
        else:
            sc.stt(tmp2[:, 0:width], o_ap, gcol, tmp_rinv[:, 0:width], ALU.mult, ALU.mult)
            sc.tt("dve", dst, tmp2[:, 0:width], extra_mul, ALU.mult)

    def hg_stage(layer):
        att.reset()
        T1 = att.alloc([128, S], F32)
        T2 = att.alloc([128, S], F32)
        T3 = att.alloc([128, S], F32)
        SM = att.alloc([128, S], F32)
        Qt = att.alloc([128, S], BF16)
        Kt = att.alloc([128, S], BF16)
        Kh = att.alloc([128, S], BF16)
        V = att.alloc([128, 16, 128], BF16)
        OUT = att.alloc([128, S], BF16)
        EGL = att.alloc([128, 64], F32)
        KM = [att.alloc([128, 4, 128], BF16) for _ in range(2)]
        SC_ = [att.alloc([128, 128], BF16) for _ in range(2)]
        ST = att.alloc([128, 128], F32)
        STB = att.alloc([128, 128], BF16)
        TSQ = att.alloc([128, 128], F32)
        TRS = att.alloc([128, 128], F32)
        TRI = att.alloc([128, 128], F32)
        TSG = att.alloc([128, 128], F32)
        TO = att.alloc([128, 128], F32)
        for h in range(4):
            def load(slot, h=h):
                W = wview(slot, [16, 4, 128])
                for j in range(4):
                    a = j * 512 + h * 128
                    sc.dma("pool", W[:, :, j, :], w_in_cols(layer, a, a + 128))

            def compute(slot, h=h):
                W = wview(slot, [16, 4, 128])
                if h == 0:
                    sc.memset("pool", SM, 1.0)
                    sc.memset("pool", SM.rearrange("p (c t) -> p c t", t=32)[:, :, 0:1], 0.0)
                for tg in range(4):
                    cols = slice(tg * 512, (tg + 1) * 512)
                    ps = PS[tg % 2]
                    proj_f(ps[:, :], lambda c: W[:, c, 1, :], tg)
                    sc.act(T1[:, cols], ps[:, :], AF.Sigmoid)
                    sc.ts("dve", T1[:, cols], T1[:, cols], omlcol(layer, h), lbcol(layer, h), ALU.mult, ALU.add)
                    sc.ts("dve", T2[:, cols], T1[:, cols], -1.0, 1.0, ALU.mult, ALU.add)
                    sc.act(T1[:, cols], T1[:, cols], AF.Ln)
                sc.scan(T3, SM, T1, 0.0, ALU.mult, ALU.add)
                sc.act(T1, T3, AF.Exp)
                sc.copy("dve", EGL, T1.rearrange("p (c t) -> p c t", t=32)[:, :, 31])
                for tg in range(4):
                    cols = slice(tg * 512, (tg + 1) * 512)
                    ps = PS[tg % 2]
                    proj_f(ps[:, :], lambda c: W[:, c, 0, :], tg)
                    sc.tt("dve", Qt[:, cols], ps[:, :], T1[:, cols], ALU.mult)
                sc.act(T1, T3, AF.Exp, scale=-1.0)
                sc.tt("dve", Kt, T2, T1, ALU.mult)
                T3v = T3.rearrange("p (c t) -> p c t", t=32)
                T1v = T1.rearrange("p (c t) -> p c t", t=32)
                sc.tt("dve", T1v, T3v[:, :, 31:32].broadcast_to([128, 64, 32]), T3v, ALU.subtract)
                sc.act(T1, T1, AF.Exp)
                sc.tt("dve", Kh, T2, T1, ALU.mult)
                for tq in range(4):
                    ps = PS[tq % 2]
                    for j in range(4):
                        tt_ = tq * 4 + j
                        for c in range(16):
                            sc.mm(ps[:, j * 128:(j + 1) * 128], HT[:, c, tt_ * 128:(tt_ + 1) * 128], W[:, c, 2, :],
                                  start=(c == 0), stop=(c == 15))
                    sc.copy("act", V[:, tq * 4:(tq + 1) * 4, :], ps[:, :].rearrange("p (a b) -> p a b", a=4))
                sc.memset("pool", ST, 0.0)
                sc.memset("pool", STB, 0.0)
                for tt_ in range(16):
                    tcols = slice(tt_ * 128, (tt_ + 1) * 128)
                    b = tt_ % 2
                    trp = PSB[:, (tt_ % 2) * 128:(tt_ % 2) * 128 + 128]
                    sc.tr(trp, Kh[:, tcols], ident_b)
                    for c4 in range(4):
                        cm = CSTF[:, CST_CMASK + c4:CST_CMASK + c4 + 1]
                        if c4 % 2 == 0:
                            sc.ts("dve", KM[b][:, c4, :], trp, cm, None, ALU.mult)
                        else:
                            sc.act(KM[b][:, c4, :], trp, AF.Copy, scale=cm)
                    scp = PS[2]
                    sc.mm(scp[:, 0:128], Kt[:, tcols], Qt[:, tcols])
                    sc.tt("dve", SC_[b], scp[:, 0:128], hgmask_f, ALU.mult)
                    op_ = PS[3 + b]
                    sc.mm(op_[:, 0:128], V[:, tt_, :], SC_[b], start=True, stop=False)
                    for c4 in range(4):
                        ch = tt_ * 4 + c4
                        sc.mm(op_[:, c4 * 32:(c4 + 1) * 32], STB, Qt[:, ch * 32:(ch + 1) * 32], start=False, stop=(c4 == 3))
                        kvp = PS[5 + (c4 % 2)]
                        sc.mm(kvp[:, 0:128], KM[b][:, c4, :], V[:, tt_, :])
                        sc.stt(ST, ST, EGL[:, ch:ch + 1], kvp[:, 0:128], ALU.mult, ALU.add)
                        sc.copy("pool", STB, ST)
                    gp = PS[tt_ % 2]
                    for c in range(16):
                        sc.mm(gp[:, 0:128], W[:, c, 3, :], HT[:, c, tcols], start=(c == 0), stop=(c == 15))
                    sc.act(TSG, gp[:, 0:128], AF.Silu)
                    group_norm_out(op_[:, 0:128], 128, vcol(layer, V_HGG, h), OUT[:, tcols], TSQ, TRS, TRI, PS[2][:, 128:256],
                                   extra_mul=TSG, tmp2=TO)
                sc.dma("sp", mixd[h * 128:(h + 1) * 128, :], OUT)

            job(load, compute)

    def sb_stage(layer):
        for h in range(6):
            def load(slot, h=h):
                W = wview(slot, [16, 3, 128])
                for j in range(3):
                    a = 2048 + j * 768 + h * 128
                    sc.dma("pool", W[:, :, j, :], w_in_cols(layer, a, a + 128))

            def compute(slot, h=h):
                att.reset()
                QT = att.alloc([128, S], BF16)
                KT = att.alloc([128, S], BF16)
                V = att.alloc([128, 16, 128], BF16)
                OUT = att.alloc([128, S], BF16)
                E = [att.alloc([128, 512], F32) for _ in range(2)]
                SPb = [att.alloc([128, 512], BF16) for _ in range(3)]
                LL = [[att.alloc([128, 512], BF16) for _ in range(3)] for _ in range(2)]
                A = [att.alloc([128, 512], BF16) for _ in range(2)]
                AD = [None] + [att.alloc([128, 512], BF16) for _ in range(3)]
                TSQ = att.alloc([128, 512], F32)
                TRS = att.alloc([128, 512], F32)
                TRI = att.alloc([128, 512], F32)
                W = wview(slot, [16, 3, 128])
                for i in range(1, 4):
                    sc.memset("pool", AD[i][:, 0:i * 128], 0.0)
                scale = 128.0 ** -0.5
                for tg in range(4):
                    cols = slice(tg * 512, (tg + 1) * 512)
                    ps = PS[0]
                    proj_f(ps[:, :], lambda c: W[:, c, 0, :], tg)
                    sc.ts("dve", QT[:, cols], ps[:, :], scale, None, ALU.mult)
                    ps = PS[1]
                    proj_f(ps[:, :], lambda c: W[:, c, 1, :], tg)
                    sc.copy("dve", KT[:, cols], ps[:, :])
                for tq in range(4):
                    ps = PS[tq % 2]
                    for j in range(4):
                        tt_ = tq * 4 + j
                        for c in range(16):
                            sc.mm(ps[:, j * 128:(j + 1) * 128], HT[:, c, tt_ * 128:(tt_ + 1) * 128], W[:, c, 2, :],
                                  start=(c == 0), stop=(c == 15))
                    sc.copy("act", V[:, tq * 4:(tq + 1) * 4, :], ps[:, :].rearrange("p (a b) -> p a b", a=4))
                pairs = []
                for qg in range(4):
                    nkb = 4 * qg + 4
                    for n, kb in enumerate(range(nkb - 1, -1, -1)):
                        i = kb - 4 * qg
                        pairs.append(dict(qg=qg, n=n, kb=kb, i=i, c0=(i * 128 if i > 0 else 0), diag=(i >= 0),
                                          nkb=nkb, idx=len(pairs)))
                ot = PS[6]

                def stA(p):
                    idx, c0, qg, n, kb = p["idx"], p["c0"], p["qg"], p["n"], p["kb"]
                    if n == 0:
                        for b_ in LL[qg % 2]:
                            sc.memset("pool", b_, 0.0)
                    zp = PS[2 + idx % 2]
                    e_ = E[idx % 2]
                    sp = SPb[idx % 3]
                    Lcur = LL[qg % 2][n % 3]
                    Lnext = LL[qg % 2][(n + 1) % 3]
                    q0 = qg * 512 + c0
                    q1 = (qg + 1) * 512
                    kcols = slice(kb * 128, (kb + 1) * 128)
                    sc.mm(zp[:, c0:512], KT[:, kcols], QT[:, q0:q1])
                    sc.act(e_[:, c0:512], zp[:, c0:512], AF.Exp)
                    sc.act(sp[:, c0:512], e_[:, c0:512], AF.Ln, bias=1.0)
                    if p["diag"]:
                        sc.tt("dve", sp[:, c0:c0 + 128], sp[:, c0:c0 + 128], strict_b, ALU.mult)
                    if n + 1 < p["nkb"]:
                        sc.tt("pool", Lnext[:, c0:512], Lcur[:, c0:512], sp[:, c0:512], ALU.add)

                def stB(p):
                    idx, c0, qg, n, kb, i = p["idx"], p["c0"], p["qg"], p["n"], p["kb"], p["i"]
                    cp = PS[4 + idx % 2]
                    sp = SPb[idx % 3]
                    Lcur = LL[qg % 2][n % 3]
                    a_ = AD[i] if i > 0 else A[idx % 2]
                    q0 = qg * 512 + c0
                    q1 = (qg + 1) * 512
                    kcols = slice(kb * 128, (kb + 1) * 128)
                    sc.mm(cp[:, c0:512], negtri_b, sp[:, c0:512], start=True, stop=False)
                    if n > 0:
                        sc.mm(cp[:, c0:512], negones_b, Lcur[:, c0:512], start=False, stop=False)
                    sc.mm(cp[:, c0:512], KT[:, kcols], QT[:, q0:q1], start=False, stop=True)
                    sc.act(a_[:, c0:512], cp[:, c0:512], AF.Exp)
                    if p["diag"]:
                        sc.tt("dve", a_[:, c0:c0 + 128], a_[:, c0:c0 + 128], strict_b, ALU.mult)
                    sc.mm(ot[:, :], V[:, kb, :], a_[:, :], start=(n == 0), stop=(n == p["nkb"] - 1))
                    if n == p["nkb"] - 1:
                        group_norm_out(ot[:, :], 512, vcol(layer, V_SBG, h), OUT[:, qg * 512:(qg + 1) * 512],
                                       TSQ, TRS, TRI, PS[0])

                stA(pairs[0])
                for k_, p in enumerate(pairs):
                    if k_ + 1 < len(pairs):
                        stA(pairs[k_ + 1])
                    stB(p)
                sc.dma("sp", mixd[(4 + h) * 128:(5 + h) * 128, :], OUT)

            job(load, compute)

    def mla_stage(layer):
        att.reset()
        CQN = att.alloc([128, 4, S], BF16)
        KVN = att.alloc([128, 2, S], BF16)
        KR2 = att.alloc([128, S], BF16)
        mla_base = att.off
        uq_v = w_uq[layer].rearrange("(c p) (h e) -> p c h e", p=128, e=192)
        ukv_v = w_ukv[layer].rearrange("(c p) (h e) -> p c h e", p=128, e=256)

        def load_q(slot):
            sc.dma("pool", wview(slot, [16, 512]), w_in_cols(layer, 4352, 4864))

        import os as _os

        def comp_q(slot):
            att.off = mla_base
            W = wview(slot, [16, 512])
            CQ = att.alloc([128, 4, 512], F32)
            TSQ = att.alloc([128, 512], F32)
            TRS = att.alloc([128, 512], F32)
            TRI = att.alloc([128, 512], F32)
            CUT = int(_os.environ.get("MLA_CUT", "9"))
            for tg in range(4):
                cols = slice(tg * 512, (tg + 1) * 512)
                for j in range(4):
                    ps = PS[j % 2]
                    proj_f(ps[:, :], lambda c: W[:, c, j * 128:(j + 1) * 128], tg)
                    sc.copy("dve", CQ[:, j, :], ps[:, :])
                    if CUT >= 2:
                        sc.act(TSQ, CQ[:, j, :], AF.Square)
                        sc.mm(PS[6][:, :], ones_f, TSQ, start=(j == 0), stop=(j == 3))
                if CUT >= 3:
                    sc.act(TRS, PS[6][:, :], AF.Sqrt, scale=1.0 / 512.0, bias=EPS)
                    sc.recip(TRI, TRS)
                if CUT >= 4:
                    for j in range(4):
                        sc.stt(CQN[:, j, cols], CQ[:, j, :], vcol(layer, V_MLAQG, j), TRI, ALU.mult, ALU.mult)

        import os as _os
        if 'q' in _os.environ.get('MLA_PRE', 'qk'):
            job(load_q, comp_q)

        def load_kv(slot):
            W = wview(slot, [16, 384])
            sc.dma("pool", W[:, :, 0:256], w_in_cols(layer, 4864, 5120))
            sc.dma("pool", W[:, :, 256:320], w_in_cols(layer, 5120, 5184))
            sc.dma("pool", W[:, :, 320:352], w_in_cols(layer, 5152, 5184))
            sc.dma("pool", W[:, :, 352:384], w_in_cols(layer, 5120, 5152))

        def comp_kv(slot):
            att.off = mla_base
            W = wview(slot, [16, 384])
            CK = att.alloc([128, 2, 512], F32)
            TSQ = att.alloc([128, 512], F32)
            TRS = att.alloc([128, 512], F32)
            TRI = att.alloc([128, 512], F32)
            KRT = att.alloc([128, 512], F32)
            for tg in range(4):
                cols = slice(tg * 512, (tg + 1) * 512)
                for j in range(2):
                    ps = PS[j % 2]
                    proj_f(ps[:, :], lambda c: W[:, c, j * 128:(j + 1) * 128], tg)
                    sc.copy("dve", CK[:, j, :], ps[:, :])
                    sc.act(TSQ, CK[:, j, :], AF.Square)
                    sc.mm(PS[6][:, :], ones_f, TSQ, start=(j == 0), stop=(j == 1))
                sc.act(TRS, PS[6][:, :], AF.Sqrt, scale=1.0 / 256.0, bias=EPS)
                sc.recip(TRI, TRS)
                for j in range(2):
                    sc.stt(KVN[:, j, cols], CK[:, j, :], vcol(layer, V_MLAKVG, j), TRI, ALU.mult, ALU.mult)
                ps = PS[2]
                proj_f(ps[:, :], lambda c: W[:, c, 256:384], tg)
                sc.tt("dve", KRT, ps[:, :], CS[:, cols], ALU.mult)
                sc.mm(PS[3][:, :], fold_f, KRT)
                sc.copy("act", KR2[:, cols], PS[3][:, :])

        if 'k' in _os.environ.get('MLA_PRE', 'qk'):
            job(load_kv, comp_kv)

        for h in range(int(_os.environ.get("MLA_HEADS", "6"))):
            def load(slot, h=h):
                Wq = wview(slot, [4, 256])
                Wk = slot[:, 1024:1024 + 512].rearrange("p (a b) -> p a b", a=2)
                sc.dma("pool", Wq[:, :, 0:128], uq_v[:, :, h, 0:128])
                sc.dma("pool", Wq[:, :, 128:192], uq_v[:, :, h, 128:192])
                sc.dma("pool", Wq[:, :, 192:224], uq_v[:, :, h, 160:192])
                sc.dma("pool", Wq[:, :, 224:256], uq_v[:, :, h, 128:160])
                sc.dma("pool", Wk, ukv_v[:, :, h, :])

            def compute(slot, h=h):
                att.off = mla_base
                Wq = wview(slot, [4, 256])
                Wk = slot[:, 1024:1024 + 512].rearrange("p (a b) -> p a b", a=2)
                QN = att.alloc([128, S], BF16)
                QR = att.alloc([128, S], BF16)
                KN = att.alloc([128, S], BF16)
                V = att.alloc([128, 16, 128], BF16)
                OUT = att.alloc([128, S], BF16)
                A = [att.alloc([128, 512], BF16) for _ in range(2)]
                AD = [None] + [att.alloc([128, 512], BF16) for _ in range(3)]
                TO = att.alloc([128, 512], F32)
                TSQ = att.alloc([128, 512], F32)
                TRS = att.alloc([128, 512], F32)
                TRI = att.alloc([128, 512], F32)
                for i in range(1, 4):
                    sc.memset("pool", AD[i][:, 0:i * 128], 0.0)
                for tg in range(4):
                    cols = slice(tg * 512, (tg + 1) * 512)
                    ps = PS[0]
                    proj_f(ps[:, :], lambda c: Wq[:, c, 0:128], tg, nk=4, rhs_fn=lambda c: CQN[:, c, cols])
                    sc.copy("act", QN[:, cols], ps[:, :])
                    ps = PS[1]
                    proj_f(ps[:, :], lambda c: Wq[:, c, 128:256], tg, nk=4, rhs_fn=lambda c: CQN[:, c, cols])
                    sc.tt("dve", QR[:, cols], ps[:, :], CS[:, cols], ALU.mult)
                    ps = PS[2]
                    proj_f(ps[:, :], lambda c: Wk[:, c, 0:128], tg, nk=2, rhs_fn=lambda c: KVN[:, c, cols])
                    sc.copy("act", KN[:, cols], ps[:, :])
                for tq in range(4):
                    ps = PS[tq % 2]
                    for j in range(4):
                        tt_ = tq * 4 + j
                        for c in range(2):
                            sc.mm(ps[:, j * 128:(j + 1) * 128], KVN[:, c, tt_ * 128:(tt_ + 1) * 128], Wk[:, c, 128:256],
                                  start=(c == 0), stop=(c == 1))
                    sc.copy("dve", V[:, tq * 4:(tq + 1) * 4, :], ps[:, :].rearrange("p (a b) -> p a b", a=4))
                scale = 192.0 ** -0.5
                pairs = []
                for qg in range(4):
                    nkb = 4 * qg + 4
                    for kb in range(nkb):
                        i = kb - 4 * qg
                        pairs.append(dict(qg=qg, kb=kb, i=i, c0=(i * 128 if i > 0 else 0), diag=(i >= 0),
                                          nkb=nkb, idx=len(pairs)))
                ot = PS[5]
                den = PS[6]

                def abuf(p):
                    return AD[p["i"]] if p["i"] > 0 else A[p["idx"] % 2]

                def stA(p):
                    idx, c0, qg, kb = p["idx"], p["c0"], p["qg"], p["kb"]
                    sp_ = PS[2 + idx % 2]
                    a_ = abuf(p)
                    q0 = qg * 512 + c0
                    q1 = (qg + 1) * 512
                    kcols = slice(kb * 128, (kb + 1) * 128)
                    sc.mm(sp_[:, c0:512], KN[:, kcols], QN[:, q0:q1], start=True, stop=False)
                    sc.mm(sp_[:, c0:512], KR2[:, kcols], QR[:, q0:q1], start=False, stop=True)
                    sc.act(a_[:, c0:512], sp_[:, c0:512], AF.Exp, scale=scale)
                    if p["diag"]:
                        sc.memset("pool", a_[64:128, c0:c0 + 64], 0.0)

                def stB(p):
                    qg, kb = p["qg"], p["kb"]
                    a_ = abuf(p)
                    sc.mm(ot[:, :], V[:, kb, :], a_[:, :], start=(kb == 0), stop=(kb == p["nkb"] - 1))
                    sc.mm(den[:, :], ones_b, a_[:, :], start=(kb == 0), stop=(kb == p["nkb"] - 1))
                    if kb == p["nkb"] - 1:
                        sc.recip(TRI, den[:, :])
                        sc.tt("dve", TO, ot[:, :], TRI, ALU.mult)
                        group_norm_out(TO, 512, vcol(layer, V_MLAOG, h), OUT[:, qg * 512:(qg + 1) * 512],
                                       TSQ, TRS, TRI, PS[4])

                stA(pairs[0])
                for k_, p in enumerate(pairs):
                    if k_ + 1 < len(pairs):
                        stA(pairs[k_ + 1])
                    stB(p)
                sc.dma("sp", mixd[(10 + h) * 128:(11 + h) * 128, :], OUT)

            job(load, compute)

    def wo_stage(layer, src):
        state = {}
        mixv = mixd.rearrange("(c p) n -> p c n", p=128)

        for dcg in range(4):
            def load(slot, dcg=dcg):
                sc.dma("pool", wview(slot, [16, 512]),
                       w_o[layer].rearrange("(c p) n -> p c n", p=128)[:, :, dcg * 512:(dcg + 1) * 512])

            def compute(slot, dcg=dcg):
                if dcg == 0:
                    att.reset()
                    state["MX"] = [att.alloc([128, 16, 512], BF16) for _ in range(2)]
                    state["XB"] = [att.alloc([128, 512], F32) for _ in range(8)]
                    state["k"] = 0
                    sc.dma("sp", state["MX"][0], mixv[:, :, 0:512])
                MX = state["MX"]
                XB = state["XB"]
                W = wview(slot, [16, 512])
                for tg in range(4):
                    cols = slice(tg * 512, (tg + 1) * 512)
                    n = dcg * 4 + tg
                    mx = MX[n % 2]
                    xbs = []
                    for j in range(4):
                        dc = dcg * 4 + j
                        xb = XB[state["k"] % 8]
                        state["k"] += 1
                        sc.dma("sp", xb, src[dc * 128:(dc + 1) * 128, cols])
                        xbs.append(xb)
                    if n + 1 < 16:
                        tg2 = (tg + 1) % 4
                        sc.dma("sp", MX[(n + 1) % 2], mixv[:, :, tg2 * 512:(tg2 + 1) * 512])
                    for j in range(4):
                        dc = dcg * 4 + j
                        ps = PS[j % 4]
                        for mc in range(16):
                            sc.mm(ps[:, :], W[:, mc, j * 128:(j + 1) * 128], mx[:, mc, :], start=(mc == 0), stop=(mc == 15))
                        sc.tt("dve", xbs[j], ps[:, :], xbs[j], ALU.add)
                        sc.dma("sp", xres[dc * 128:(dc + 1) * 128, cols], xbs[j])

            job(load, compute)

    def ffn_stage(layer):
        state = {}
        wg = w_gate[layer].rearrange("(c p) n -> p c n", p=128)
        wu = w_up[layer].rearrange("(c p) n -> p c n", p=128)
        wd = w_down[layer].rearrange("(c p) n -> p c n", p=128)
        for tt_ in range(4):
            cols = slice(tt_ * 512, (tt_ + 1) * 512)
            for fcp in range(22):
                def load(slot, fcp=fcp):
                    W = wview(slot, [16, 2, 256])
                    sc.dma("pool", W[:, :, 0, :], wg[:, :, fcp * 256:(fcp + 1) * 256])
                    sc.dma("pool", W[:, :, 1, :], wu[:, :, fcp * 256:(fcp + 1) * 256])

                def compute(slot, fcp=fcp, tt_=tt_, cols=cols):
                    if fcp == 0 and tt_ == 0:
                        att.reset()
                        state["ACT"] = att.alloc([128, NFC, 512], BF16)
                        state["SG"] = [att.alloc([128, 512], F32) for _ in range(2)]
                        state["XB"] = [att.alloc([128, 512], F32) for _ in range(8)]
                        state["k"] = 0
                    W = wview(slot, [16, 2, 256])
                    for j in range(2):
                        fc = fcp * 2 + j
                        pg = PS[j * 2]
                        pu = PS[j * 2 + 1]
                        proj_f(pg[:, :], lambda c: W[:, c, 0, j * 128:(j + 1) * 128], tt_)
                        proj_f(pu[:, :], lambda c: W[:, c, 1, j * 128:(j + 1) * 128], tt_)
                        sg = state["SG"][fc % 2]
                        sc.act(sg, pg[:, :], AF.Silu)
                        sc.tt("dve", state["ACT"][:, fc, :], pu[:, :], sg, ALU.mult)

                job(load, compute)
            for dcp in range(8):
                for half in range(2):
                    def load(slot, dcp=dcp, half=half):
                        sc.dma("pool", wview(slot, [22, 256]), wd[:, half * 22:(half + 1) * 22, dcp * 256:(dcp + 1) * 256])

                    def compute(slot, dcp=dcp, half=half, cols=cols):
                        W = wview(slot, [22, 256])
                        if half == 0:
                            state["xbs"] = []
                            for j in range(2):
                                dc = dcp * 2 + j
                                xb = state["XB"][state["k"] % 8]
                                state["k"] += 1
                                sc.dma("sp", xb, xres[dc * 128:(dc + 1) * 128, cols])
                                state["xbs"].append(xb)
                        for j in range(2):
                            ps = PS[4 + j]
                            for f in range(22):
                                fc = half * 22 + f
                                sc.mm(ps[:, :], W[:, f, j * 128:(j + 1) * 128], state["ACT"][:, fc, :],
                                      start=(fc == 0), stop=(fc == NFC - 1))
                        if half == 1:
                            for j in range(2):
                                dc = dcp * 2 + j
                                ps = PS[4 + j]
                                xb = state["xbs"][j]
                                sc.tt("dve", xb, ps[:, :], xb, ALU.add)
                                sc.dma("sp", xres[dc * 128:(dc + 1) * 128, cols], xb)

                    job(load, compute)

    def plain(fn):
        job(None, lambda slot: fn())

    stages = stages or ("norm", "hg", "sb", "mla", "wo", "ffn", "final")
    for layer in range(nlayers):
        src = xT_in if layer == 0 else xres
        if "norm" in stages:
            plain(lambda layer=layer, src=src: norm_stage(src, lambda c: vcol(layer, V_ATTN, c)))
            if debug and layer == 0 and "hT" in dbg:
                plain(lambda: sc.dma("sp", dbg["hT"].rearrange("(c p) n -> p c n", p=128), HT[:, :, :], is_out=True))
        if "hg" in stages:
            hg_stage(layer)
        if "sb" in stages:
            sb_stage(layer)
        if "mla" in stages:
            mla_stage(layer)
        if debug and layer == 0 and "mix" in dbg:
            def dump_mix():
                att.reset()
                t = att.alloc([128, 16, 512], BF16)
                for tg in range(4):
                    sc.dma("sp", t, mixd.rearrange("(c p) n -> p c n", p=128)[:, :, tg * 512:(tg + 1) * 512])
                    sc.dma("sp", dbg["mix"].rearrange("(c p) n -> p c n", p=128)[:, :, tg * 512:(tg + 1) * 512], t, is_out=True)
            plain(dump_mix)
        if "wo" in stages:
            wo_stage(layer, src)

        def dump_x(key):
            att.reset()
            t = att.alloc([128, 2048], F32)
            for c in range(16):
                sc.dma("sp", t, xres[c * 128:(c + 1) * 128, :])
                sc.dma("sp", dbg[key][c * 128:(c + 1) * 128, :], t, is_out=True)
        if debug and layer == 0 and "x1" in dbg:
            plain(lambda: dump_x("x1"))
        if "ffn" in stages:
            plain(lambda layer=layer: norm_stage(xres, lambda c: vcol(layer, V_FFN, c)))
            ffn_stage(layer)
        if debug and layer == 0 and "x2" in dbg:
            plain(lambda: dump_x("x2"))
    if "final" in stages:
        plain(lambda: norm_stage(xres, lambda c: VEC[:, V_FINAL + c:V_FINAL + c + 1], dst_dram=outT))

    wjobs = [k for k, (ld, _) in enumerate(jobs) if ld is not None]
    slot_of = {k: WS[n % NSLOT][:, :] for n, k in enumerate(wjobs)}
    issued = [0]

    def issue_upto(n):
        while issued[0] < min(n, len(wjobs)):
            k = wjobs[issued[0]]
            jobs[k][0](slot_of[k])
            issued[0] += 1

    nw = 0
    for k, (ld, comp) in enumerate(jobs):
        if ld is not None:
            issue_upto(nw + NSLOT)
            nw += 1
        else:
            issue_upto(nw + NSLOT - 1)
        comp(slot_of.get(k))

    sc.emit_all()
    nc._in_names = {"xT", "pos", "vec", "cst"} | ({"w_in"} if w_in is not None else set()) | \
        ({"mla_w_uq", "mla_w_ukv"} if w_uq is not None else set()) | ({"w_o"} if w_o is not None else set()) | \
        ({"w_gate", "w_up", "w_down"} if w_gate is not None else set())
    nc._nops = len(sc.ops)
    return nc


_PROG_CACHE = {}


def _get_prog(debug=False, nlayers=DEPTH, stages=None):
    key = (debug, nlayers, stages)
    if key not in _PROG_CACHE:
        _PROG_CACHE[key] = build_program(debug=debug, nlayers=nlayers, stages=stages)
    return _PROG_CACHE[key]


def _pack_vec(inp):
    v = np.zeros((128, V_N), np.float32)

    def cols(a):
        a = np.asarray(a, np.float32)
        return a.reshape(-1, 128).T

    for l in range(DEPTH):
        b = l * VPL
        v[:, b + V_ATTN:b + V_ATTN + 16] = cols(inp["attn_norm_g"][l])
        v[:, b + V_FFN:b + V_FFN + 16] = cols(inp["ffn_norm_g"][l])
        v[:, b + V_HGG:b + V_HGG + 4] = cols(inp["hg_norm_g"][l])
        v[:, b + V_SBG:b + V_SBG + 6] = cols(inp["sb_norm_g"][l])
        v[:, b + V_MLAOG:b + V_MLAOG + 6] = cols(inp["mla_out_norm_g"][l])
        v[:, b + V_MLAQG:b + V_MLAQG + 4] = cols(inp["mla_q_norm_g"][l])
        v[:, b + V_MLAKVG:b + V_MLAKVG + 2] = cols(inp["mla_kv_norm_g"][l])
        v[:, b + V_LB:b + V_LB + 4] = cols(inp["hg_lower_bounds"][l])
    v[:, V_FINAL:V_FINAL + 16] = cols(inp["final_norm_g"])
    return v


def _in_maps(inp, names=None, ncores=NCORES):
    x = np.asarray(inp["x"], np.float32)
    pos = np.asarray(inp["positions"], np.int32)
    vec = _pack_vec(inp)
    cst = _make_cst()
    shared = {
        "vec": vec, "cst": cst,
        "w_in": np.ascontiguousarray(inp["w_in"], dtype=np.float32),
        "mla_w_uq": np.ascontiguousarray(inp["mla_w_uq"], dtype=np.float32),
        "mla_w_ukv": np.ascontiguousarray(inp["mla_w_ukv"], dtype=np.float32),
        "w_o": np.ascontiguousarray(inp["w_o"], dtype=np.float32),
        "w_gate": np.ascontiguousarray(inp["w_gate"], dtype=np.float32),
        "w_up": np.ascontiguousarray(inp["w_up"], dtype=np.float32),
        "w_down": np.ascontiguousarray(inp["w_down"], dtype=np.float32),
    }
    if names is not None:
        shared = {k: v for k, v in shared.items() if k in names}
    maps = []
    for b in range(ncores):
        m = dict(shared)
        m["xT"] = np.ascontiguousarray(x[b].T)
        m["pos"] = np.ascontiguousarray(pos[b:b + 1])
        maps.append(m)
    return maps


def kernel(**inputs):
    nc = _get_prog()
    res = run_bass_kernel_spmd(nc, _in_maps(inputs), core_ids=list(range(NCORES)))
    out = np.stack([np.ascontiguousarray(np.asarray(r["outT"], np.float32).T) for r in res.results], axis=0)
    return out
```

```python
import math
import numpy as np
import concourse.bass as bass
import concourse.mybir as mybir
from concourse.bass_utils import run_bass_kernel_spmd

F32 = mybir.dt.float32
BF16 = mybir.dt.bfloat16
I32 = mybir.dt.int32
AF = mybir.ActivationFunctionType
ALU = mybir.AluOpType

S = 2048
D = 2048
DEPTH = 2
NCORES = 8
EPS = 1e-6
IN_COLS = 5184
D_FF = 5632
NFC = D_FF // 128
P = 128

_ESZ = {str(F32): 4, str(BF16): 2, str(I32): 4}


def _esz(dt):
    return _ESZ[str(dt)]


def _box(ap):
    t = ap.tensor
    dims = [(int(s), int(c)) for s, c in ap.ap]
    es = _esz(ap.dtype)
    off = int(ap.offset)
    if type(t).__name__.startswith("DRam"):
        ext = sum((c - 1) * abs(s) for s, c in dims)
        return (t.name, 0, 1, off * es, (off + ext + 1) * es)
    pstep, pc = dims[0]
    if pstep == 0:
        pstep = 1 << 40
    p0 = off // pstep
    f0 = off % pstep
    ext = sum((c - 1) * abs(s) for s, c in dims[1:])
    return (t.name, p0, p0 + pc, f0 * es, (f0 + ext + 1) * es)


class Sched:
    ENG = ("pe", "dve", "act", "pool", "sp")

    def __init__(self, nc):
        self.nc = nc
        self.ops = []
        self.recs = {}
        self.psum_last = {}
        self.out_dmas = []

    def _access(self, box, idx, w, eng, dma, deps):
        name, p0, p1, f0, f1 = box
        recs = self.recs.get(name)
        if recs is None:
            recs = self.recs[name] = []
        new = []
        for r in recs:
            rp0, rp1, rf0, rf1, ridx, rw, reng, rdma = r
            ov = rp0 < p1 and p0 < rp1 and rf0 < f1 and f0 < rf1
            if ov and (w or rw) and ridx != idx:
                deps.add(ridx)
            if ridx == idx:
                new.append(r)
                continue
            if w and rp0 >= p0 and rp1 <= p1 and rf0 >= f0 and rf1 <= f1:
                continue
            if (not w) and (not rw) and reng == eng and (not dma) and (not rdma) \
                    and rp0 == p0 and rp1 == p1 and rf0 == f0 and rf1 == f1:
                continue
            new.append(r)
        new.append((p0, p1, f0, f1, idx, w, eng, dma))
        self.recs[name] = new

    def _psum_access(self, name, idx, w, eng, deps):
        r = self.psum_last.get(name)
        if r is not None and r[0] != idx:
            ridx, rw, reng = r
            if not (reng == eng and not w and not rw):
                deps.add(ridx)
        if r is not None and r[0] == idx:
            self.psum_last[name] = (idx, w or r[1], eng)
        else:
            self.psum_last[name] = (idx, w, eng)

    def add(self, eng, emit, reads, writes, dma=False, is_out=False):
        idx = len(self.ops)
        deps = set()
        for ap in reads:
            if type(ap.tensor).__name__.startswith("PSum"):
                self._psum_access(ap.tensor.name, idx, False, eng, deps)
            else:
                self._access(_box(ap), idx, False, eng, dma, deps)
        for ap in writes:
            if type(ap.tensor).__name__.startswith("PSum"):
                self._psum_access(ap.tensor.name, idx, True, eng, deps)
            else:
                self._access(_box(ap), idx, True, eng, dma, deps)
        op = dict(idx=idx, eng=eng, emit=emit, deps=deps, dma=dma, signal=False, sem=None, val=0)
        self.ops.append(op)
        if is_out:
            self.out_dmas.append(idx)
        return idx

    def mm(self, out, lhsT, rhs, start=True, stop=True):
        self.add("pe", lambda e: e.matmul(out, lhsT, rhs, start=start, stop=stop), [lhsT, rhs], [out])

    def tr(self, out, in_, ident):
        self.add("pe", lambda e: e.transpose(out, in_, ident), [in_, ident], [out])

    def act(self, out, in_, func, scale=None, bias=None):
        kw = {}
        rd = [in_]
        if scale is not None:
            kw["scale"] = scale
            if not isinstance(scale, (int, float)):
                rd.append(scale)
        if bias is not None:
            kw["bias"] = bias
            if not isinstance(bias, (int, float)):
                rd.append(bias)
        self.add("act", lambda e: e.activation(out, in_, func, **kw), rd, [out])

    def _veng(self, eng):
        return eng

    def tt(self, eng, out, in0, in1, op):
        self.add(eng, lambda e: e.tensor_tensor(out, in0, in1, op), [in0, in1], [out])

    def ts(self, eng, out, in0, s1, s2, op0, op1=None):
        rd = [in0]
        if not isinstance(s1, (int, float)):
            rd.append(s1)
        if s2 is not None and not isinstance(s2, (int, float)):
            rd.append(s2)
        if op1 is None:
            self.add(eng, lambda e: e.tensor_scalar(out, in0, s1, None, op0), rd, [out])
        else:
            self.add(eng, lambda e: e.tensor_scalar(out, in0, s1, s2, op0, op1), rd, [out])

    def stt(self, out, in0, scalar, in1, op0, op1):
        rd = [in0, in1]
        if not isinstance(scalar, (int, float)):
            rd.append(scalar)
        self.add("dve", lambda e: e.scalar_tensor_tensor(out, in0, scalar, in1, op0, op1), rd, [out])

    def scan(self, out, d0, d1, initial, op0, op1):
        self.add("dve", lambda e: e.tensor_tensor_scan(out, d0, d1, initial, op0, op1), [d0, d1], [out])

    def copy(self, eng, out, in_):
        if eng == "act":
            self.add("act", lambda e: e.copy(out, in_), [in_], [out])
        else:
            self.add(eng, lambda e: e.tensor_copy(out, in_), [in_], [out])

    def memset(self, eng, ap, val):
        self.add(eng, lambda e: e.memset(ap, val), [], [ap])

    def recip(self, out, in_):
        self.add("dve", lambda e: e.reciprocal(out, in_), [in_], [out])

    def dma(self, eng, out, in_, is_out=False):
        self.add(eng, lambda e: e.dma_start(out=out, in_=in_), [in_], [out], dma=True, is_out=is_out)

    def emit_all(self):
        nc = self.nc
        ops = self.ops
        NPOOL = 24
        dsem = [nc.alloc_semaphore("dq%d" % i) for i in range(NPOOL)]
        dcnt = [0] * NPOOL
        dlast = [None] * NPOOL
        nd = 0
        for op in ops:
            if op["dma"]:
                k = nd % NPOOL
                nd += 1
                if dlast[k] is not None:
                    op["deps"].add(dlast[k])
                dcnt[k] += 16
                op["sem"] = dsem[k]
                op["val"] = dcnt[k]
                op["semid"] = ("d", k)
                dlast[k] = op["idx"]
        for op in ops:
            for d in op["deps"]:
                dop = ops[d]
                if dop["dma"]:
                    continue
                if dop["eng"] == "pe" and op["eng"] == "pe" and not op["dma"]:
                    continue
                dop["signal"] = True
        MAXC = 30000
        cur = {}
        for op in ops:
            if op["dma"] or not op["signal"]:
                continue
            e = op["eng"]
            if e not in cur or cur[e][1] >= MAXC:
                gen = 0 if e not in cur else cur[e][2] + 1
                cur[e] = [nc.alloc_semaphore("c_%s_%d" % (e, gen)), 0, gen]
            cur[e][1] += 1
            op["sem"] = cur[e][0]
            op["val"] = cur[e][1]
            op["semid"] = ("c", e, cur[e][2])

        handles = {"pe": "tensor", "dve": "vector", "act": "scalar", "pool": "gpsimd", "sp": "sync"}

        def emit_engine(ename, e):
            waited = {}
            last = None
            for op in ops:
                if op["eng"] != ename:
                    continue
                need = {}
                for d in op["deps"]:
                    dop = ops[d]
                    if (not dop["dma"]) and dop["eng"] == "pe" and ename == "pe" and not op["dma"]:
                        continue
                    sid = dop["semid"]
                    if waited.get(sid, 0) >= dop["val"]:
                        continue
                    if sid not in need or need[sid][1] < dop["val"]:
                        need[sid] = (dop["sem"], dop["val"])
                for sid, (sem, val) in need.items():
                    e.wait_ge(sem, val)
                    waited[sid] = val
                ins = op["emit"](e)
                if op["dma"]:
                    ins.then_inc(op["sem"], 16)
                elif op["signal"]:
                    ins.then_inc(op["sem"], 1)
            if ename == "sp":
                fin = {}
                for d in self.out_dmas:
                    dop = ops[d]
                    sid = dop["semid"]
                    if sid not in fin or fin[sid][1] < dop["val"]:
                        fin[sid] = (dop["sem"], dop["val"])
                for sid, (sem, val) in fin.items():
                    if waited.get(sid, 0) < val:
                        e.wait_ge(sem, val)

        with nc.Block() as block:
            @block.tensor
            def _(e):
                emit_engine("pe", e)

            @block.vector
            def _(e):
                emit_engine("dve", e)

            @block.scalar
            def _(e):
                emit_engine("act", e)

            @block.gpsimd
            def _(e):
                emit_engine("pool", e)

            @block.sync
            def _(e):
                emit_engine("sp", e)


class Arena:
    def __init__(self, t, nbytes):
        self.t = t
        self.nbytes = nbytes
        self.off = 0

    def reset(self):
        self.off = 0

    def alloc(self, shape, dt):
        es = _esz(dt)
        n = 1
        for s in shape[1:]:
            n *= s
        nb = (n * es + 63) // 64 * 64
        assert self.off + nb <= self.nbytes, ("arena overflow", self.off, nb, self.nbytes)
        a = self.t[0:shape[0], self.off // 2:(self.off + n * es) // 2]
        self.off += nb
        if dt != BF16:
            a = a.bitcast(dt)
        if len(shape) == 3:
            a = a.rearrange("p (a b) -> p a b", a=shape[1])
        elif len(shape) == 4:
            a = a.rearrange("p (a b c) -> p a b c", a=shape[1], b=shape[2])
        return a


CST_IDENT = 0
CST_ONES = 128
CST_NEGTRI = 256
CST_STRICT = 384
CST_NEGONES = 512
CST_NB = 640
CST_HGMASK = 640
CST_FOLD = 768
CST_CMASK = 896
CST_INVF = 900
CST_PHASE = 901
CST_SIGN = 902
CST_N = 904

VPL = 58
V_ATTN, V_FFN, V_HGG, V_SBG, V_MLAOG, V_MLAQG, V_MLAKVG, V_LB = 0, 16, 32, 36, 42, 48, 52, 54
V_FINAL = 2 * VPL
V_N = 2 * VPL + 16


def _make_cst():
    c = np.zeros((128, CST_N), np.float32)
    i = np.arange(128)
    c[:, CST_IDENT:CST_IDENT + 128] = np.eye(128, dtype=np.float32)
    c[:, CST_ONES:CST_ONES + 128] = 1.0
    c[:, CST_NEGTRI:CST_NEGTRI + 128] = -(i[:, None] >= i[None, :]).astype(np.float32)
    c[:, CST_STRICT:CST_STRICT + 128] = (i[:, None] < i[None, :]).astype(np.float32)
    c[:, CST_NEGONES:CST_NEGONES + 128] = -1.0
    c[:, CST_HGMASK:CST_HGMASK + 128] = ((i[:, None] // 32 == i[None, :] // 32) & (i[:, None] <= i[None, :])).astype(np.float32)
    c[:, CST_FOLD:CST_FOLD + 128] = (i[:, None] % 64 == i[None, :] % 64).astype(np.float32)
    for k in range(4):
        c[:, CST_CMASK + k] = (i // 32 == k).astype(np.float32)
    inv_freq = (np.float32(10000.0) ** (-np.arange(0, 64, 2, dtype=np.float32) / np.float32(64))).astype(np.float32)
    c[:, CST_INVF] = inv_freq[i % 32]
    c[:, CST_PHASE] = np.where(i < 64, np.float32(math.pi / 2), np.float32(0.0))
    c[:, CST_SIGN] = np.where((i >= 64) & (i < 96), -1.0, 1.0)
    return c


def build_program(debug=False, nlayers=DEPTH, stages=None):
    nc = bass.Bass("TRN2", target_bir_lowering=False)
    sc = Sched(nc)

    def dram_in(name, shape, dt=F32):
        return nc.dram_tensor(name, list(shape), dt, kind="ExternalInput").ap()

    need = set(stages or ("norm", "hg", "sb", "mla", "wo", "ffn", "final"))
    xT_in = dram_in("xT", [D, S])
    pos_in = dram_in("pos", [1, S], I32)
    vec_in = dram_in("vec", [128, V_N])
    cst_in = dram_in("cst", [128, CST_N])
    w_in = dram_in("w_in", [DEPTH, D, IN_COLS]) if need & {"hg", "sb", "mla"} else None
    w_uq = dram_in("mla_w_uq", [DEPTH, 512, 1152]) if "mla" in need else None
    w_ukv = dram_in("mla_w_ukv", [DEPTH, 256, 1536]) if "mla" in need else None
    w_o = dram_in("w_o", [DEPTH, D, D]) if "wo" in need else None
    w_gate = dram_in("w_gate", [DEPTH, D, D_FF]) if "ffn" in need else None
    w_up = dram_in("w_up", [DEPTH, D, D_FF]) if "ffn" in need else None
    w_down = dram_in("w_down", [DEPTH, D_FF, D]) if "ffn" in need else None
    outT = nc.dram_tensor("outT", [D, S], F32, kind="ExternalOutput").ap()
    xres = nc.dram_tensor("xres", [D, S], F32, kind="Internal").ap()
    mixd = nc.dram_tensor("mixd", [D, S], BF16, kind="Internal").ap()
    dbg = {}
    if debug:
        st_ = set(stages or ())
        if "norm" in st_:
            dbg["hT"] = nc.dram_tensor("dbg_hT", [D, S], BF16, kind="ExternalOutput").ap()
        if st_ & {"hg", "sb", "mla"}:
            dbg["mix"] = nc.dram_tensor("dbg_mix", [D, S], BF16, kind="ExternalOutput").ap()
        if "wo" in st_:
            dbg["x1"] = nc.dram_tensor("dbg_x1", [D, S], F32, kind="ExternalOutput").ap()
        if "ffn" in st_:
            dbg["x2"] = nc.dram_tensor("dbg_x2", [D, S], F32, kind="ExternalOutput").ap()
        dbg["cs"] = nc.dram_tensor("dbg_cs", [128, S], F32, kind="ExternalOutput").ap()
        dbg["lbt"] = nc.dram_tensor("dbg_lbt", [128, 32], F32, kind="ExternalOutput").ap()

    HT = nc.alloc_sbuf_tensor("HT", [128, 16, S], BF16)
    ATT_BYTES = 68 * 1024
    ATTt = nc.alloc_sbuf_tensor("ATT", [128, ATT_BYTES // 2], BF16)
    att = Arena(ATTt, ATT_BYTES)
    NSLOT = 3
    WS = [nc.alloc_sbuf_tensor("WS%d" % i, [128, 8192], BF16) for i in range(NSLOT)]
    CS = nc.alloc_sbuf_tensor("CS", [128, S], F32)
    VEC = nc.alloc_sbuf_tensor("VEC", [128, V_N], F32)
    CSTF = nc.alloc_sbuf_tensor("CSTF", [128, CST_N], F32)
    CSTB = nc.alloc_sbuf_tensor("CSTB", [128, CST_NB], BF16)
    LBT = nc.alloc_sbuf_tensor("LBT", [128, 32], F32)

    PS = [nc.alloc_psum_tensor("PS%d" % i, [128, 512], F32) for i in range(7)]
    PSB = nc.alloc_psum_tensor("PSB", [128, 1024], BF16)

    ident_b = CSTB[:, 0:128]
    ones_b = CSTB[:, 128:256]
    negtri_b = CSTB[:, 256:384]
    strict_b = CSTB[:, 384:512]
    negones_b = CSTB[:, 512:640]
    ones_f = CSTF[:, CST_ONES:CST_ONES + 128]
    hgmask_f = CSTF[:, CST_HGMASK:CST_HGMASK + 128]
    fold_f = CSTF[:, CST_FOLD:CST_FOLD + 128]

    def vcol(layer, base, j):
        c = layer * VPL + base + j
        return VEC[:, c:c + 1]

    sc.dma("sp", VEC[:, :], vec_in)
    sc.dma("sp", CSTF[:, :], cst_in)
    sc.dma("pool", CSTB[:, :], cst_in[:, 0:CST_NB])

    r0 = VEC[:, V_LB:V_LB + 4]
    r1 = VEC[:, VPL + V_LB:VPL + V_LB + 4]
    sc.tt("dve", LBT[:, 24:28], r0, r1, ALU.subtract)
    sc.act(LBT[:, 0:4], LBT[:, 24:28], AF.Sigmoid)
    sc.act(LBT[:, 4:8], LBT[:, 24:28], AF.Sigmoid, scale=-1.0)
    sc.tt("dve", LBT[:, 8:12], LBT[:, 0:4], LBT[:, 0:4], ALU.subtract)
    sc.tt("dve", LBT[:, 28:32], LBT[:, 0:4], LBT[:, 4:8], ALU.add)
    sc.tt("dve", LBT[:, 12:16], LBT[:, 28:32], LBT[:, 0:4], ALU.subtract)
    sc.ts("dve", LBT[:, 16:24], LBT[:, 8:16], -1.0, 1.0, ALU.mult, ALU.add)

    def lbcol(layer, h):
        return LBT[:, 8 + 4 * layer + h:9 + 4 * layer + h]

    def omlcol(layer, h):
        return LBT[:, 16 + 4 * layer + h:17 + 4 * layer + h]

    att.reset()
    posi = att.alloc([128, S], I32)
    ang = att.alloc([128, S], F32)
    kq = att.alloc([128, S], F32)
    kqi = att.alloc([128, S], I32)
    sc.dma("sp", posi, pos_in.partition_broadcast(128))
    sc.copy("dve", ang, posi)
    sc.ts("dve", ang, ang, CSTF[:, CST_INVF:CST_INVF + 1], CSTF[:, CST_PHASE:CST_PHASE + 1], ALU.mult, ALU.add)
    sc.ts("dve", kq, ang, float(1.0 / (2 * math.pi)), None, ALU.mult)
    sc.copy("dve", kqi, kq)
    sc.copy("dve", kq, kqi)
    C1 = 6.28125
    C2 = float(2 * math.pi - 6.28125)
    sc.stt(ang, kq, -C1, ang, ALU.mult, ALU.add)
    sc.stt(ang, kq, -C2, ang, ALU.mult, ALU.add)
    sc.ts("dve", kq, ang, float(math.pi), float(-2 * math.pi), ALU.is_gt, ALU.mult)
    sc.tt("dve", ang, ang, kq, ALU.add)
    sc.ts("dve", kq, ang, float(-math.pi), float(2 * math.pi), ALU.is_lt, ALU.mult)
    sc.tt("dve", ang, ang, kq, ALU.add)
    sc.ts("dve", ang, ang, float(math.pi), float(-math.pi), ALU.min, ALU.max)
    sc.act(CS[:, :], ang, AF.Sin)
    sc.ts("dve", CS[:, :], CS[:, :], CSTF[:, CST_SIGN:CST_SIGN + 1], None, ALU.mult)

    if debug:
        sc.dma("sp", dbg["cs"], CS[:, :], is_out=True)
        sc.dma("sp", dbg["lbt"], LBT[:, :], is_out=True)

    jobs = []

    def job(load, compute):
        jobs.append((load, compute))

    def wview(slot, shape):
        n = 1
        for s in shape:
            n *= s
        a = slot[:, 0:n]
        if len(shape) == 2:
            return a.rearrange("p (a b) -> p a b", a=shape[0])
        if len(shape) == 3:
            return a.rearrange("p (a b c) -> p a b c", a=shape[0], b=shape[1])
        return a

    def w_in_cols(layer, a, b):
        return w_in[layer].rearrange("(c p) n -> p c n", p=128)[:, :, a:b]

    def proj_f(ps, wfn, tg, nk=16, rhs_fn=None):
        for c in range(nk):
            rhs = HT[:, c, tg * 512:(tg + 1) * 512] if rhs_fn is None else rhs_fn(c)
            sc.mm(ps, wfn(c), rhs, start=(c == 0), stop=(c == nk - 1))

    def norm_stage(src, gcol_fn, dst_dram=None):
        att.reset()
        XB = [att.alloc([128, 512], F32) for _ in range(4)]
        SQ = [att.alloc([128, 512], F32) for _ in range(2)]
        RS = att.alloc([128, 512], F32)
        RINV = att.alloc([128, 512], F32)
        k = 0
        for tg in range(4):
            cols = slice(tg * 512, (tg + 1) * 512)
            st = PS[6]
            for c in range(16):
                xb = XB[k % 4]
                k += 1
                sc.dma("sp", xb, src[c * 128:(c + 1) * 128, cols])
                sq = SQ[c % 2]
                sc.act(sq, xb, AF.Square)
                sc.mm(st[:, :], ones_f, sq, start=(c == 0), stop=(c == 15))
            sc.act(RS, st[:, :], AF.Sqrt, scale=1.0 / D, bias=EPS)
            sc.recip(RINV, RS)
            xbs = {}

            def ld(c):
                nonlocal k
                xbs[c] = XB[k % 4]
                k += 1
                sc.dma("sp", xbs[c], src[c * 128:(c + 1) * 128, cols])

            ld(0)
            ld(1)
            for c in range(16):
                xb = xbs[c]
                if dst_dram is None:
                    sc.stt(HT[:, c, cols], xb, gcol_fn(c), RINV, ALU.mult, ALU.mult)
                else:
                    sc.stt(xb, xb, gcol_fn(c), RINV, ALU.mult, ALU.mult)
                    sc.dma("sp", dst_dram[c * 128:(c + 1) * 128, cols], xb, is_out=True)
                if c + 2 < 16:
                    ld(c + 2)

    def group_norm_out(o_ap, width, gcol, dst, tmp_sq, tmp_rs, tmp_rinv, st_ps, extra_mul=None, tmp2=None):
        sc.act(tmp_sq[:, 0:width], o_ap, AF.Square)
        sc.mm(st_ps[:, 0:width], ones_f, tmp_sq[:, 0:width], start=True, stop=True)
        sc.act(tmp_rs[:, 0:width], st_ps[:, 0:width], AF.Sqrt, scale=1.0 / 128.0, bias=EPS)
        sc.recip(tmp_rinv[:, 0:width], tmp_rs[:, 0:width])
        if extra_mul is None:
            sc.stt(dst, o_ap, gcol, tmp_rinv[:, 0:width], ALU.mult, ALU.mult)
        else:
            sc.stt(tmp2[:, 0:width], o_ap, gcol, tmp_rinv[:, 0:width], ALU.mult, ALU.mult)
            sc.tt("dve", dst, tmp2[:, 0:width], extra_mul, ALU.mult)

    def hg_stage(layer):
        att.reset()
        T1 = att.alloc([128, S], F32)
        T2 = att.alloc([128, S], F32)
        T3 = att.alloc([128, S], F32)
        SM = att.alloc([128, S], F32)
        Qt = att.alloc([128, S], BF16)
        Kt = att.alloc([128, S], BF16)
        Kh = att.alloc([128, S], BF16)
        V = att.alloc([128, 16, 128], BF16)
        OUT = att.alloc([128, S], BF16)
        EGL = att.alloc([128, 64], F32)
        KM = [att.alloc([128, 4, 128], BF16) for _ in range(2)]
        SC_ = [att.alloc([128, 128], BF16) for _ in range(2)]
        ST = [att.alloc([128, 128], F32) for _ in range(4)]
        STB = [att.alloc([128, 128], BF16) for _ in range(4)]
        TRIB = [att.alloc([128, 128], F32) for _ in range(2)]
        SCF = [att.alloc([128, 128], F32) for _ in range(2)]
        OSB = [att.alloc([128, 128], F32) for _ in range(2)]
        TSQ = att.alloc([128, 128], F32)
        TRS = att.alloc([128, 128], F32)
        TRI = att.alloc([128, 128], F32)
        TSG = att.alloc([128, 128], F32)
        TO = att.alloc([128, 128], F32)
        for h in range(4):
            def load(slot, h=h):
                W = wview(slot, [16, 4, 128])
                for j in range(4):
                    a = j * 512 + h * 128
                    sc.dma("pool", W[:, :, j, :], w_in_cols(layer, a, a + 128))

            def compute(slot, h=h):
                W = wview(slot, [16, 4, 128])
                if h == 0:
                    sc.memset("pool", SM, 1.0)
                    sc.memset("pool", SM.rearrange("p (c t) -> p c t", t=32)[:, :, 0:1], 0.0)
                for tg in range(4):
                    cols = slice(tg * 512, (tg + 1) * 512)
                    ps = PS[tg % 2]
                    proj_f(ps[:, :], lambda c: W[:, c, 1, :], tg)
                    sc.act(T1[:, cols], ps[:, :], AF.Sigmoid)
                    sc.ts("dve", T1[:, cols], T1[:, cols], omlcol(layer, h), lbcol(layer, h), ALU.mult, ALU.add)
                    sc.ts("dve", T2[:, cols], T1[:, cols], -1.0, 1.0, ALU.mult, ALU.add)
                    sc.act(T1[:, cols], T1[:, cols], AF.Ln)
                sc.scan(T3, SM, T1, 0.0, ALU.mult, ALU.add)
                sc.act(T1, T3, AF.Exp)
                sc.copy("dve", EGL, T1.rearrange("p (c t) -> p c t", t=32)[:, :, 31])
                for tg in range(4):
                    cols = slice(tg * 512, (tg + 1) * 512)
                    ps = PS[tg % 2]
                    proj_f(ps[:, :], lambda c: W[:, c, 0, :], tg)
                    sc.tt("dve", Qt[:, cols], ps[:, :], T1[:, cols], ALU.mult)
                sc.act(T1, T3, AF.Exp, scale=-1.0)
                sc.tt("dve", Kt, T2, T1, ALU.mult)
                T3v = T3.rearrange("p (c t) -> p c t", t=32)
                T1v = T1.rearrange("p (c t) -> p c t", t=32)
                sc.tt("dve", T1v, T3v[:, :, 31:32].broadcast_to([128, 64, 32]), T3v, ALU.subtract)
                sc.act(T1, T1, AF.Exp)
                sc.tt("dve", Kh, T2, T1, ALU.mult)
                for tq in range(4):
                    ps = PS[tq % 2]
                    for j in range(4):
                        tt_ = tq * 4 + j
                        for c in range(16):
                            sc.mm(ps[:, j * 128:(j + 1) * 128], HT[:, c, tt_ * 128:(tt_ + 1) * 128], W[:, c, 2, :],
                                  start=(c == 0), stop=(c == 15))
                    sc.copy("act", V[:, tq * 4:(tq + 1) * 4, :], ps[:, :].rearrange("p (a b) -> p a b", a=4))
                for tg in range(4):
                    cols = slice(tg * 512, (tg + 1) * 512)
                    ps = PS[tg % 2]
                    proj_f(ps[:, :], lambda c: W[:, c, 3, :], tg)
                    sc.act(T2[:, cols], ps[:, :], AF.Silu)
                sc.memset("pool", ST[0], 0.0)
                sc.memset("pool", STB[0], 0.0)
                gcol = vcol(layer, V_HGG, h)

                def stF(tt_):
                    tcols = slice(tt_ * 128, (tt_ + 1) * 128)
                    b = tt_ % 2
                    trp = PSB[:, b * 128:b * 128 + 128]
                    sc.tr(trp, Kh[:, tcols], ident_b)
                    for c4 in range(4):
                        cm = CSTF[:, CST_CMASK + c4:CST_CMASK + c4 + 1]
                        sc.act(KM[b][:, c4, :], trp, AF.Copy, scale=cm)
                    sc.mm(PS[2][:, 0:128], Kt[:, tcols], Qt[:, tcols])
                    sc.copy("act", SCF[b], PS[2][:, 0:128])
                    sc.tt("pool", SC_[b], SCF[b], hgmask_f, ALU.mult)
                    kvb = PS[5 + b]
                    for c4 in range(4):
                        sc.mm(kvb[:, c4 * 128:(c4 + 1) * 128], KM[b][:, c4, :], V[:, tt_, :])

                def stR(tt_):
                    b = tt_ % 2
                    op_ = PS[3 + b]
                    kvb = PS[5 + b]
                    sc.mm(op_[:, 0:128], V[:, tt_, :], SC_[b], start=True, stop=False)
                    for c4 in range(4):
                        ch = tt_ * 4 + c4
                        sc.mm(op_[:, c4 * 32:(c4 + 1) * 32], STB[ch % 4], Qt[:, ch * 32:(ch + 1) * 32],
                              start=False, stop=(c4 == 3))
                        sc.stt(ST[(ch + 1) % 4], ST[ch % 4], EGL[:, ch:ch + 1], kvb[:, c4 * 128:(c4 + 1) * 128],
                               ALU.mult, ALU.add)
                        sc.copy("pool", STB[(ch + 1) % 4], ST[(ch + 1) % 4])

                def stN1(tt_):
                    b = tt_ % 2
                    op_ = PS[3 + b]
                    sc.act(TSQ, op_[:, 0:128], AF.Square)
                    sc.copy("act", OSB[b], op_[:, 0:128])
                    sc.mm(PS[2][:, 128:256], ones_f, TSQ)
                    sc.act(TRS, PS[2][:, 128:256], AF.Ln, scale=1.0 / 128.0, bias=EPS)
                    sc.act(TRIB[b], TRS, AF.Exp, scale=-0.5)

                def stN2(tt_):
                    tcols = slice(tt_ * 128, (tt_ + 1) * 128)
                    b = tt_ % 2
                    sc.tt("pool", TO, OSB[b], TRIB[b], ALU.mult)
                    sc.tt("pool", TSG, TO, T2[:, tcols], ALU.mult)
                    sc.ts("pool", OUT[:, tcols], TSG, gcol, 1.0, ALU.mult, ALU.mult)

                stF(0)
                stF(1)
                for tt_ in range(16):
                    stR(tt_)
                    if tt_ + 2 < 16:
                        stF(tt_ + 2)
                    if tt_ >= 2:
                        stN2(tt_ - 2)
                    if tt_ >= 1:
                        stN1(tt_ - 1)
                stN2(14)
                stN1(15)
                stN2(15)
                sc.dma("sp", mixd[h * 128:(h + 1) * 128, :], OUT)

            job(load, compute)

    def sb_stage(layer):
        st = {}
        scale = 128.0 ** -0.5
        PSTAT = PSB[:, :].bitcast(F32)

        def alloc_all():
            att.reset()
            st["QT"] = [att.alloc([128, S], BF16) for _ in range(2)]
            st["KT"] = [att.alloc([128, S], BF16) for _ in range(2)]
            st["V"] = [att.alloc([128, 16, 128], BF16) for _ in range(2)]
            st["OUT"] = [att.alloc([128, S], BF16) for _ in range(2)]
            st["E"] = [att.alloc([128, 512], F32) for _ in range(2)]
            st["SPb"] = [att.alloc([128, 512], BF16) for _ in range(3)]
            st["LL"] = [[att.alloc([128, 512], BF16) for _ in range(3)] for _ in range(2)]
            st["A"] = [att.alloc([128, 512], BF16) for _ in range(2)]
            st["AD"] = [None] + [att.alloc([128, 512], BF16) for _ in range(3)]
            st["TSQ"] = att.alloc([128, 512], F32)
            st["TRS"] = att.alloc([128, 512], F32)
            st["TRI"] = att.alloc([128, 512], F32)
            for i in range(1, 4):
                sc.memset("pool", st["AD"][i][:, 0:i * 128], 0.0)

        def proj_units(slot, hs):
            W = wview(slot, [16, 3, 128])
            QT, KT, V = st["QT"][hs], st["KT"][hs], st["V"][hs]
            units = []
            for tg in range(4):
                cols = slice(tg * 512, (tg + 1) * 512)

                def uq(tg=tg, cols=cols):
                    proj_f(PS[1][:, :], lambda c: W[:, c, 0, :], tg)
                    sc.ts("dve", QT[:, cols], PS[1][:, :], scale, None, ALU.mult)

                def uk(tg=tg, cols=cols):
                    proj_f(PS[1][:, :], lambda c: W[:, c, 1, :], tg)
                    sc.copy("dve", KT[:, cols], PS[1][:, :])

                units += [uq, uk]
            for tq in range(4):
                for j in range(4):
                    def uv(tq=tq, j=j):
                        tt_ = tq * 4 + j
                        for c in range(16):
                            sc.mm(PS[0][:, j * 128:(j + 1) * 128], HT[:, c, tt_ * 128:(tt_ + 1) * 128], W[:, c, 2, :],
                                  start=(c == 0), stop=(c == 15))
                        if j == 3:
                            sc.copy("dve", V[:, tq * 4:(tq + 1) * 4, :], PS[0][:, :].rearrange("p (a b) -> p a b", a=4))
                    units.append(uv)
            return units

        def attn(h, units):
            hs = h % 2
            QT, KT, V, OUT = st["QT"][hs], st["KT"][hs], st["V"][hs], st["OUT"][hs]
            E, SPb, LL, A, AD = st["E"], st["SPb"], st["LL"], st["A"], st["AD"]
            pairs = []
            for qg in range(4):
                nkb = 4 * qg + 4
                for n, kb in enumerate(range(nkb - 1, -1, -1)):
                    i = kb - 4 * qg
                    pairs.append(dict(qg=qg, n=n, kb=kb, i=i, c0=(i * 128 if i > 0 else 0), diag=(i >= 0),
                                      nkb=nkb, idx=len(pairs)))
            ot = PS[6]

            def stA(p):
                idx, c0, qg, n, kb = p["idx"], p["c0"], p["qg"], p["n"], p["kb"]
                if n == 0:
                    for b_ in LL[qg % 2]:
                        sc.memset("pool", b_, 0.0)
                zp = PS[2 + idx % 2]
                e_ = E[idx % 2]
                sp = SPb[idx % 3]
                Lcur = LL[qg % 2][n % 3]
                Lnext = LL[qg % 2][(n + 1) % 3]
                q0 = qg * 512 + c0
                q1 = (qg + 1) * 512
                kcols = slice(kb * 128, (kb + 1) * 128)
                sc.mm(zp[:, c0:512], KT[:, kcols], QT[:, q0:q1])
                sc.act(e_[:, c0:512], zp[:, c0:512], AF.Exp)
                sc.act(sp[:, c0:512], e_[:, c0:512], AF.Ln, bias=1.0)
                if p["diag"]:
                    sc.tt("dve", sp[:, c0:c0 + 128], sp[:, c0:c0 + 128], strict_b, ALU.mult)
                if n + 1 < p["nkb"]:
                    sc.tt("pool", Lnext[:, c0:512], Lcur[:, c0:512], sp[:, c0:512], ALU.add)

            def stB(p):
                idx, c0, qg, n, kb, i = p["idx"], p["c0"], p["qg"], p["n"], p["kb"], p["i"]
                cp = PS[4 + idx % 2]
                sp = SPb[idx % 3]
                Lcur = LL[qg % 2][n % 3]
                a_ = AD[i] if i > 0 else A[idx % 2]
                q0 = qg * 512 + c0
                q1 = (qg + 1) * 512
                kcols = slice(kb * 128, (kb + 1) * 128)
                sc.mm(cp[:, c0:512], negtri_b, sp[:, c0:512], start=True, stop=False)
                if n > 0:
                    sc.mm(cp[:, c0:512], negones_b, Lcur[:, c0:512], start=False, stop=False)
                sc.mm(cp[:, c0:512], KT[:, kcols], QT[:, q0:q1], start=False, stop=True)
                sc.act(a_[:, c0:512], cp[:, c0:512], AF.Exp)
                if p["diag"]:
                    sc.tt("dve", a_[:, c0:c0 + 128], a_[:, c0:c0 + 128], strict_b, ALU.mult)
                sc.mm(ot[:, :], V[:, kb, :], a_[:, :], start=(n == 0), stop=(n == p["nkb"] - 1))
                if n == p["nkb"] - 1:
                    group_norm_out(ot[:, :], 512, vcol(layer, V_SBG, h), OUT[:, qg * 512:(qg + 1) * 512],
                                   st["TSQ"], st["TRS"], st["TRI"], PSTAT)

            units = list(units)
            stA(pairs[0])
            for k_, p in enumerate(pairs):
                if k_ + 1 < len(pairs):
                    stA(pairs[k_ + 1])
                stB(p)
                if units:
                    units.pop(0)()
            while units:
                units.pop(0)()
            sc.dma("sp", mixd[(4 + h) * 128:(5 + h) * 128, :], OUT)

        def wload(hh):
            def load(slot):
                W = wview(slot, [16, 3, 128])
                for j in range(3):
                    a = 2048 + j * 768 + hh * 128
                    sc.dma("pool", W[:, :, j, :], w_in_cols(layer, a, a + 128))
            return load

        def pre(slot):
            alloc_all()
            for u in proj_units(slot, 0):
                u()

        job(wload(0), pre)
        for h in range(6):
            if h < 5:
                job(wload(h + 1), lambda slot, h=h: attn(h, proj_units(slot, (h + 1) % 2)))
            else:
                job(None, lambda slot, h=h: attn(h, []))

    def mla_stage(layer):
        att.reset()
        CQN = att.alloc([128, 4, S], BF16)
        KVN = att.alloc([128, 2, S], BF16)
        KR2 = att.alloc([128, S], BF16)
        mla_base = att.off
        uq_v = w_uq[layer].rearrange("(c p) (h e) -> p c h e", p=128, e=192)
        ukv_v = w_ukv[layer].rearrange("(c p) (h e) -> p c h e", p=128, e=256)

        def load_q(slot):
            sc.dma("pool", wview(slot, [16, 512]), w_in_cols(layer, 4352, 4864))

        import os as _os

        def comp_q(slot):
            att.off = mla_base
            W = wview(slot, [16, 512])
            CQ = att.alloc([128, 4, 512], F32)
            TSQ = att.alloc([128, 512], F32)
            TRS = att.alloc([128, 512], F32)
            TRI = att.alloc([128, 512], F32)
            CUT = int(_os.environ.get("MLA_CUT", "9"))
            for tg in range(4):
                cols = slice(tg * 512, (tg + 1) * 512)
                for j in range(4):
                    ps = PS[j % 2]
                    proj_f(ps[:, :], lambda c: W[:, c, j * 128:(j + 1) * 128], tg)
                    sc.copy("dve", CQ[:, j, :], ps[:, :])
                    if CUT >= 2:
                        sc.act(TSQ, CQ[:, j, :], AF.Square)
                        sc.mm(PS[6][:, :], ones_f, TSQ, start=(j == 0), stop=(j == 3))
                if CUT >= 3:
                    sc.act(TRS, PS[6][:, :], AF.Sqrt, scale=1.0 / 512.0, bias=EPS)
                    sc.recip(TRI, TRS)
                if CUT >= 4:
                    for j in range(4):
                        sc.stt(CQN[:, j, cols], CQ[:, j, :], vcol(layer, V_MLAQG, j), TRI, ALU.mult, ALU.mult)

        import os as _os
        if 'q' in _os.environ.get('MLA_PRE', 'qk'):
            job(load_q, comp_q)

        def load_kv(slot):
            W = wview(slot, [16, 384])
            sc.dma("pool", W[:, :, 0:256], w_in_cols(layer, 4864, 5120))
            sc.dma("pool", W[:, :, 256:320], w_in_cols(layer, 5120, 5184))
            sc.dma("pool", W[:, :, 320:352], w_in_cols(layer, 5152, 5184))
            sc.dma("pool", W[:, :, 352:384], w_in_cols(layer, 5120, 5152))

        def comp_kv(slot):
            att.off = mla_base
            W = wview(slot, [16, 384])
            CK = att.alloc([128, 2, 512], F32)
            TSQ = att.alloc([128, 512], F32)
            TRS = att.alloc([128, 512], F32)
            TRI = att.alloc([128, 512], F32)
            KRT = att.alloc([128, 512], F32)
            for tg in range(4):
                cols = slice(tg * 512, (tg + 1) * 512)
                for j in range(2):
                    ps = PS[j % 2]
                    proj_f(ps[:, :], lambda c: W[:, c, j * 128:(j + 1) * 128], tg)
                    sc.copy("dve", CK[:, j, :], ps[:, :])
                    sc.act(TSQ, CK[:, j, :], AF.Square)
                    sc.mm(PS[6][:, :], ones_f, TSQ, start=(j == 0), stop=(j == 1))
                sc.act(TRS, PS[6][:, :], AF.Sqrt, scale=1.0 / 256.0, bias=EPS)
                sc.recip(TRI, TRS)
                for j in range(2):
                    sc.stt(KVN[:, j, cols], CK[:, j, :], vcol(layer, V_MLAKVG, j), TRI, ALU.mult, ALU.mult)
                ps = PS[2]
                proj_f(ps[:, :], lambda c: W[:, c, 256:384], tg)
                sc.tt("dve", KRT, ps[:, :], CS[:, cols], ALU.mult)
                sc.mm(PS[3][:, :], fold_f, KRT)
                sc.copy("act", KR2[:, cols], PS[3][:, :])

        if 'k' in _os.environ.get('MLA_PRE', 'qk'):
            job(load_kv, comp_kv)

        for h in range(int(_os.environ.get("MLA_HEADS", "6"))):
            def load(slot, h=h):
                Wq = wview(slot, [4, 256])
                Wk = slot[:, 1024:1024 + 512].rearrange("p (a b) -> p a b", a=2)
                sc.dma("pool", Wq[:, :, 0:128], uq_v[:, :, h, 0:128])
                sc.dma("pool", Wq[:, :, 128:192], uq_v[:, :, h, 128:192])
                sc.dma("pool", Wq[:, :, 192:224], uq_v[:, :, h, 160:192])
                sc.dma("pool", Wq[:, :, 224:256], uq_v[:, :, h, 128:160])
                sc.dma("pool", Wk, ukv_v[:, :, h, :])

            def compute(slot, h=h):
                att.off = mla_base
                Wq = wview(slot, [4, 256])
                Wk = slot[:, 1024:1024 + 512].rearrange("p (a b) -> p a b", a=2)
                QN = att.alloc([128, S], BF16)
                QR = att.alloc([128, S], BF16)
                KN = att.alloc([128, S], BF16)
                V = att.alloc([128, 16, 128], BF16)
                OUT = att.alloc([128, S], BF16)
                A = [att.alloc([128, 512], BF16) for _ in range(2)]
                AD = [None] + [att.alloc([128, 512], BF16) for _ in range(3)]
                TO = att.alloc([128, 512], F32)
                TSQ = att.alloc([128, 512], F32)
                TRS = att.alloc([128, 512], F32)
                TRI = att.alloc([128, 512], F32)
                for i in range(1, 4):
                    sc.memset("pool", AD[i][:, 0:i * 128], 0.0)
                for tg in range(4):
                    cols = slice(tg * 512, (tg + 1) * 512)
                    ps = PS[0]
                    proj_f(ps[:, :], lambda c: Wq[:, c, 0:128], tg, nk=4, rhs_fn=lambda c: CQN[:, c, cols])
                    sc.copy("act", QN[:, cols], ps[:, :])
                    ps = PS[1]
                    proj_f(ps[:, :], lambda c: Wq[:, c, 128:256], tg, nk=4, rhs_fn=lambda c: CQN[:, c, cols])
                    sc.tt("dve", QR[:, cols], ps[:, :], CS[:, cols], ALU.mult)
                    ps = PS[2]
                    proj_f(ps[:, :], lambda c: Wk[:, c, 0:128], tg, nk=2, rhs_fn=lambda c: KVN[:, c, cols])
                    sc.copy("act", KN[:, cols], ps[:, :])
                for tq in range(4):
                    ps = PS[tq % 2]
                    for j in range(4):
                        tt_ = tq * 4 + j
                        for c in range(2):
                            sc.mm(ps[:, j * 128:(j + 1) * 128], KVN[:, c, tt_ * 128:(tt_ + 1) * 128], Wk[:, c, 128:256],
                                  start=(c == 0), stop=(c == 1))
                    sc.copy("dve", V[:, tq * 4:(tq + 1) * 4, :], ps[:, :].rearrange("p (a b) -> p a b", a=4))
                scale = 192.0 ** -0.5
                pairs = []
                for qg in range(4):
                    nkb = 4 * qg + 4
                    for kb in range(nkb):
                        i = kb - 4 * qg
                        pairs.append(dict(qg=qg, kb=kb, i=i, c0=(i * 128 if i > 0 else 0), diag=(i >= 0),
                                          nkb=nkb, idx=len(pairs)))
                ot = PS[5]
                den = PS[6]

                def abuf(p):
                    return AD[p["i"]] if p["i"] > 0 else A[p["idx"] % 2]

                def stA(p):
                    idx, c0, qg, kb = p["idx"], p["c0"], p["qg"], p["kb"]
                    sp_ = PS[2 + idx % 2]
                    a_ = abuf(p)
                    q0 = qg * 512 + c0
                    q1 = (qg + 1) * 512
                    kcols = slice(kb * 128, (kb + 1) * 128)
                    sc.mm(sp_[:, c0:512], KN[:, kcols], QN[:, q0:q1], start=True, stop=False)
                    sc.mm(sp_[:, c0:512], KR2[:, kcols], QR[:, q0:q1], start=False, stop=True)
                    sc.act(a_[:, c0:512], sp_[:, c0:512], AF.Exp, scale=scale)
                    if p["diag"]:
                        sc.memset("pool", a_[64:128, c0:c0 + 64], 0.0)

                def stB(p):
                    qg, kb = p["qg"], p["kb"]
                    a_ = abuf(p)
                    sc.mm(ot[:, :], V[:, kb, :], a_[:, :], start=(kb == 0), stop=(kb == p["nkb"] - 1))
                    sc.mm(den[:, :], ones_b, a_[:, :], start=(kb == 0), stop=(kb == p["nkb"] - 1))
                    if kb == p["nkb"] - 1:
                        sc.recip(TRI, den[:, :])
                        sc.tt("dve", TO, ot[:, :], TRI, ALU.mult)
                        group_norm_out(TO, 512, vcol(layer, V_MLAOG, h), OUT[:, qg * 512:(qg + 1) * 512],
                                       TSQ, TRS, TRI, PS[4])

                stA(pairs[0])
                for k_, p in enumerate(pairs):
                    if k_ + 1 < len(pairs):
                        stA(pairs[k_ + 1])
                    stB(p)
                sc.dma("sp", mixd[(10 + h) * 128:(11 + h) * 128, :], OUT)

            job(load, compute)

    def wo_stage(layer, src):
        state = {}
        mixv = mixd.rearrange("(c p) n -> p c n", p=128)

        for dcg in range(4):
            def load(slot, dcg=dcg):
                sc.dma("pool", wview(slot, [16, 512]),
                       w_o[layer].rearrange("(c p) n -> p c n", p=128)[:, :, dcg * 512:(dcg + 1) * 512])

            def compute(slot, dcg=dcg):
                if dcg == 0:
                    att.reset()
                    state["MX"] = [att.alloc([128, 16, 512], BF16) for _ in range(2)]
                    state["XB"] = [att.alloc([128, 512], F32) for _ in range(8)]
                    state["k"] = 0
                    sc.dma("sp", state["MX"][0], mixv[:, :, 0:512])
                MX = state["MX"]
                XB = state["XB"]
                W = wview(slot, [16, 512])
                for tg in range(4):
                    cols = slice(tg * 512, (tg + 1) * 512)
                    n = dcg * 4 + tg
                    mx = MX[n % 2]
                    xbs = []
                    for j in range(4):
                        dc = dcg * 4 + j
                        xb = XB[state["k"] % 8]
                        state["k"] += 1
                        sc.dma("sp", xb, src[dc * 128:(dc + 1) * 128, cols])
                        xbs.append(xb)
                    if n + 1 < 16:
                        tg2 = (tg + 1) % 4
                        sc.dma("sp", MX[(n + 1) % 2], mixv[:, :, tg2 * 512:(tg2 + 1) * 512])
                    for j in range(4):
                        dc = dcg * 4 + j
                        ps = PS[j % 4]
                        for mc in range(16):
                            sc.mm(ps[:, :], W[:, mc, j * 128:(j + 1) * 128], mx[:, mc, :], start=(mc == 0), stop=(mc == 15))
                        sc.tt("dve", xbs[j], ps[:, :], xbs[j], ALU.add)
                        sc.dma("sp", xres[dc * 128:(dc + 1) * 128, cols], xbs[j])

            job(load, compute)

    def ffn_stage(layer):
        state = {}
        wg = w_gate[layer].rearrange("(c p) n -> p c n", p=128)
        wu = w_up[layer].rearrange("(c p) n -> p c n", p=128)
        wd = w_down[layer].rearrange("(c p) n -> p c n", p=128)
        for tt_ in range(4):
            cols = slice(tt_ * 512, (tt_ + 1) * 512)
            for fcp in range(22):
                def load(slot, fcp=fcp):
                    W = wview(slot, [16, 2, 256])
                    sc.dma("pool", W[:, :, 0, :], wg[:, :, fcp * 256:(fcp + 1) * 256])
                    sc.dma("pool", W[:, :, 1, :], wu[:, :, fcp * 256:(fcp + 1) * 256])

                def compute(slot, fcp=fcp, tt_=tt_, cols=cols):
                    if fcp == 0 and tt_ == 0:
                        att.reset()
                        state["ACT"] = att.alloc([128, NFC, 512], BF16)
                        state["SG"] = [att.alloc([128, 512], F32) for _ in range(2)]
                        state["XB"] = [att.alloc([128, 512], F32) for _ in range(8)]
                        state["k"] = 0
                    W = wview(slot, [16, 2, 256])
                    for j in range(2):
                        fc = fcp * 2 + j
                        pg = PS[j * 2]
                        pu = PS[j * 2 + 1]
                        proj_f(pg[:, :], lambda c: W[:, c, 0, j * 128:(j + 1) * 128], tt_)
                        proj_f(pu[:, :], lambda c: W[:, c, 1, j * 128:(j + 1) * 128], tt_)
                        sg = state["SG"][fc % 2]
                        sc.act(sg, pg[:, :], AF.Silu)
                        sc.tt("dve", state["ACT"][:, fc, :], pu[:, :], sg, ALU.mult)

                job(load, compute)
            for dcp in range(8):
                for half in range(2):
                    def load(slot, dcp=dcp, half=half):
                        sc.dma("pool", wview(slot, [22, 256]), wd[:, half * 22:(half + 1) * 22, dcp * 256:(dcp + 1) * 256])

                    def compute(slot, dcp=dcp, half=half, cols=cols):
                        W = wview(slot, [22, 256])
                        if half == 0:
                            state["xbs"] = []
                            for j in range(2):
                                dc = dcp * 2 + j
                                xb = state["XB"][state["k"] % 8]
                                state["k"] += 1
                                sc.dma("sp", xb, xres[dc * 128:(dc + 1) * 128, cols])
                                state["xbs"].append(xb)
                        for j in range(2):
                            ps = PS[4 + j]
                            for f in range(22):
                                fc = half * 22 + f
                                sc.mm(ps[:, :], W[:, f, j * 128:(j + 1) * 128], state["ACT"][:, fc, :],
                                      start=(fc == 0), stop=(fc == NFC - 1))
                        if half == 1:
                            for j in range(2):
                                dc = dcp * 2 + j
                                ps = PS[4 + j]
                                xb = state["xbs"][j]
                                sc.tt("dve", xb, ps[:, :], xb, ALU.add)
                                sc.dma("sp", xres[dc * 128:(dc + 1) * 128, cols], xb)

                    job(load, compute)

    def plain(fn):
        job(None, lambda slot: fn())

    stages = stages or ("norm", "hg", "sb", "mla", "wo", "ffn", "final")
    for layer in range(nlayers):
        src = xT_in if layer == 0 else xres
        if "norm" in stages:
            plain(lambda layer=layer, src=src: norm_stage(src, lambda c: vcol(layer, V_ATTN, c)))
            if debug and layer == 0 and "hT" in dbg:
                plain(lambda: sc.dma("sp", dbg["hT"].rearrange("(c p) n -> p c n", p=128), HT[:, :, :], is_out=True))
        if "hg" in stages:
            hg_stage(layer)
        if "sb" in stages:
            sb_stage(layer)
        if "mla" in stages:
            mla_stage(layer)
        if debug and layer == 0 and "mix" in dbg:
            def dump_mix():
                att.reset()
                t = att.alloc([128, 16, 512], BF16)
                for tg in range(4):
                    sc.dma("sp", t, mixd.rearrange("(c p) n -> p c n", p=128)[:, :, tg * 512:(tg + 1) * 512])
                    sc.dma("sp", dbg["mix"].rearrange("(c p) n -> p c n", p=128)[:, :, tg * 512:(tg + 1) * 512], t, is_out=True)
            plain(dump_mix)
        if "wo" in stages:
            wo_stage(layer, src)

        def dump_x(key):
            att.reset()
            t = att.alloc([128, 2048], F32)
            for c in range(16):
                sc.dma("sp", t, xres[c * 128:(c + 1) * 128, :])
                sc.dma("sp", dbg[key][c * 128:(c + 1) * 128, :], t, is_out=True)
        if debug and layer == 0 and "x1" in dbg:
            plain(lambda: dump_x("x1"))
        if "ffn" in stages:
            plain(lambda layer=layer: norm_stage(xres, lambda c: vcol(layer, V_FFN, c)))
            ffn_stage(layer)
        if debug and layer == 0 and "x2" in dbg:
            plain(lambda: dump_x("x2"))
    if "final" in stages:
        plain(lambda: norm_stage(xres, lambda c: VEC[:, V_FINAL + c:V_FINAL + c + 1], dst_dram=outT))

    wjobs = [k for k, (ld, _) in enumerate(jobs) if ld is not None]
    slot_of = {k: WS[n % NSLOT][:, :] for n, k in enumerate(wjobs)}
    issued = [0]

    def issue_upto(n):
        while issued[0] < min(n, len(wjobs)):
            k = wjobs[issued[0]]
            jobs[k][0](slot_of[k])
            issued[0] += 1

    nw = 0
    for k, (ld, comp) in enumerate(jobs):
        if ld is not None:
            issue_upto(nw + NSLOT)
            nw += 1
        else:
            issue_upto(nw + NSLOT - 1)
        comp(slot_of.get(k))

    sc.emit_all()
    nc._in_names = {"xT", "pos", "vec", "cst"} | ({"w_in"} if w_in is not None else set()) | \
        ({"mla_w_uq", "mla_w_ukv"} if w_uq is not None else set()) | ({"w_o"} if w_o is not None else set()) | \
        ({"w_gate", "w_up", "w_down"} if w_gate is not None else set())
    nc._nops = len(sc.ops)
    return nc


_PROG_CACHE = {}


def _get_prog(debug=False, nlayers=DEPTH, stages=None):
    key = (debug, nlayers, stages)
    if key not in _PROG_CACHE:
        _PROG_CACHE[key] = build_program(debug=debug, nlayers=nlayers, stages=stages)
    return _PROG_CACHE[key]


def _pack_vec(inp):
    v = np.zeros((128, V_N), np.float32)

    def cols(a):
        a = np.asarray(a, np.float32)
        return a.reshape(-1, 128).T

    for l in range(DEPTH):
        b = l * VPL
        v[:, b + V_ATTN:b + V_ATTN + 16] = cols(inp["attn_norm_g"][l])
        v[:, b + V_FFN:b + V_FFN + 16] = cols(inp["ffn_norm_g"][l])
        v[:, b + V_HGG:b + V_HGG + 4] = cols(inp["hg_norm_g"][l])
        v[:, b + V_SBG:b + V_SBG + 6] = cols(inp["sb_norm_g"][l])
        v[:, b + V_MLAOG:b + V_MLAOG + 6] = cols(inp["mla_out_norm_g"][l])
        v[:, b + V_MLAQG:b + V_MLAQG + 4] = cols(inp["mla_q_norm_g"][l])
        v[:, b + V_MLAKVG:b + V_MLAKVG + 2] = cols(inp["mla_kv_norm_g"][l])
        v[:, b + V_LB:b + V_LB + 4] = cols(inp["hg_lower_bounds"][l])
    v[:, V_FINAL:V_FINAL + 16] = cols(inp["final_norm_g"])
    return v


def _in_maps(inp, names=None, ncores=NCORES):
    x = np.asarray(inp["x"], np.float32)
    pos = np.asarray(inp["positions"], np.int32)
    vec = _pack_vec(inp)
    cst = _make_cst()
    shared = {
        "vec": vec, "cst": cst,
        "w_in": np.ascontiguousarray(inp["w_in"], dtype=np.float32),
        "mla_w_uq": np.ascontiguousarray(inp["mla_w_uq"], dtype=np.float32),
        "mla_w_ukv": np.ascontiguousarray(inp["mla_w_ukv"], dtype=np.float32),
        "w_o": np.ascontiguousarray(inp["w_o"], dtype=np.float32),
        "w_gate": np.ascontiguousarray(inp["w_gate"], dtype=np.float32),
        "w_up": np.ascontiguousarray(inp["w_up"], dtype=np.float32),
        "w_down": np.ascontiguousarray(inp["w_down"], dtype=np.float32),
    }
    if names is not None:
        shared = {k: v for k, v in shared.items() if k in names}
    maps = []
    for b in range(ncores):
        m = dict(shared)
        m["xT"] = np.ascontiguousarray(x[b].T)
        m["pos"] = np.ascontiguousarray(pos[b:b + 1])
        maps.append(m)
    return maps


def kernel(**inputs):
    nc = _get_prog()
    res = run_bass_kernel_spmd(nc, _in_maps(inputs), core_ids=list(range(NCORES)))
    out = np.stack([np.ascontiguousarray(np.asarray(r["outT"], np.float32).T) for r in res.results], axis=0)
    return out
```

```python
import math
import numpy as np
import concourse.bass as bass
import concourse.mybir as mybir
from concourse.bass_utils import run_bass_kernel_spmd

F32 = mybir.dt.float32
BF16 = mybir.dt.bfloat16
I32 = mybir.dt.int32
AF = mybir.ActivationFunctionType
ALU = mybir.AluOpType

S = 2048
D = 2048
DEPTH = 2
NCORES = 8
EPS = 1e-6
IN_COLS = 5184
D_FF = 5632
NFC = D_FF // 128
P = 128

_ESZ = {str(F32): 4, str(BF16): 2, str(I32): 4}


def _esz(dt):
    return _ESZ[str(dt)]


def _box(ap):
    t = ap.tensor
    dims = [(int(s), int(c)) for s, c in ap.ap]
    es = _esz(ap.dtype)
    off = int(ap.offset)
    if type(t).__name__.startswith("DRam"):
        ext = sum((c - 1) * abs(s) for s, c in dims)
        return (t.name, 0, 1, off * es, (off + ext + 1) * es)
    pstep, pc = dims[0]
    if pstep == 0:
        pstep = 1 << 40
    p0 = off // pstep
    f0 = off % pstep
    ext = sum((c - 1) * abs(s) for s, c in dims[1:])
    return (t.name, p0, p0 + pc, f0 * es, (f0 + ext + 1) * es)


class Sched:
    ENG = ("pe", "dve", "act", "pool", "sp")

    def __init__(self, nc):
        self.nc = nc
        self.ops = []
        self.recs = {}
        self.psum_last = {}
        self.out_dmas = []

    def _access(self, box, idx, w, eng, dma, deps):
        name, p0, p1, f0, f1 = box
        recs = self.recs.get(name)
        if recs is None:
            recs = self.recs[name] = []
        new = []
        for r in recs:
            rp0, rp1, rf0, rf1, ridx, rw, reng, rdma = r
            ov = rp0 < p1 and p0 < rp1 and rf0 < f1 and f0 < rf1
            if ov and (w or rw) and ridx != idx:
                deps.add(ridx)
            if ridx == idx:
                new.append(r)
                continue
            if w and rp0 >= p0 and rp1 <= p1 and rf0 >= f0 and rf1 <= f1:
                continue
            if (not w) and (not rw) and reng == eng and (not dma) and (not rdma) \
                    and rp0 == p0 and rp1 == p1 and rf0 == f0 and rf1 == f1:
                continue
            new.append(r)
        new.append((p0, p1, f0, f1, idx, w, eng, dma))
        self.recs[name] = new

    def _psum_access(self, name, idx, w, eng, deps):
        r = self.psum_last.get(name)
        if r is not None and r[0] != idx:
            ridx, rw, reng = r
            if not (reng == eng and not w and not rw):
                deps.add(ridx)
        if r is not None and r[0] == idx:
            self.psum_last[name] = (idx, w or r[1], eng)
        else:
            self.psum_last[name] = (idx, w, eng)

    def add(self, eng, emit, reads, writes, dma=False, is_out=False):
        idx = len(self.ops)
        deps = set()
        for ap in reads:
            if type(ap.tensor).__name__.startswith("PSum"):
                self._psum_access(ap.tensor.name, idx, False, eng, deps)
            else:
                self._access(_box(ap), idx, False, eng, dma, deps)
        for ap in writes:
            if type(ap.tensor).__name__.startswith("PSum"):
                self._psum_access(ap.tensor.name, idx, True, eng, deps)
            else:
                self._access(_box(ap), idx, True, eng, dma, deps)
        op = dict(idx=idx, eng=eng, emit=emit, deps=deps, dma=dma, signal=False, sem=None, val=0)
        self.ops.append(op)
        if is_out:
            self.out_dmas.append(idx)
        return idx

    def mm(self, out, lhsT, rhs, start=True, stop=True):
        self.add("pe", lambda e: e.matmul(out, lhsT, rhs, start=start, stop=stop), [lhsT, rhs], [out])

    def tr(self, out, in_, ident):
        self.add("pe", lambda e: e.transpose(out, in_, ident), [in_, ident], [out])

    def act(self, out, in_, func, scale=None, bias=None):
        kw = {}
        rd = [in_]
        if scale is not None:
            kw["scale"] = scale
            if not isinstance(scale, (int, float)):
                rd.append(scale)
        if bias is not None:
            kw["bias"] = bias
            if not isinstance(bias, (int, float)):
                rd.append(bias)
        self.add("act", lambda e: e.activation(out, in_, func, **kw), rd, [out])

    def _veng(self, eng):
        return eng

    def tt(self, eng, out, in0, in1, op):
        self.add(eng, lambda e: e.tensor_tensor(out, in0, in1, op), [in0, in1], [out])

    def ts(self, eng, out, in0, s1, s2, op0, op1=None):
        rd = [in0]
        if not isinstance(s1, (int, float)):
            rd.append(s1)
        if s2 is not None and not isinstance(s2, (int, float)):
            rd.append(s2)
        if op1 is None:
            self.add(eng, lambda e: e.tensor_scalar(out, in0, s1, None, op0), rd, [out])
        else:
            self.add(eng, lambda e: e.tensor_scalar(out, in0, s1, s2, op0, op1), rd, [out])

    def stt(self, out, in0, scalar, in1, op0, op1):
        rd = [in0, in1]
        if not isinstance(scalar, (int, float)):
            rd.append(scalar)
        self.add("dve", lambda e: e.scalar_tensor_tensor(out, in0, scalar, in1, op0, op1), rd, [out])

    def scan(self, out, d0, d1, initial, op0, op1):
        self.add("dve", lambda e: e.tensor_tensor_scan(out, d0, d1, initial, op0, op1), [d0, d1], [out])

    def copy(self, eng, out, in_):
        if eng == "act":
            self.add("act", lambda e: e.copy(out, in_), [in_], [out])
        else:
            self.add(eng, lambda e: e.tensor_copy(out, in_), [in_], [out])

    def memset(self, eng, ap, val):
        self.add(eng, lambda e: e.memset(ap, val), [], [ap])

    def recip(self, out, in_):
        self.add("dve", lambda e: e.reciprocal(out, in_), [in_], [out])

    def dma(self, eng, out, in_, is_out=False):
        self.add(eng, lambda e: e.dma_start(out=out, in_=in_), [in_], [out], dma=True, is_out=is_out)

    def emit_all(self):
        nc = self.nc
        ops = self.ops
        NPOOL = 24
        dsem = [nc.alloc_semaphore("dq%d" % i) for i in range(NPOOL)]
        dcnt = [0] * NPOOL
        dlast = [None] * NPOOL
        nd = 0
        for op in ops:
            if op["dma"]:
                k = nd % NPOOL
                nd += 1
                if dlast[k] is not None:
                    op["deps"].add(dlast[k])
                dcnt[k] += 16
                op["sem"] = dsem[k]
                op["val"] = dcnt[k]
                op["semid"] = ("d", k)
                dlast[k] = op["idx"]
        for op in ops:
            for d in op["deps"]:
                dop = ops[d]
                if dop["dma"]:
                    continue
                if dop["eng"] == "pe" and op["eng"] == "pe" and not op["dma"]:
                    continue
                dop["signal"] = True
        MAXC = 30000
        cur = {}
        for op in ops:
            if op["dma"] or not op["signal"]:
                continue
            e = op["eng"]
            if e not in cur or cur[e][1] >= MAXC:
                gen = 0 if e not in cur else cur[e][2] + 1
                cur[e] = [nc.alloc_semaphore("c_%s_%d" % (e, gen)), 0, gen]
            cur[e][1] += 1
            op["sem"] = cur[e][0]
            op["val"] = cur[e][1]
            op["semid"] = ("c", e, cur[e][2])

        handles = {"pe": "tensor", "dve": "vector", "act": "scalar", "pool": "gpsimd", "sp": "sync"}

        def emit_engine(ename, e):
            waited = {}
            last = None
            for op in ops:
                if op["eng"] != ename:
                    continue
                need = {}
                for d in op["deps"]:
                    dop = ops[d]
                    if (not dop["dma"]) and dop["eng"] == "pe" and ename == "pe" and not op["dma"]:
                        continue
                    sid = dop["semid"]
                    if waited.get(sid, 0) >= dop["val"]:
                        continue
                    if sid not in need or need[sid][1] < dop["val"]:
                        need[sid] = (dop["sem"], dop["val"])
                for sid, (sem, val) in need.items():
                    e.wait_ge(sem, val)
                    waited[sid] = val
                ins = op["emit"](e)
                if op["dma"]:
                    ins.then_inc(op["sem"], 16)
                elif op["signal"]:
                    ins.then_inc(op["sem"], 1)
            if ename == "sp":
                fin = {}
                for d in self.out_dmas:
                    dop = ops[d]
                    sid = dop["semid"]
                    if sid not in fin or fin[sid][1] < dop["val"]:
                        fin[sid] = (dop["sem"], dop["val"])
                for sid, (sem, val) in fin.items():
                    if waited.get(sid, 0) < val:
                        e.wait_ge(sem, val)

        with nc.Block() as block:
            @block.tensor
            def _(e):
                emit_engine("pe", e)

            @block.vector
            def _(e):
                emit_engine("dve", e)

            @block.scalar
            def _(e):
                emit_engine("act", e)

            @block.gpsimd
            def _(e):
                emit_engine("pool", e)

            @block.sync
            def _(e):
                emit_engine("sp", e)


class Arena:
    def __init__(self, t, nbytes):
        self.t = t
        self.nbytes = nbytes
        self.off = 0

    def reset(self):
        self.off = 0

    def alloc(self, shape, dt):
        es = _esz(dt)
        n = 1
        for s in shape[1:]:
            n *= s
        nb = (n * es + 63) // 64 * 64
        assert self.off + nb <= self.nbytes, ("arena overflow", self.off, nb, self.nbytes)
        a = self.t[0:shape[0], self.off // 2:(self.off + n * es) // 2]
        self.off += nb
        if dt != BF16:
            a = a.bitcast(dt)
        if len(shape) == 3:
            a = a.rearrange("p (a b) -> p a b", a=shape[1])
        elif len(shape) == 4:
            a = a.rearrange("p (a b c) -> p a b c", a=shape[1], b=shape[2])
        return a


CST_IDENT = 0
CST_ONES = 128
CST_NEGTRI = 256
CST_STRICT = 384
CST_NEGONES = 512
CST_NB = 640
CST_HGMASK = 640
CST_FOLD = 768
CST_CMASK = 896
CST_INVF = 900
CST_PHASE = 901
CST_SIGN = 902
CST_N = 904

VPL = 58
V_ATTN, V_FFN, V_HGG, V_SBG, V_MLAOG, V_MLAQG, V_MLAKVG, V_LB = 0, 16, 32, 36, 42, 48, 52, 54
V_FINAL = 2 * VPL
V_N = 2 * VPL + 16


def _make_cst():
    c = np.zeros((128, CST_N), np.float32)
    i = np.arange(128)
    c[:, CST_IDENT:CST_IDENT + 128] = np.eye(128, dtype=np.float32)
    c[:, CST_ONES:CST_ONES + 128] = 1.0
    c[:, CST_NEGTRI:CST_NEGTRI + 128] = -(i[:, None] >= i[None, :]).astype(np.float32)
    c[:, CST_STRICT:CST_STRICT + 128] = (i[:, None] < i[None, :]).astype(np.float32)
    c[:, CST_NEGONES:CST_NEGONES + 128] = -1.0
    c[:, CST_HGMASK:CST_HGMASK + 128] = ((i[:, None] // 32 == i[None, :] // 32) & (i[:, None] <= i[None, :])).astype(np.float32)
    c[:, CST_FOLD:CST_FOLD + 128] = (i[:, None] % 64 == i[None, :] % 64).astype(np.float32)
    for k in range(4):
        c[:, CST_CMASK + k] = (i // 32 == k).astype(np.float32)
    inv_freq = (np.float32(10000.0) ** (-np.arange(0, 64, 2, dtype=np.float32) / np.float32(64))).astype(np.float32)
    c[:, CST_INVF] = inv_freq[i % 32]
    c[:, CST_PHASE] = np.where(i < 64, np.float32(math.pi / 2), np.float32(0.0))
    c[:, CST_SIGN] = np.where((i >= 64) & (i < 96), -1.0, 1.0)
    return c


def build_program(debug=False, nlayers=DEPTH, stages=None):
    nc = bass.Bass("TRN2", target_bir_lowering=False)
    sc = Sched(nc)

    def dram_in(name, shape, dt=F32):
        return nc.dram_tensor(name, list(shape), dt, kind="ExternalInput").ap()

    need = set(stages or ("norm", "hg", "sb", "mla", "wo", "ffn", "final"))
    xT_in = dram_in("xT", [D, S])
    pos_in = dram_in("pos", [1, S], I32)
    vec_in = dram_in("vec", [128, V_N])
    cst_in = dram_in("cst", [128, CST_N])
    w_in = dram_in("w_in", [DEPTH, D, IN_COLS]) if need & {"hg", "sb", "mla"} else None
    w_uq = dram_in("mla_w_uq", [DEPTH, 512, 1152]) if "mla" in need else None
    w_ukv = dram_in("mla_w_ukv", [DEPTH, 256, 1536]) if "mla" in need else None
    w_o = dram_in("w_o", [DEPTH, D, D]) if "wo" in need else None
    w_gate = dram_in("w_gate", [DEPTH, D, D_FF]) if "ffn" in need else None
    w_up = dram_in("w_up", [DEPTH, D, D_FF]) if "ffn" in need else None
    w_down = dram_in("w_down", [DEPTH, D_FF, D]) if "ffn" in need else None
    outT = nc.dram_tensor("outT", [D, S], F32, kind="ExternalOutput").ap()
    xres = nc.dram_tensor("xres", [D, S], F32, kind="Internal").ap()
    mixd = nc.dram_tensor("mixd", [D, S], BF16, kind="Internal").ap()
    dbg = {}
    if debug:
        st_ = set(stages or ())
        if "norm" in st_:
            dbg["hT"] = nc.dram_tensor("dbg_hT", [D, S], BF16, kind="ExternalOutput").ap()
        if st_ & {"hg", "sb", "mla"}:
            dbg["mix"] = nc.dram_tensor("dbg_mix", [D, S], BF16, kind="ExternalOutput").ap()
        if "wo" in st_:
            dbg["x1"] = nc.dram_tensor("dbg_x1", [D, S], F32, kind="ExternalOutput").ap()
        if "ffn" in st_:
            dbg["x2"] = nc.dram_tensor("dbg_x2", [D, S], F32, kind="ExternalOutput").ap()
        dbg["cs"] = nc.dram_tensor("dbg_cs", [128, S], F32, kind="ExternalOutput").ap()
        dbg["lbt"] = nc.dram_tensor("dbg_lbt", [128, 32], F32, kind="ExternalOutput").ap()

    HT = nc.alloc_sbuf_tensor("HT", [128, 16, S], BF16)
    ATT_BYTES = 68 * 1024
    ATTt = nc.alloc_sbuf_tensor("ATT", [128, ATT_BYTES // 2], BF16)
    att = Arena(ATTt, ATT_BYTES)
    NSLOT = 3
    WS = [nc.alloc_sbuf_tensor("WS%d" % i, [128, 8192], BF16) for i in range(NSLOT)]
    CS = nc.alloc_sbuf_tensor("CS", [128, S], F32)
    VEC = nc.alloc_sbuf_tensor("VEC", [128, V_N], F32)
    CSTF = nc.alloc_sbuf_tensor("CSTF", [128, CST_N], F32)
    CSTB = nc.alloc_sbuf_tensor("CSTB", [128, CST_NB], BF16)
    LBT = nc.alloc_sbuf_tensor("LBT", [128, 32], F32)

    PS = [nc.alloc_psum_tensor("PS%d" % i, [128, 512], F32) for i in range(7)]
    PSB = nc.alloc_psum_tensor("PSB", [128, 1024], BF16)

    ident_b = CSTB[:, 0:128]
    ones_b = CSTB[:, 128:256]
    negtri_b = CSTB[:, 256:384]
    strict_b = CSTB[:, 384:512]
    negones_b = CSTB[:, 512:640]
    ones_f = CSTF[:, CST_ONES:CST_ONES + 128]
    hgmask_f = CSTF[:, CST_HGMASK:CST_HGMASK + 128]
    fold_f = CSTF[:, CST_FOLD:CST_FOLD + 128]

    def vcol(layer, base, j):
        c = layer * VPL + base + j
        return VEC[:, c:c + 1]

    sc.dma("sp", VEC[:, :], vec_in)
    sc.dma("sp", CSTF[:, :], cst_in)
    sc.dma("pool", CSTB[:, :], cst_in[:, 0:CST_NB])

    r0 = VEC[:, V_LB:V_LB + 4]
    r1 = VEC[:, VPL + V_LB:VPL + V_LB + 4]
    sc.tt("dve", LBT[:, 24:28], r0, r1, ALU.subtract)
    sc.act(LBT[:, 0:4], LBT[:, 24:28], AF.Sigmoid)
    sc.act(LBT[:, 4:8], LBT[:, 24:28], AF.Sigmoid, scale=-1.0)
    sc.tt("dve", LBT[:, 8:12], LBT[:, 0:4], LBT[:, 0:4], ALU.subtract)
    sc.tt("dve", LBT[:, 28:32], LBT[:, 0:4], LBT[:, 4:8], ALU.add)
    sc.tt("dve", LBT[:, 12:16], LBT[:, 28:32], LBT[:, 0:4], ALU.subtract)
    sc.ts("dve", LBT[:, 16:24], LBT[:, 8:16], -1.0, 1.0, ALU.mult, ALU.add)

    def lbcol(layer, h):
        return LBT[:, 8 + 4 * layer + h:9 + 4 * layer + h]

    def omlcol(layer, h):
        return LBT[:, 16 + 4 * layer + h:17 + 4 * layer + h]

    att.reset()
    posi = att.alloc([128, S], I32)
    ang = att.alloc([128, S], F32)
    kq = att.alloc([128, S], F32)
    kqi = att.alloc([128, S], I32)
    sc.dma("sp", posi, pos_in.partition_broadcast(128))
    sc.copy("dve", ang, posi)
    sc.ts("dve", ang, ang, CSTF[:, CST_INVF:CST_INVF + 1], CSTF[:, CST_PHASE:CST_PHASE + 1], ALU.mult, ALU.add)
    sc.ts("dve", kq, ang, float(1.0 / (2 * math.pi)), None, ALU.mult)
    sc.copy("dve", kqi, kq)
    sc.copy("dve", kq, kqi)
    C1 = 6.28125
    C2 = float(2 * math.pi - 6.28125)
    sc.stt(ang, kq, -C1, ang, ALU.mult, ALU.add)
    sc.stt(ang, kq, -C2, ang, ALU.mult, ALU.add)
    sc.ts("dve", kq, ang, float(math.pi), float(-2 * math.pi), ALU.is_gt, ALU.mult)
    sc.tt("dve", ang, ang, kq, ALU.add)
    sc.ts("dve", kq, ang, float(-math.pi), float(2 * math.pi), ALU.is_lt, ALU.mult)
    sc.tt("dve", ang, ang, kq, ALU.add)
    sc.ts("dve", ang, ang, float(math.pi), float(-math.pi), ALU.min, ALU.max)
    sc.act(CS[:, :], ang, AF.Sin)
    sc.ts("dve", CS[:, :], CS[:, :], CSTF[:, CST_SIGN:CST_SIGN + 1], None, ALU.mult)

    if debug:
        sc.dma("sp", dbg["cs"], CS[:, :], is_out=True)
        sc.dma("sp", dbg["lbt"], LBT[:, :], is_out=True)

    jobs = []

    def job(load, compute):
        jobs.append((load, compute))

    def wview(slot, shape):
        n = 1
        for s in shape:
            n *= s
        a = slot[:, 0:n]
        if len(shape) == 2:
            return a.rearrange("p (a b) -> p a b", a=shape[0])
        if len(shape) == 3:
            return a.rearrange("p (a b c) -> p a b c", a=shape[0], b=shape[1])
        return a

    def w_in_cols(layer, a, b):
        return w_in[layer].rearrange("(c p) n -> p c n", p=128)[:, :, a:b]

    def proj_f(ps, wfn, tg, nk=16, rhs_fn=None):
        for c in range(nk):
            rhs = HT[:, c, tg * 512:(tg + 1) * 512] if rhs_fn is None else rhs_fn(c)
            sc.mm(ps, wfn(c), rhs, start=(c == 0), stop=(c == nk - 1))

    def norm_stage(src, gcol_fn, dst_dram=None):
        att.reset()
        NB = 24
        XB = [att.alloc([128, 512], F32) for _ in range(NB)]
        SQ = [att.alloc([128, 512], F32) for _ in range(2)]
        RS = att.alloc([128, 512], F32)
        RINV = att.alloc([128, 512], F32)
        k = 0
        for tg in range(4):
            cols = slice(tg * 512, (tg + 1) * 512)
            st = PS[6]
            xs = []
            for c in range(16):
                xb = XB[k % NB]
                k += 1
                sc.dma("sp", xb, src[c * 128:(c + 1) * 128, cols])
                xs.append(xb)
            for c in range(16):
                sq = SQ[c % 2]
                sc.act(sq, xs[c], AF.Square)
                sc.mm(st[:, :], ones_f, sq, start=(c == 0), stop=(c == 15))
            sc.act(RS, st[:, :], AF.Sqrt, scale=1.0 / D, bias=EPS)
            sc.recip(RINV, RS)
            for c in range(16):
                xb = xs[c]
                if dst_dram is None:
                    sc.stt(HT[:, c, cols], xb, gcol_fn(c), RINV, ALU.mult, ALU.mult)
                else:
                    sc.stt(xb, xb, gcol_fn(c), RINV, ALU.mult, ALU.mult)
                    sc.dma("sp", dst_dram[c * 128:(c + 1) * 128, cols], xb, is_out=True)

    def group_norm_out(o_ap, width, gcol, dst, tmp_sq, tmp_rs, tmp_rinv, st_ps, extra_mul=None, tmp2=None):
        sc.act(tmp_sq[:, 0:width], o_ap, AF.Square)
        sc.mm(st_ps[:, 0:width], ones_f, tmp_sq[:, 0:width], start=True, stop=True)
        sc.act(tmp_rs[:, 0:width], st_ps[:, 0:width], AF.Sqrt, scale=1.0 / 128.0, bias=EPS)
        sc.recip(tmp_rinv[:, 0:width], tmp_rs[:, 0:width])
        if extra_mul is None:
            sc.stt(dst, o_ap, gcol, tmp_rinv[:, 0:width], ALU.mult, ALU.mult)
        else:
            sc.stt(tmp2[:, 0:width], o_ap, gcol, tmp_rinv[:, 0:width], ALU.mult, ALU.mult)
            sc.tt("dve", dst, tmp2[:, 0:width], extra_mul, ALU.mult)

    def hg_stage(layer):
        att.reset()
        T1 = att.alloc([128, S], F32)
        T2 = att.alloc([128, S], F32)
        T3 = att.alloc([128, S], F32)
        SM = att.alloc([128, S], F32)
        Qt = att.alloc([128, S], BF16)
        Kt = att.alloc([128, S], BF16)
        Kh = att.alloc([128, S], BF16)
        V = att.alloc([128, 16, 128], BF16)
        OUT = att.alloc([128, S], BF16)
        EGL = att.alloc([128, 64], F32)
        KM = [att.alloc([128, 4, 128], BF16) for _ in range(2)]
        SC_ = [att.alloc([128, 128], BF16) for _ in range(2)]
        ST = [att.alloc([128, 128], F32) for _ in range(4)]
        STB = [att.alloc([128, 128], BF16) for _ in range(4)]
        TRIB = [att.alloc([128, 128], F32) for _ in range(2)]
        VT = [att.alloc([128, 512], BF16) for _ in range(2)]
        SCF = [att.alloc([128, 128], F32) for _ in range(2)]
        OSB = [att.alloc([128, 128], F32) for _ in range(2)]
        TSQ = att.alloc([128, 128], F32)
        TRS = att.alloc([128, 128], F32)
        TRI = att.alloc([128, 128], F32)
        TSG = att.alloc([128, 128], F32)
        TO = att.alloc([128, 128], F32)
        for h in range(4):
            def load(slot, h=h):
                W = wview(slot, [16, 4, 128])
                for j in range(4):
                    a = j * 512 + h * 128
                    sc.dma("pool", W[:, :, j, :], w_in_cols(layer, a, a + 128))

            def compute(slot, h=h):
                W = wview(slot, [16, 4, 128])
                if h == 0:
                    sc.memset("pool", SM, 1.0)
                    sc.memset("pool", SM.rearrange("p (c t) -> p c t", t=32)[:, :, 0:1], 0.0)
                for tg in range(4):
                    cols = slice(tg * 512, (tg + 1) * 512)
                    ps = PS[tg % 2]
                    proj_f(ps[:, :], lambda c: W[:, c, 1, :], tg)
                    sc.act(T1[:, cols], ps[:, :], AF.Sigmoid)
                    sc.ts("dve", T1[:, cols], T1[:, cols], omlcol(layer, h), lbcol(layer, h), ALU.mult, ALU.add)
                    sc.ts("dve", T2[:, cols], T1[:, cols], -1.0, 1.0, ALU.mult, ALU.add)
                    sc.act(T1[:, cols], T1[:, cols], AF.Ln)
                sc.scan(T3, SM, T1, 0.0, ALU.mult, ALU.add)
                sc.act(T1, T3, AF.Exp)
                sc.copy("dve", EGL, T1.rearrange("p (c t) -> p c t", t=32)[:, :, 31])
                for tg in range(4):
                    cols = slice(tg * 512, (tg + 1) * 512)
                    ps = PS[tg % 2]
                    proj_f(ps[:, :], lambda c: W[:, c, 0, :], tg)
                    sc.tt("dve", Qt[:, cols], ps[:, :], T1[:, cols], ALU.mult)
                sc.act(T1, T3, AF.Exp, scale=-1.0)
                sc.tt("dve", Kt, T2, T1, ALU.mult)
                T3v = T3.rearrange("p (c t) -> p c t", t=32)
                T1v = T1.rearrange("p (c t) -> p c t", t=32)
                sc.tt("dve", T1v, T3v[:, :, 31:32].broadcast_to([128, 64, 32]), T3v, ALU.subtract)
                sc.act(T1, T1, AF.Exp)
                sc.tt("dve", Kh, T2, T1, ALU.mult)
                for tg in range(4):
                    ps = PS[tg % 2]
                    proj_f(ps[:, :], lambda c: W[:, c, 2, :], tg)
                    vt = VT[tg % 2]
                    sc.copy("dve", vt, ps[:, :])
                    for j in range(4):
                        sc.tr(PSB[:, j * 128:(j + 1) * 128], vt[:, j * 128:(j + 1) * 128], ident_b)
                    sc.copy("act", V[:, tg * 4:(tg + 1) * 4, :], PSB[:, 0:512].rearrange("p (a b) -> p a b", a=4))
                for tg in range(4):
                    cols = slice(tg * 512, (tg + 1) * 512)
                    ps = PS[tg % 2]
                    proj_f(ps[:, :], lambda c: W[:, c, 3, :], tg)
                    sc.act(T2[:, cols], ps[:, :], AF.Silu)
                sc.memset("pool", ST[0], 0.0)
                sc.memset("pool", STB[0], 0.0)
                gcol = vcol(layer, V_HGG, h)

                def stF(tt_):
                    tcols = slice(tt_ * 128, (tt_ + 1) * 128)
                    b = tt_ % 2
                    trp = PSB[:, b * 128:b * 128 + 128]
                    sc.tr(trp, Kh[:, tcols], ident_b)
                    for c4 in range(4):
                        cm = CSTF[:, CST_CMASK + c4:CST_CMASK + c4 + 1]
                        sc.act(KM[b][:, c4, :], trp, AF.Copy, scale=cm)
                    sc.mm(PS[2][:, 0:128], Kt[:, tcols], Qt[:, tcols])
                    sc.copy("act", SCF[b], PS[2][:, 0:128])
                    sc.tt("pool", SC_[b], SCF[b], hgmask_f, ALU.mult)
                    kvb = PS[5 + b]
                    for c4 in range(4):
                        sc.mm(kvb[:, c4 * 128:(c4 + 1) * 128], KM[b][:, c4, :], V[:, tt_, :])

                def stR(tt_):
                    b = tt_ % 2
                    op_ = PS[3 + b]
                    kvb = PS[5 + b]
                    sc.mm(op_[:, 0:128], V[:, tt_, :], SC_[b], start=True, stop=False)
                    for c4 in range(4):
                        ch = tt_ * 4 + c4
                        sc.mm(op_[:, c4 * 32:(c4 + 1) * 32], STB[ch % 4], Qt[:, ch * 32:(ch + 1) * 32],
                              start=False, stop=(c4 == 3))
                        sc.stt(ST[(ch + 1) % 4], ST[ch % 4], EGL[:, ch:ch + 1], kvb[:, c4 * 128:(c4 + 1) * 128],
                               ALU.mult, ALU.add)
                        sc.copy("pool", STB[(ch + 1) % 4], ST[(ch + 1) % 4])

                def stN1(tt_):
                    b = tt_ % 2
                    op_ = PS[3 + b]
                    sc.act(TSQ, op_[:, 0:128], AF.Square)
                    sc.copy("act", OSB[b], op_[:, 0:128])
                    sc.mm(PS[2][:, 128:256], ones_f, TSQ)
                    sc.act(TRS, PS[2][:, 128:256], AF.Ln, scale=1.0 / 128.0, bias=EPS)
                    sc.act(TRIB[b], TRS, AF.Exp, scale=-0.5)

                def stN2(tt_):
                    tcols = slice(tt_ * 128, (tt_ + 1) * 128)
                    b = tt_ % 2
                    sc.tt("pool", TO, OSB[b], TRIB[b], ALU.mult)
                    sc.tt("pool", TSG, TO, T2[:, tcols], ALU.mult)
                    sc.ts("pool", OUT[:, tcols], TSG, gcol, 1.0, ALU.mult, ALU.mult)

                stF(0)
                stF(1)
                for tt_ in range(16):
                    stR(tt_)
                    if tt_ + 2 < 16:
                        stF(tt_ + 2)
                    if tt_ >= 2:
                        stN2(tt_ - 2)
                    if tt_ >= 1:
                        stN1(tt_ - 1)
                stN2(14)
                stN1(15)
                stN2(15)
                sc.dma("sp", mixd[h * 128:(h + 1) * 128, :], OUT)

            job(load, compute)

    def sb_stage(layer):
        st = {}
        scale = 128.0 ** -0.5
        PSTAT = PSB[:, :].bitcast(F32)

        def alloc_all():
            att.reset()
            st["QT"] = [att.alloc([128, S], BF16) for _ in range(2)]
            st["KT"] = [att.alloc([128, S], BF16) for _ in range(2)]
            st["V"] = [att.alloc([128, 16, 128], BF16) for _ in range(2)]
            st["OUT"] = [att.alloc([128, S], BF16) for _ in range(2)]
            st["E"] = [att.alloc([128, 512], F32) for _ in range(2)]
            st["SPb"] = [att.alloc([128, 512], BF16) for _ in range(3)]
            st["LL"] = [[att.alloc([128, 512], BF16) for _ in range(3)] for _ in range(2)]
            st["A"] = [att.alloc([128, 512], BF16) for _ in range(2)]
            st["AD"] = [None] + [att.alloc([128, 512], BF16) for _ in range(3)]
            st["TSQ"] = att.alloc([128, 512], F32)
            st["TRS"] = att.alloc([128, 512], F32)
            st["TRI"] = att.alloc([128, 512], F32)
            st["VT"] = [att.alloc([128, 512], BF16) for _ in range(2)]
            for i in range(1, 4):
                sc.memset("pool", st["AD"][i][:, 0:i * 128], 0.0)

        def proj_units(slot, hs):
            W = wview(slot, [16, 3, 128])
            QT, KT, V = st["QT"][hs], st["KT"][hs], st["V"][hs]
            units = []
            for tg in range(4):
                cols = slice(tg * 512, (tg + 1) * 512)

                def uq(tg=tg, cols=cols):
                    proj_f(PS[1][:, :], lambda c: W[:, c, 0, :], tg)
                    sc.ts("dve", QT[:, cols], PS[1][:, :], scale, None, ALU.mult)

                def uk(tg=tg, cols=cols):
                    proj_f(PS[1][:, :], lambda c: W[:, c, 1, :], tg)
                    sc.copy("dve", KT[:, cols], PS[1][:, :])

                units += [uq, uk]
            for tg in range(4):
                def uv(tg=tg):
                    proj_f(PS[0][:, :], lambda c: W[:, c, 2, :], tg)
                    vt = st["VT"][tg % 2]
                    sc.copy("dve", vt, PS[0][:, :])
                    for j in range(4):
                        sc.tr(PSB[:, j * 128:(j + 1) * 128], vt[:, j * 128:(j + 1) * 128], ident_b)
                    sc.copy("dve", V[:, tg * 4:(tg + 1) * 4, :], PSB[:, 0:512].rearrange("p (a b) -> p a b", a=4))
                units.append(uv)
            return units

        def attn(h, units):
            hs = h % 2
            QT, KT, V, OUT = st["QT"][hs], st["KT"][hs], st["V"][hs], st["OUT"][hs]
            E, SPb, LL, A, AD = st["E"], st["SPb"], st["LL"], st["A"], st["AD"]
            pairs = []
            for qg in range(4):
                nkb = 4 * qg + 4
                for n, kb in enumerate(range(nkb - 1, -1, -1)):
                    i = kb - 4 * qg
                    pairs.append(dict(qg=qg, n=n, kb=kb, i=i, c0=(i * 128 if i > 0 else 0), diag=(i >= 0),
                                      nkb=nkb, idx=len(pairs)))
            ot = PS[6]

            def stA(p):
                idx, c0, qg, n, kb = p["idx"], p["c0"], p["qg"], p["n"], p["kb"]
                if n == 0:
                    for b_ in LL[qg % 2]:
                        sc.memset("pool", b_, 0.0)
                zp = PS[2 + idx % 2]
                e_ = E[idx % 2]
                sp = SPb[idx % 3]
                Lcur = LL[qg % 2][n % 3]
                Lnext = LL[qg % 2][(n + 1) % 3]
                q0 = qg * 512 + c0
                q1 = (qg + 1) * 512
                kcols = slice(kb * 128, (kb + 1) * 128)
                sc.mm(zp[:, c0:512], KT[:, kcols], QT[:, q0:q1])
                sc.act(e_[:, c0:512], zp[:, c0:512], AF.Exp)
                sc.act(sp[:, c0:512], e_[:, c0:512], AF.Ln, bias=1.0)
                if p["diag"]:
                    sc.tt("dve", sp[:, c0:c0 + 128], sp[:, c0:c0 + 128], strict_b, ALU.mult)
                if n + 1 < p["nkb"]:
                    sc.tt("pool", Lnext[:, c0:512], Lcur[:, c0:512], sp[:, c0:512], ALU.add)

            def stB(p):
                idx, c0, qg, n, kb, i = p["idx"], p["c0"], p["qg"], p["n"], p["kb"], p["i"]
                cp = PS[4 + idx % 2]
                sp = SPb[idx % 3]
                Lcur = LL[qg % 2][n % 3]
                a_ = AD[i] if i > 0 else A[idx % 2]
                q0 = qg * 512 + c0
                q1 = (qg + 1) * 512
                kcols = slice(kb * 128, (kb + 1) * 128)
                sc.mm(cp[:, c0:512], negtri_b, sp[:, c0:512], start=True, stop=False)
                if n > 0:
                    sc.mm(cp[:, c0:512], negones_b, Lcur[:, c0:512], start=False, stop=False)
                sc.mm(cp[:, c0:512], KT[:, kcols], QT[:, q0:q1], start=False, stop=True)
                sc.act(a_[:, c0:512], cp[:, c0:512], AF.Exp)
                if p["diag"]:
                    sc.tt("dve", a_[:, c0:c0 + 128], a_[:, c0:c0 + 128], strict_b, ALU.mult)

            def stC(p):
                idx, qg, n, kb, i = p["idx"], p["qg"], p["n"], p["kb"], p["i"]
                a_ = AD[i] if i > 0 else A[idx % 2]
                sc.mm(ot[:, :], V[:, kb, :], a_[:, :], start=(n == 0), stop=(n == p["nkb"] - 1))
                if n == p["nkb"] - 1:
                    group_norm_out(ot[:, :], 512, vcol(layer, V_SBG, h), OUT[:, qg * 512:(qg + 1) * 512],
                                   st["TSQ"], st["TRS"], st["TRI"], PSTAT)

            units = list(units)
            NP_ = len(pairs)
            stA(pairs[0])
            stA(pairs[1])
            stB(pairs[0])
            for k_ in range(NP_):
                if k_ + 2 < NP_:
                    stA(pairs[k_ + 2])
                if k_ + 1 < NP_:
                    stB(pairs[k_ + 1])
                stC(pairs[k_])
                if units:
                    units.pop(0)()
            while units:
                units.pop(0)()
            sc.dma("sp", mixd[(4 + h) * 128:(5 + h) * 128, :], OUT)

        def wload(hh):
            def load(slot):
                W = wview(slot, [16, 3, 128])
                for j in range(3):
                    a = 2048 + j * 768 + hh * 128
                    sc.dma("pool", W[:, :, j, :], w_in_cols(layer, a, a + 128))
            return load

        def pre(slot):
            alloc_all()
            for u in proj_units(slot, 0):
                u()

        job(wload(0), pre)
        for h in range(6):
            if h < 5:
                job(wload(h + 1), lambda slot, h=h: attn(h, proj_units(slot, (h + 1) % 2)))
            else:
                job(None, lambda slot, h=h: attn(h, []))

    def mla_stage(layer):
        att.reset()
        CQN = att.alloc([128, 4, S], BF16)
        KVN = att.alloc([128, 2, S], BF16)
        KR2 = att.alloc([128, S], BF16)
        mla_base = att.off
        uq_v = w_uq[layer].rearrange("(c p) (h e) -> p c h e", p=128, e=192)
        ukv_v = w_ukv[layer].rearrange("(c p) (h e) -> p c h e", p=128, e=256)

        def load_q(slot):
            sc.dma("pool", wview(slot, [16, 512]), w_in_cols(layer, 4352, 4864))

        import os as _os

        def comp_q(slot):
            att.off = mla_base
            W = wview(slot, [16, 512])
            CQ = att.alloc([128, 4, 512], F32)
            TSQ = att.alloc([128, 512], F32)
            TRS = att.alloc([128, 512], F32)
            TRI = att.alloc([128, 512], F32)
            CUT = int(_os.environ.get("MLA_CUT", "9"))
            for tg in range(4):
                cols = slice(tg * 512, (tg + 1) * 512)
                for j in range(4):
                    ps = PS[j % 2]
                    proj_f(ps[:, :], lambda c: W[:, c, j * 128:(j + 1) * 128], tg)
                    sc.copy("dve", CQ[:, j, :], ps[:, :])
                    if CUT >= 2:
                        sc.act(TSQ, CQ[:, j, :], AF.Square)
                        sc.mm(PS[6][:, :], ones_f, TSQ, start=(j == 0), stop=(j == 3))
                if CUT >= 3:
                    sc.act(TRS, PS[6][:, :], AF.Sqrt, scale=1.0 / 512.0, bias=EPS)
                    sc.recip(TRI, TRS)
                if CUT >= 4:
                    for j in range(4):
                        sc.stt(CQN[:, j, cols], CQ[:, j, :], vcol(layer, V_MLAQG, j), TRI, ALU.mult, ALU.mult)

        import os as _os
        if 'q' in _os.environ.get('MLA_PRE', 'qk'):
            job(load_q, comp_q)

        def load_kv(slot):
            W = wview(slot, [16, 384])
            sc.dma("pool", W[:, :, 0:256], w_in_cols(layer, 4864, 5120))
            sc.dma("pool", W[:, :, 256:320], w_in_cols(layer, 5120, 5184))
            sc.dma("pool", W[:, :, 320:352], w_in_cols(layer, 5152, 5184))
            sc.dma("pool", W[:, :, 352:384], w_in_cols(layer, 5120, 5152))

        def comp_kv(slot):
            att.off = mla_base
            W = wview(slot, [16, 384])
            CK = att.alloc([128, 2, 512], F32)
            TSQ = att.alloc([128, 512], F32)
            TRS = att.alloc([128, 512], F32)
            TRI = att.alloc([128, 512], F32)
            KRT = att.alloc([128, 512], F32)
            for tg in range(4):
                cols = slice(tg * 512, (tg + 1) * 512)
                for j in range(2):
                    ps = PS[j % 2]
                    proj_f(ps[:, :], lambda c: W[:, c, j * 128:(j + 1) * 128], tg)
                    sc.copy("dve", CK[:, j, :], ps[:, :])
                    sc.act(TSQ, CK[:, j, :], AF.Square)
                    sc.mm(PS[6][:, :], ones_f, TSQ, start=(j == 0), stop=(j == 1))
                sc.act(TRS, PS[6][:, :], AF.Sqrt, scale=1.0 / 256.0, bias=EPS)
                sc.recip(TRI, TRS)
                for j in range(2):
                    sc.stt(KVN[:, j, cols], CK[:, j, :], vcol(layer, V_MLAKVG, j), TRI, ALU.mult, ALU.mult)
                ps = PS[2]
                proj_f(ps[:, :], lambda c: W[:, c, 256:384], tg)
                sc.tt("dve", KRT, ps[:, :], CS[:, cols], ALU.mult)
                sc.mm(PS[3][:, :], fold_f, KRT)
                sc.copy("act", KR2[:, cols], PS[3][:, :])

        if 'k' in _os.environ.get('MLA_PRE', 'qk'):
            job(load_kv, comp_kv)

        for h in range(int(_os.environ.get("MLA_HEADS", "6"))):
            def load(slot, h=h):
                Wq = wview(slot, [4, 256])
                Wk = slot[:, 1024:1024 + 512].rearrange("p (a b) -> p a b", a=2)
                sc.dma("pool", Wq[:, :, 0:128], uq_v[:, :, h, 0:128])
                sc.dma("pool", Wq[:, :, 128:192], uq_v[:, :, h, 128:192])
                sc.dma("pool", Wq[:, :, 192:224], uq_v[:, :, h, 160:192])
                sc.dma("pool", Wq[:, :, 224:256], uq_v[:, :, h, 128:160])
                sc.dma("pool", Wk, ukv_v[:, :, h, :])

            def compute(slot, h=h):
                att.off = mla_base
                Wq = wview(slot, [4, 256])
                Wk = slot[:, 1024:1024 + 512].rearrange("p (a b) -> p a b", a=2)
                QN = att.alloc([128, S], BF16)
                QR = att.alloc([128, S], BF16)
                KN = att.alloc([128, S], BF16)
                V = att.alloc([128, 16, 128], BF16)
                OUT = att.alloc([128, S], BF16)
                A = [att.alloc([128, 512], BF16) for _ in range(3)]
                AD = [None] + [att.alloc([128, 512], BF16) for _ in range(3)]
                TO = att.alloc([128, 512], F32)
                TSQ = att.alloc([128, 512], F32)
                TRS = att.alloc([128, 512], F32)
                TRI = att.alloc([128, 512], F32)
                for i in range(1, 4):
                    sc.memset("pool", AD[i][:, 0:i * 128], 0.0)
                for tg in range(4):
                    cols = slice(tg * 512, (tg + 1) * 512)
                    ps = PS[0]
                    proj_f(ps[:, :], lambda c: Wq[:, c, 0:128], tg, nk=4, rhs_fn=lambda c: CQN[:, c, cols])
                    sc.copy("act", QN[:, cols], ps[:, :])
                    ps = PS[1]
                    proj_f(ps[:, :], lambda c: Wq[:, c, 128:256], tg, nk=4, rhs_fn=lambda c: CQN[:, c, cols])
                    sc.tt("dve", QR[:, cols], ps[:, :], CS[:, cols], ALU.mult)
                    ps = PS[2]
                    proj_f(ps[:, :], lambda c: Wk[:, c, 0:128], tg, nk=2, rhs_fn=lambda c: KVN[:, c, cols])
                    sc.copy("act", KN[:, cols], ps[:, :])
                for tq in range(4):
                    ps = PS[tq % 2]
                    for j in range(4):
                        tt_ = tq * 4 + j
                        for c in range(2):
                            sc.mm(ps[:, j * 128:(j + 1) * 128], KVN[:, c, tt_ * 128:(tt_ + 1) * 128], Wk[:, c, 128:256],
                                  start=(c == 0), stop=(c == 1))
                    sc.copy("dve", V[:, tq * 4:(tq + 1) * 4, :], ps[:, :].rearrange("p (a b) -> p a b", a=4))
                scale = 192.0 ** -0.5
                pairs = []
                for qg in range(4):
                    nkb = 4 * qg + 4
                    for kb in range(nkb):
                        i = kb - 4 * qg
                        pairs.append(dict(qg=qg, kb=kb, i=i, c0=(i * 128 if i > 0 else 0), diag=(i >= 0),
                                          nkb=nkb, idx=len(pairs)))
                ot = PS[5]
                den = PS[6]

                def abuf(p):
                    return AD[p["i"]] if p["i"] > 0 else A[p["idx"] % 3]

                SPBANK = [PS[1], PS[2], PS[3]]

                def stA(p):
                    idx, c0, qg, kb = p["idx"], p["c0"], p["qg"], p["kb"]
                    sp_ = SPBANK[idx % 3]
                    a_ = abuf(p)
                    q0 = qg * 512 + c0
                    q1 = (qg + 1) * 512
                    kcols = slice(kb * 128, (kb + 1) * 128)
                    sc.mm(sp_[:, c0:512], KN[:, kcols], QN[:, q0:q1], start=True, stop=False)
                    sc.mm(sp_[:, c0:512], KR2[:, kcols], QR[:, q0:q1], start=False, stop=True)
                    sc.act(a_[:, c0:512], sp_[:, c0:512], AF.Exp, scale=scale)
                    if p["diag"]:
                        sc.memset("pool", a_[64:128, c0:c0 + 64], 0.0)

                def stB(p):
                    qg, kb = p["qg"], p["kb"]
                    a_ = abuf(p)
                    sc.mm(ot[:, :], V[:, kb, :], a_[:, :], start=(kb == 0), stop=(kb == p["nkb"] - 1))
                    sc.mm(den[:, :], ones_b, a_[:, :], start=(kb == 0), stop=(kb == p["nkb"] - 1))
                    if kb == p["nkb"] - 1:
                        sc.recip(TRI, den[:, :])
                        sc.tt("dve", TO, ot[:, :], TRI, ALU.mult)
                        group_norm_out(TO, 512, vcol(layer, V_MLAOG, h), OUT[:, qg * 512:(qg + 1) * 512],
                                       TSQ, TRS, TRI, PS[4])

                NP_ = len(pairs)
                stA(pairs[0])
                stA(pairs[1])
                for k_ in range(NP_):
                    if k_ + 2 < NP_:
                        stA(pairs[k_ + 2])
                    stB(pairs[k_])
                sc.dma("sp", mixd[(10 + h) * 128:(11 + h) * 128, :], OUT)

            job(load, compute)

    def wo_stage(layer, src):
        state = {}
        mixv = mixd.rearrange("(c p) n -> p c n", p=128)

        for dcg in range(4):
            def load(slot, dcg=dcg):
                sc.dma("pool", wview(slot, [16, 512]),
                       w_o[layer].rearrange("(c p) n -> p c n", p=128)[:, :, dcg * 512:(dcg + 1) * 512])

            def compute(slot, dcg=dcg):
                if dcg == 0:
                    att.reset()
                    state["MX"] = [att.alloc([128, 16, 512], BF16) for _ in range(2)]
                    state["XB"] = [att.alloc([128, 512], F32) for _ in range(8)]
                    state["k"] = 0
                    sc.dma("sp", state["MX"][0], mixv[:, :, 0:512])
                MX = state["MX"]
                XB = state["XB"]
                W = wview(slot, [16, 512])
                for tg in range(4):
                    cols = slice(tg * 512, (tg + 1) * 512)
                    n = dcg * 4 + tg
                    mx = MX[n % 2]
                    xbs = []
                    for j in range(4):
                        dc = dcg * 4 + j
                        xb = XB[state["k"] % 8]
                        state["k"] += 1
                        sc.dma("sp", xb, src[dc * 128:(dc + 1) * 128, cols])
                        xbs.append(xb)
                    if n + 1 < 16:
                        tg2 = (tg + 1) % 4
                        sc.dma("sp", MX[(n + 1) % 2], mixv[:, :, tg2 * 512:(tg2 + 1) * 512])
                    for j in range(4):
                        dc = dcg * 4 + j
                        ps = PS[j % 4]
                        for mc in range(16):
                            sc.mm(ps[:, :], W[:, mc, j * 128:(j + 1) * 128], mx[:, mc, :], start=(mc == 0), stop=(mc == 15))
                        sc.tt("dve", xbs[j], ps[:, :], xbs[j], ALU.add)
                        sc.dma("sp", xres[dc * 128:(dc + 1) * 128, cols], xbs[j])

            job(load, compute)

    def ffn_stage(layer):
        state = {}
        wg = w_gate[layer].rearrange("(c p) n -> p c n", p=128)
        wu = w_up[layer].rearrange("(c p) n -> p c n", p=128)
        wd = w_down[layer].rearrange("(c p) n -> p c n", p=128)
        for tt_ in range(4):
            cols = slice(tt_ * 512, (tt_ + 1) * 512)
            for fcp in range(22):
                def load(slot, fcp=fcp):
                    W = wview(slot, [16, 2, 256])
                    sc.dma("pool", W[:, :, 0, :], wg[:, :, fcp * 256:(fcp + 1) * 256])
                    sc.dma("pool", W[:, :, 1, :], wu[:, :, fcp * 256:(fcp + 1) * 256])

                def compute(slot, fcp=fcp, tt_=tt_, cols=cols):
                    if fcp == 0 and tt_ == 0:
                        att.reset()
                        state["ACT"] = att.alloc([128, NFC, 512], BF16)
                        state["SG"] = [att.alloc([128, 512], F32) for _ in range(2)]
                        state["XB"] = [att.alloc([128, 512], F32) for _ in range(8)]
                        state["k"] = 0
                    W = wview(slot, [16, 2, 256])
                    for j in range(2):
                        fc = fcp * 2 + j
                        pg = PS[j * 2]
                        pu = PS[j * 2 + 1]
                        proj_f(pg[:, :], lambda c: W[:, c, 0, j * 128:(j + 1) * 128], tt_)
                        proj_f(pu[:, :], lambda c: W[:, c, 1, j * 128:(j + 1) * 128], tt_)
                        sg = state["SG"][fc % 2]
                        sc.act(sg, pg[:, :], AF.Silu)
                        sc.tt("dve", state["ACT"][:, fc, :], pu[:, :], sg, ALU.mult)

                job(load, compute)
            for dcp in range(8):
                for half in range(2):
                    def load(slot, dcp=dcp, half=half):
                        sc.dma("pool", wview(slot, [22, 256]), wd[:, half * 22:(half + 1) * 22, dcp * 256:(dcp + 1) * 256])

                    def compute(slot, dcp=dcp, half=half, cols=cols):
                        W = wview(slot, [22, 256])
                        if half == 0:
                            state["xbs"] = []
                            for j in range(2):
                                dc = dcp * 2 + j
                                xb = state["XB"][state["k"] % 8]
                                state["k"] += 1
                                sc.dma("sp", xb, xres[dc * 128:(dc + 1) * 128, cols])
                                state["xbs"].append(xb)
                        for j in range(2):
                            ps = PS[4 + j]
                            for f in range(22):
                                fc = half * 22 + f
                                sc.mm(ps[:, :], W[:, f, j * 128:(j + 1) * 128], state["ACT"][:, fc, :],
                                      start=(fc == 0), stop=(fc == NFC - 1))
                        if half == 1:
                            for j in range(2):
                                dc = dcp * 2 + j
                                ps = PS[4 + j]
                                xb = state["xbs"][j]
                                sc.tt("dve", xb, ps[:, :], xb, ALU.add)
                                sc.dma("sp", xres[dc * 128:(dc + 1) * 128, cols], xb)

                    job(load, compute)

    def plain(fn):
        job(None, lambda slot: fn())

    stages = stages or ("norm", "hg", "sb", "mla", "wo", "ffn", "final")
    for layer in range(nlayers):
        src = xT_in if layer == 0 else xres
        if "norm" in stages:
            plain(lambda layer=layer, src=src: norm_stage(src, lambda c: vcol(layer, V_ATTN, c)))
            if debug and layer == 0 and "hT" in dbg:
                plain(lambda: sc.dma("sp", dbg["hT"].rearrange("(c p) n -> p c n", p=128), HT[:, :, :], is_out=True))
        if "hg" in stages:
            hg_stage(layer)
        if "sb" in stages:
            sb_stage(layer)
        if "mla" in stages:
            mla_stage(layer)
        if debug and layer == 0 and "mix" in dbg:
            def dump_mix():
                att.reset()
                t = att.alloc([128, 16, 512], BF16)
                for tg in range(4):
                    sc.dma("sp", t, mixd.rearrange("(c p) n -> p c n", p=128)[:, :, tg * 512:(tg + 1) * 512])
                    sc.dma("sp", dbg["mix"].rearrange("(c p) n -> p c n", p=128)[:, :, tg * 512:(tg + 1) * 512], t, is_out=True)
            plain(dump_mix)
        if "wo" in stages:
            wo_stage(layer, src)

        def dump_x(key):
            att.reset()
            t = att.alloc([128, 2048], F32)
            for c in range(16):
                sc.dma("sp", t, xres[c * 128:(c + 1) * 128, :])
                sc.dma("sp", dbg[key][c * 128:(c + 1) * 128, :], t, is_out=True)
        if debug and layer == 0 and "x1" in dbg:
            plain(lambda: dump_x("x1"))
        if "ffn" in stages:
            plain(lambda layer=layer: norm_stage(xres, lambda c: vcol(layer, V_FFN, c)))
            ffn_stage(layer)
        if debug and layer == 0 and "x2" in dbg:
            plain(lambda: dump_x("x2"))
    if "final" in stages:
        plain(lambda: norm_stage(xres, lambda c: VEC[:, V_FINAL + c:V_FINAL + c + 1], dst_dram=outT))

    wjobs = [k for k, (ld, _) in enumerate(jobs) if ld is not None]
    slot_of = {k: WS[n % NSLOT][:, :] for n, k in enumerate(wjobs)}
    issued = [0]

    def issue_upto(n):
        while issued[0] < min(n, len(wjobs)):
            k = wjobs[issued[0]]
            jobs[k][0](slot_of[k])
            issued[0] += 1

    nw = 0
    for k, (ld, comp) in enumerate(jobs):
        if ld is not None:
            issue_upto(nw + NSLOT)
            nw += 1
        else:
            issue_upto(nw + NSLOT - 1)
        comp(slot_of.get(k))

    sc.emit_all()
    nc._in_names = {"xT", "pos", "vec", "cst"} | ({"w_in"} if w_in is not None else set()) | \
        ({"mla_w_uq", "mla_w_ukv"} if w_uq is not None else set()) | ({"w_o"} if w_o is not None else set()) | \
        ({"w_gate", "w_up", "w_down"} if w_gate is not None else set())
    nc._nops = len(sc.ops)
    return nc


_PROG_CACHE = {}


def _get_prog(debug=False, nlayers=DEPTH, stages=None):
    key = (debug, nlayers, stages)
    if key not in _PROG_CACHE:
        _PROG_CACHE[key] = build_program(debug=debug, nlayers=nlayers, stages=stages)
    return _PROG_CACHE[key]


def _pack_vec(inp):
    v = np.zeros((128, V_N), np.float32)

    def cols(a):
        a = np.asarray(a, np.float32)
        return a.reshape(-1, 128).T

    for l in range(DEPTH):
        b = l * VPL
        v[:, b + V_ATTN:b + V_ATTN + 16] = cols(inp["attn_norm_g"][l])
        v[:, b + V_FFN:b + V_FFN + 16] = cols(inp["ffn_norm_g"][l])
        v[:, b + V_HGG:b + V_HGG + 4] = cols(inp["hg_norm_g"][l])
        v[:, b + V_SBG:b + V_SBG + 6] = cols(inp["sb_norm_g"][l])
        v[:, b + V_MLAOG:b + V_MLAOG + 6] = cols(inp["mla_out_norm_g"][l])
        v[:, b + V_MLAQG:b + V_MLAQG + 4] = cols(inp["mla_q_norm_g"][l])
        v[:, b + V_MLAKVG:b + V_MLAKVG + 2] = cols(inp["mla_kv_norm_g"][l])
        v[:, b + V_LB:b + V_LB + 4] = cols(inp["hg_lower_bounds"][l])
    v[:, V_FINAL:V_FINAL + 16] = cols(inp["final_norm_g"])
    return v


def _in_maps(inp, names=None, ncores=NCORES):
    x = np.asarray(inp["x"], np.float32)
    pos = np.asarray(inp["positions"], np.int32)
    vec = _pack_vec(inp)
    cst = _make_cst()
    shared = {
        "vec": vec, "cst": cst,
        "w_in": np.ascontiguousarray(inp["w_in"], dtype=np.float32),
        "mla_w_uq": np.ascontiguousarray(inp["mla_w_uq"], dtype=np.float32),
        "mla_w_ukv": np.ascontiguousarray(inp["mla_w_ukv"], dtype=np.float32),
        "w_o": np.ascontiguousarray(inp["w_o"], dtype=np.float32),
        "w_gate": np.ascontiguousarray(inp["w_gate"], dtype=np.float32),
        "w_up": np.ascontiguousarray(inp["w_up"], dtype=np.float32),
        "w_down": np.ascontiguousarray(inp["w_down"], dtype=np.float32),
    }
    if names is not None:
        shared = {k: v for k, v in shared.items() if k in names}
    maps = []
    for b in range(ncores):
        m = dict(shared)
        m["xT"] = np.ascontiguousarray(x[b].T)
        m["pos"] = np.ascontiguousarray(pos[b:b + 1])
        maps.append(m)
    return maps


def kernel(**inputs):
    nc = _get_prog()
    res = run_bass_kernel_spmd(nc, _in_maps(inputs), core_ids=list(range(NCORES)))
    out = np.stack([np.ascontiguousarray(np.asarray(r["outT"], np.float32).T) for r in res.results], axis=0)
    return out
```

```python
import math
import numpy as np
import concourse.bass as bass
import concourse.mybir as mybir
from concourse.bass_utils import run_bass_kernel_spmd

F32 = mybir.dt.float32
BF16 = mybir.dt.bfloat16
I32 = mybir.dt.int32
AF = mybir.ActivationFunctionType
ALU = mybir.AluOpType

S = 2048
D = 2048
DEPTH = 2
NCORES = 8
EPS = 1e-6
IN_COLS = 5184
D_FF = 5632
NFC = D_FF // 128
P = 128

_ESZ = {str(F32): 4, str(BF16): 2, str(I32): 4}


def _esz(dt):
    return _ESZ[str(dt)]


def _box(ap):
    t = ap.tensor
    dims = [(int(s), int(c)) for s, c in ap.ap]
    es = _esz(ap.dtype)
    off = int(ap.offset)
    if type(t).__name__.startswith("DRam"):
        ext = sum((c - 1) * abs(s) for s, c in dims)
        return (t.name, 0, 1, off * es, (off + ext + 1) * es)
    pstep, pc = dims[0]
    if pstep == 0:
        pstep = 1 << 40
    p0 = off // pstep
    f0 = off % pstep
    ext = sum((c - 1) * abs(s) for s, c in dims[1:])
    return (t.name, p0, p0 + pc, f0 * es, (f0 + ext + 1) * es)


class Sched:
    ENG = ("pe", "dve", "act", "pool", "sp")

    def __init__(self, nc):
        self.nc = nc
        self.ops = []
        self.recs = {}
        self.psum_last = {}
        self.out_dmas = []

    def _access(self, box, idx, w, eng, dma, deps):
        name, p0, p1, f0, f1 = box
        recs = self.recs.get(name)
        if recs is None:
            recs = self.recs[name] = []
        new = []
        for r in recs:
            rp0, rp1, rf0, rf1, ridx, rw, reng, rdma = r
            ov = rp0 < p1 and p0 < rp1 and rf0 < f1 and f0 < rf1
            if ov and (w or rw) and ridx != idx:
                deps.add(ridx)
            if ridx == idx:
                new.append(r)
                continue
            if w and rp0 >= p0 and rp1 <= p1 and rf0 >= f0 and rf1 <= f1:
                continue
            if (not w) and (not rw) and reng == eng and (not dma) and (not rdma) \
                    and rp0 == p0 and rp1 == p1 and rf0 == f0 and rf1 == f1:
                continue
            new.append(r)
        new.append((p0, p1, f0, f1, idx, w, eng, dma))
        self.recs[name] = new

    def _psum_access(self, name, idx, w, eng, deps):
        r = self.psum_last.get(name)
        if r is not None and r[0] != idx:
            ridx, rw, reng = r
            if not (reng == eng and not w and not rw):
                deps.add(ridx)
        if r is not None and r[0] == idx:
            self.psum_last[name] = (idx, w or r[1], eng)
        else:
            self.psum_last[name] = (idx, w, eng)

    def add(self, eng, emit, reads, writes, dma=False, is_out=False):
        idx = len(self.ops)
        deps = set()
        for ap in reads:
            if type(ap.tensor).__name__.startswith("PSum"):
                self._psum_access(ap.tensor.name, idx, False, eng, deps)
            else:
                self._access(_box(ap), idx, False, eng, dma, deps)
        for ap in writes:
            if type(ap.tensor).__name__.startswith("PSum"):
                self._psum_access(ap.tensor.name, idx, True, eng, deps)
            else:
                self._access(_box(ap), idx, True, eng, dma, deps)
        op = dict(idx=idx, eng=eng, emit=emit, deps=deps, dma=dma, signal=False, sem=None, val=0)
        self.ops.append(op)
        if is_out:
            self.out_dmas.append(idx)
        return idx

    def mm(self, out, lhsT, rhs, start=True, stop=True):
        self.add("pe", lambda e: e.matmul(out, lhsT, rhs, start=start, stop=stop), [lhsT, rhs], [out])

    def tr(self, out, in_, ident):
        self.add("pe", lambda e: e.transpose(out, in_, ident), [in_, ident], [out])

    def act(self, out, in_, func, scale=None, bias=None):
        kw = {}
        rd = [in_]
        if scale is not None:
            kw["scale"] = scale
            if not isinstance(scale, (int, float)):
                rd.append(scale)
        if bias is not None:
            kw["bias"] = bias
            if not isinstance(bias, (int, float)):
                rd.append(bias)
        self.add("act", lambda e: e.activation(out, in_, func, **kw), rd, [out])

    def _veng(self, eng):
        return eng

    def tt(self, eng, out, in0, in1, op):
        self.add(eng, lambda e: e.tensor_tensor(out, in0, in1, op), [in0, in1], [out])

    def ts(self, eng, out, in0, s1, s2, op0, op1=None):
        rd = [in0]
        if not isinstance(s1, (int, float)):
            rd.append(s1)
        if s2 is not None and not isinstance(s2, (int, float)):
            rd.append(s2)
        if op1 is None:
            self.add(eng, lambda e: e.tensor_scalar(out, in0, s1, None, op0), rd, [out])
        else:
            self.add(eng, lambda e: e.tensor_scalar(out, in0, s1, s2, op0, op1), rd, [out])

    def stt(self, out, in0, scalar, in1, op0, op1):
        rd = [in0, in1]
        if not isinstance(scalar, (int, float)):
            rd.append(scalar)
        self.add("dve", lambda e: e.scalar_tensor_tensor(out, in0, scalar, in1, op0, op1), rd, [out])

    def scan(self, out, d0, d1, initial, op0, op1):
        self.add("dve", lambda e: e.tensor_tensor_scan(out, d0, d1, initial, op0, op1), [d0, d1], [out])

    def copy(self, eng, out, in_):
        if eng == "act":
            self.add("act", lambda e: e.copy(out, in_), [in_], [out])
        else:
            self.add(eng, lambda e: e.tensor_copy(out, in_), [in_], [out])

    def memset(self, eng, ap, val):
        self.add(eng, lambda e: e.memset(ap, val), [], [ap])

    def recip(self, out, in_):
        self.add("dve", lambda e: e.reciprocal(out, in_), [in_], [out])

    def dma(self, eng, out, in_, is_out=False):
        self.add(eng, lambda e: e.dma_start(out=out, in_=in_), [in_], [out], dma=True, is_out=is_out)

    def emit_all(self):
        nc = self.nc
        ops = self.ops
        NPOOL = 24
        dsem = [nc.alloc_semaphore("dq%d" % i) for i in range(NPOOL)]
        dcnt = [0] * NPOOL
        dlast = [None] * NPOOL
        nd = 0
        for op in ops:
            if op["dma"]:
                k = nd % NPOOL
                nd += 1
                if dlast[k] is not None:
                    op["deps"].add(dlast[k])
                dcnt[k] += 16
                op["sem"] = dsem[k]
                op["val"] = dcnt[k]
                op["semid"] = ("d", k)
                dlast[k] = op["idx"]
        for op in ops:
            for d in op["deps"]:
                dop = ops[d]
                if dop["dma"]:
                    continue
                if dop["eng"] == "pe" and op["eng"] == "pe" and not op["dma"]:
                    continue
                dop["signal"] = True
        MAXC = 30000
        cur = {}
        for op in ops:
            if op["dma"] or not op["signal"]:
                continue
            e = op["eng"]
            if e not in cur or cur[e][1] >= MAXC:
                gen = 0 if e not in cur else cur[e][2] + 1
                cur[e] = [nc.alloc_semaphore("c_%s_%d" % (e, gen)), 0, gen]
            cur[e][1] += 1
            op["sem"] = cur[e][0]
            op["val"] = cur[e][1]
            op["semid"] = ("c", e, cur[e][2])

        handles = {"pe": "tensor", "dve": "vector", "act": "scalar", "pool": "gpsimd", "sp": "sync"}

        def emit_engine(ename, e):
            waited = {}
            last = None
            for op in ops:
                if op["eng"] != ename:
                    continue
                need = {}
                for d in op["deps"]:
                    dop = ops[d]
                    if (not dop["dma"]) and dop["eng"] == "pe" and ename == "pe" and not op["dma"]:
                        continue
                    sid = dop["semid"]
                    if waited.get(sid, 0) >= dop["val"]:
                        continue
                    if sid not in need or need[sid][1] < dop["val"]:
                        need[sid] = (dop["sem"], dop["val"])
                for sid, (sem, val) in need.items():
                    e.wait_ge(sem, val)
                    waited[sid] = val
                ins = op["emit"](e)
                if op["dma"]:
                    ins.then_inc(op["sem"], 16)
                elif op["signal"]:
                    ins.then_inc(op["sem"], 1)
            if ename == "sp":
                fin = {}
                for d in self.out_dmas:
                    dop = ops[d]
                    sid = dop["semid"]
                    if sid not in fin or fin[sid][1] < dop["val"]:
                        fin[sid] = (dop["sem"], dop["val"])
                for sid, (sem, val) in fin.items():
                    if waited.get(sid, 0) < val:
                        e.wait_ge(sem, val)

        with nc.Block() as block:
            @block.tensor
            def _(e):
                emit_engine("pe", e)

            @block.vector
            def _(e):
                emit_engine("dve", e)

            @block.scalar
            def _(e):
                emit_engine("act", e)

            @block.gpsimd
            def _(e):
                emit_engine("pool", e)

            @block.sync
            def _(e):
                emit_engine("sp", e)


class Arena:
    def __init__(self, t, nbytes):
        self.t = t
        self.nbytes = nbytes
        self.off = 0

    def reset(self):
        self.off = 0

    def alloc(self, shape, dt):
        es = _esz(dt)
        n = 1
        for s in shape[1:]:
            n *= s
        nb = (n * es + 63) // 64 * 64
        assert self.off + nb <= self.nbytes, ("arena overflow", self.off, nb, self.nbytes)
        a = self.t[0:shape[0], self.off // 2:(self.off + n * es) // 2]
        self.off += nb
        if dt != BF16:
            a = a.bitcast(dt)
        if len(shape) == 3:
            a = a.rearrange("p (a b) -> p a b", a=shape[1])
        elif len(shape) == 4:
            a = a.rearrange("p (a b c) -> p a b c", a=shape[1], b=shape[2])
        return a


CST_IDENT = 0
CST_ONES = 128
CST_NEGTRI = 256
CST_STRICT = 384
CST_NEGONES = 512
CST_NB = 640
CST_HGMASK = 640
CST_FOLD = 768
CST_CMASK = 896
CST_INVF = 900
CST_PHASE = 901
CST_SIGN = 902
CST_N = 904

VPL = 58
V_ATTN, V_FFN, V_HGG, V_SBG, V_MLAOG, V_MLAQG, V_MLAKVG, V_LB = 0, 16, 32, 36, 42, 48, 52, 54
V_FINAL = 2 * VPL
V_N = 2 * VPL + 16


def _make_cst():
    c = np.zeros((128, CST_N), np.float32)
    i = np.arange(128)
    c[:, CST_IDENT:CST_IDENT + 128] = np.eye(128, dtype=np.float32)
    c[:, CST_ONES:CST_ONES + 128] = 1.0
    c[:, CST_NEGTRI:CST_NEGTRI + 128] = -(i[:, None] >= i[None, :]).astype(np.float32)
    c[:, CST_STRICT:CST_STRICT + 128] = (i[:, None] < i[None, :]).astype(np.float32)
    c[:, CST_NEGONES:CST_NEGONES + 128] = -1.0
    c[:, CST_HGMASK:CST_HGMASK + 128] = ((i[:, None] // 32 == i[None, :] // 32) & (i[:, None] <= i[None, :])).astype(np.float32)
    c[:, CST_FOLD:CST_FOLD + 128] = (i[:, None] % 64 == i[None, :] % 64).astype(np.float32)
    for k in range(4):
        c[:, CST_CMASK + k] = (i // 32 == k).astype(np.float32)
    inv_freq = (np.float32(10000.0) ** (-np.arange(0, 64, 2, dtype=np.float32) / np.float32(64))).astype(np.float32)
    c[:, CST_INVF] = inv_freq[i % 32]
    c[:, CST_PHASE] = np.where(i < 64, np.float32(math.pi / 2), np.float32(0.0))
    c[:, CST_SIGN] = np.where((i >= 64) & (i < 96), -1.0, 1.0)
    return c


def build_program(debug=False, nlayers=DEPTH, stages=None):
    nc = bass.Bass("TRN2", target_bir_lowering=False)
    sc = Sched(nc)

    def dram_in(name, shape, dt=F32):
        return nc.dram_tensor(name, list(shape), dt, kind="ExternalInput").ap()

    need = set(stages or ("norm", "hg", "sb", "mla", "wo", "ffn", "final"))
    xT_in = dram_in("xT", [D, S])
    pos_in = dram_in("pos", [1, S], I32)
    vec_in = dram_in("vec", [128, V_N])
    cst_in = dram_in("cst", [128, CST_N])
    w_in = dram_in("w_in", [DEPTH, D, IN_COLS]) if need & {"hg", "sb", "mla"} else None
    w_uq = dram_in("mla_w_uq", [DEPTH, 512, 1152]) if "mla" in need else None
    w_ukv = dram_in("mla_w_ukv", [DEPTH, 256, 1536]) if "mla" in need else None
    w_o = dram_in("w_o", [DEPTH, D, D]) if "wo" in need else None
    w_gate = dram_in("w_gate", [DEPTH, D, D_FF]) if "ffn" in need else None
    w_up = dram_in("w_up", [DEPTH, D, D_FF]) if "ffn" in need else None
    w_down = dram_in("w_down", [DEPTH, D_FF, D]) if "ffn" in need else None
    outT = nc.dram_tensor("outT", [D, S], F32, kind="ExternalOutput").ap()
    xres = nc.dram_tensor("xres", [D, S], F32, kind="Internal").ap()
    mixd = nc.dram_tensor("mixd", [D, S], BF16, kind="Internal").ap()
    dbg = {}
    if debug:
        st_ = set(stages or ())
        if "norm" in st_:
            dbg["hT"] = nc.dram_tensor("dbg_hT", [D, S], BF16, kind="ExternalOutput").ap()
        if st_ & {"hg", "sb", "mla"}:
            dbg["mix"] = nc.dram_tensor("dbg_mix", [D, S], BF16, kind="ExternalOutput").ap()
        if "wo" in st_:
            dbg["x1"] = nc.dram_tensor("dbg_x1", [D, S], F32, kind="ExternalOutput").ap()
        if "ffn" in st_:
            dbg["x2"] = nc.dram_tensor("dbg_x2", [D, S], F32, kind="ExternalOutput").ap()
        dbg["cs"] = nc.dram_tensor("dbg_cs", [128, S], F32, kind="ExternalOutput").ap()
        dbg["lbt"] = nc.dram_tensor("dbg_lbt", [128, 32], F32, kind="ExternalOutput").ap()

    HT = nc.alloc_sbuf_tensor("HT", [128, 16, S], BF16)
    ATT_BYTES = 68 * 1024
    ATTt = nc.alloc_sbuf_tensor("ATT", [128, ATT_BYTES // 2], BF16)
    att = Arena(ATTt, ATT_BYTES)
    NSLOT = 3
    WS = [nc.alloc_sbuf_tensor("WS%d" % i, [128, 8192], BF16) for i in range(NSLOT)]
    CS = nc.alloc_sbuf_tensor("CS", [128, S], F32)
    VEC = nc.alloc_sbuf_tensor("VEC", [128, V_N], F32)
    CSTF = nc.alloc_sbuf_tensor("CSTF", [128, CST_N], F32)
    CSTB = nc.alloc_sbuf_tensor("CSTB", [128, CST_NB], BF16)
    LBT = nc.alloc_sbuf_tensor("LBT", [128, 32], F32)

    PS = [nc.alloc_psum_tensor("PS%d" % i, [128, 512], F32) for i in range(7)]
    PSB = nc.alloc_psum_tensor("PSB", [128, 1024], BF16)

    ident_b = CSTB[:, 0:128]
    ones_b = CSTB[:, 128:256]
    negtri_b = CSTB[:, 256:384]
    strict_b = CSTB[:, 384:512]
    negones_b = CSTB[:, 512:640]
    ones_f = CSTF[:, CST_ONES:CST_ONES + 128]
    hgmask_f = CSTF[:, CST_HGMASK:CST_HGMASK + 128]
    fold_f = CSTF[:, CST_FOLD:CST_FOLD + 128]

    def vcol(layer, base, j):
        c = layer * VPL + base + j
        return VEC[:, c:c + 1]

    sc.dma("sp", VEC[:, :], vec_in)
    sc.dma("sp", CSTF[:, :], cst_in)
    sc.dma("pool", CSTB[:, :], cst_in[:, 0:CST_NB])

    r0 = VEC[:, V_LB:V_LB + 4]
    r1 = VEC[:, VPL + V_LB:VPL + V_LB + 4]
    sc.tt("dve", LBT[:, 24:28], r0, r1, ALU.subtract)
    sc.act(LBT[:, 0:4], LBT[:, 24:28], AF.Sigmoid)
    sc.act(LBT[:, 4:8], LBT[:, 24:28], AF.Sigmoid, scale=-1.0)
    sc.tt("dve", LBT[:, 8:12], LBT[:, 0:4], LBT[:, 0:4], ALU.subtract)
    sc.tt("dve", LBT[:, 28:32], LBT[:, 0:4], LBT[:, 4:8], ALU.add)
    sc.tt("dve", LBT[:, 12:16], LBT[:, 28:32], LBT[:, 0:4], ALU.subtract)
    sc.ts("dve", LBT[:, 16:24], LBT[:, 8:16], -1.0, 1.0, ALU.mult, ALU.add)

    def lbcol(layer, h):
        return LBT[:, 8 + 4 * layer + h:9 + 4 * layer + h]

    def omlcol(layer, h):
        return LBT[:, 16 + 4 * layer + h:17 + 4 * layer + h]

    att.reset()
    posi = att.alloc([128, S], I32)
    ang = att.alloc([128, S], F32)
    kq = att.alloc([128, S], F32)
    kqi = att.alloc([128, S], I32)
    sc.dma("sp", posi, pos_in.partition_broadcast(128))
    sc.copy("dve", ang, posi)
    sc.ts("dve", ang, ang, CSTF[:, CST_INVF:CST_INVF + 1], CSTF[:, CST_PHASE:CST_PHASE + 1], ALU.mult, ALU.add)
    sc.ts("dve", kq, ang, float(1.0 / (2 * math.pi)), None, ALU.mult)
    sc.copy("dve", kqi, kq)
    sc.copy("dve", kq, kqi)
    C1 = 6.28125
    C2 = float(2 * math.pi - 6.28125)
    sc.stt(ang, kq, -C1, ang, ALU.mult, ALU.add)
    sc.stt(ang, kq, -C2, ang, ALU.mult, ALU.add)
    sc.ts("dve", kq, ang, float(math.pi), float(-2 * math.pi), ALU.is_gt, ALU.mult)
    sc.tt("dve", ang, ang, kq, ALU.add)
    sc.ts("dve", kq, ang, float(-math.pi), float(2 * math.pi), ALU.is_lt, ALU.mult)
    sc.tt("dve", ang, ang, kq, ALU.add)
    sc.ts("dve", ang, ang, float(math.pi), float(-math.pi), ALU.min, ALU.max)
    sc.act(CS[:, :], ang, AF.Sin)
    sc.ts("dve", CS[:, :], CS[:, :], CSTF[:, CST_SIGN:CST_SIGN + 1], None, ALU.mult)

    if debug:
        sc.dma("sp", dbg["cs"], CS[:, :], is_out=True)
        sc.dma("sp", dbg["lbt"], LBT[:, :], is_out=True)

    jobs = []

    def job(load, compute):
        jobs.append((load, compute))

    def wview(slot, shape):
        n = 1
        for s in shape:
            n *= s
        a = slot[:, 0:n]
        if len(shape) == 2:
            return a.rearrange("p (a b) -> p a b", a=shape[0])
        if len(shape) == 3:
            return a.rearrange("p (a b c) -> p a b c", a=shape[0], b=shape[1])
        return a

    def w_in_cols(layer, a, b):
        return w_in[layer].rearrange("(c p) n -> p c n", p=128)[:, :, a:b]

    def proj_f(ps, wfn, tg, nk=16, rhs_fn=None):
        for c in range(nk):
            rhs = HT[:, c, tg * 512:(tg + 1) * 512] if rhs_fn is None else rhs_fn(c)
            sc.mm(ps, wfn(c), rhs, start=(c == 0), stop=(c == nk - 1))

    def norm_stage(src, gcol_fn, dst_dram=None):
        att.reset()
        NB = 24
        XB = [att.alloc([128, 512], F32) for _ in range(NB)]
        SQ = [att.alloc([128, 512], F32) for _ in range(2)]
        RS = att.alloc([128, 512], F32)
        RINV = att.alloc([128, 512], F32)
        k = 0
        for tg in range(4):
            cols = slice(tg * 512, (tg + 1) * 512)
            st = PS[6]
            xs = []
            for c in range(16):
                xb = XB[k % NB]
                k += 1
                sc.dma("sp", xb, src[c * 128:(c + 1) * 128, cols])
                xs.append(xb)
            for c in range(16):
                sq = SQ[c % 2]
                sc.act(sq, xs[c], AF.Square)
                sc.mm(st[:, :], ones_f, sq, start=(c == 0), stop=(c == 15))
            sc.act(RS, st[:, :], AF.Sqrt, scale=1.0 / D, bias=EPS)
            sc.recip(RINV, RS)
            for c in range(16):
                xb = xs[c]
                if dst_dram is None:
                    sc.stt(HT[:, c, cols], xb, gcol_fn(c), RINV, ALU.mult, ALU.mult)
                else:
                    sc.stt(xb, xb, gcol_fn(c), RINV, ALU.mult, ALU.mult)
                    sc.dma("sp", dst_dram[c * 128:(c + 1) * 128, cols], xb, is_out=True)

    def group_norm_out(o_ap, width, gcol, dst, tmp_sq, tmp_rs, tmp_rinv, st_ps, extra_mul=None, tmp2=None):
        sc.act(tmp_sq[:, 0:width], o_ap, AF.Square)
        sc.mm(st_ps[:, 0:width], ones_f, tmp_sq[:, 0:width], start=True, stop=True)
        sc.act(tmp_rs[:, 0:width], st_ps[:, 0:width], AF.Sqrt, scale=1.0 / 128.0, bias=EPS)
        sc.recip(tmp_rinv[:, 0:width], tmp_rs[:, 0:width])
        if extra_mul is None:
            sc.stt(dst, o_ap, gcol, tmp_rinv[:, 0:width], ALU.mult, ALU.mult)
        else:
            sc.stt(tmp2[:, 0:width], o_ap, gcol, tmp_rinv[:, 0:width], ALU.mult, ALU.mult)
            sc.tt("dve", dst, tmp2[:, 0:width], extra_mul, ALU.mult)

    def hg_stage(layer):
        att.reset()
        T1 = att.alloc([128, S], F32)
        T2 = att.alloc([128, S], F32)
        T3 = att.alloc([128, S], F32)
        SM = att.alloc([128, S], F32)
        Qt = att.alloc([128, S], BF16)
        Kt = att.alloc([128, S], BF16)
        Kh = att.alloc([128, S], BF16)
        V = att.alloc([128, 16, 128], BF16)
        OUT = att.alloc([128, S], BF16)
        EGL = att.alloc([128, 64], F32)
        KM = [att.alloc([128, 4, 128], BF16) for _ in range(2)]
        SC_ = [att.alloc([128, 128], BF16) for _ in range(2)]
        ST = [att.alloc([128, 128], F32) for _ in range(4)]
        STB = [att.alloc([128, 128], BF16) for _ in range(4)]
        TRIB = [att.alloc([128, 128], F32) for _ in range(2)]
        VT = [att.alloc([128, 512], BF16) for _ in range(2)]
        SCF = [att.alloc([128, 128], F32) for _ in range(2)]
        OSB = [att.alloc([128, 128], F32) for _ in range(2)]
        TSQ = att.alloc([128, 128], F32)
        TRS = att.alloc([128, 128], F32)
        TRI = att.alloc([128, 128], F32)
        TSG = att.alloc([128, 128], F32)
        TO = att.alloc([128, 128], F32)
        for h in range(4):
            def load(slot, h=h):
                W = wview(slot, [16, 4, 128])
                for j in range(4):
                    a = j * 512 + h * 128
                    sc.dma("pool", W[:, :, j, :], w_in_cols(layer, a, a + 128))

            def compute(slot, h=h):
                W = wview(slot, [16, 4, 128])
                if h == 0:
                    sc.memset("pool", SM, 1.0)
                    sc.memset("pool", SM.rearrange("p (c t) -> p c t", t=32)[:, :, 0:1], 0.0)
                for tg in range(4):
                    cols = slice(tg * 512, (tg + 1) * 512)
                    ps = PS[tg % 2]
                    proj_f(ps[:, :], lambda c: W[:, c, 1, :], tg)
                    sc.act(T1[:, cols], ps[:, :], AF.Sigmoid)
                    sc.ts("dve", T1[:, cols], T1[:, cols], omlcol(layer, h), lbcol(layer, h), ALU.mult, ALU.add)
                    sc.ts("dve", T2[:, cols], T1[:, cols], -1.0, 1.0, ALU.mult, ALU.add)
                    sc.act(T1[:, cols], T1[:, cols], AF.Ln)
                sc.scan(T3, SM, T1, 0.0, ALU.mult, ALU.add)
                sc.act(T1, T3, AF.Exp)
                sc.copy("dve", EGL, T1.rearrange("p (c t) -> p c t", t=32)[:, :, 31])
                for tg in range(4):
                    cols = slice(tg * 512, (tg + 1) * 512)
                    ps = PS[tg % 2]
                    proj_f(ps[:, :], lambda c: W[:, c, 0, :], tg)
                    sc.tt("dve", Qt[:, cols], ps[:, :], T1[:, cols], ALU.mult)
                sc.act(T1, T3, AF.Exp, scale=-1.0)
                sc.tt("dve", Kt, T2, T1, ALU.mult)
                T3v = T3.rearrange("p (c t) -> p c t", t=32)
                T1v = T1.rearrange("p (c t) -> p c t", t=32)
                sc.tt("dve", T1v, T3v[:, :, 31:32].broadcast_to([128, 64, 32]), T3v, ALU.subtract)
                sc.act(T1, T1, AF.Exp)
                sc.tt("dve", Kh, T2, T1, ALU.mult)
                for tg in range(4):
                    ps = PS[tg % 2]
                    proj_f(ps[:, :], lambda c: W[:, c, 2, :], tg)
                    vt = VT[tg % 2]
                    sc.copy("dve", vt, ps[:, :])
                    for j in range(4):
                        sc.tr(PSB[:, j * 128:(j + 1) * 128], vt[:, j * 128:(j + 1) * 128], ident_b)
                    sc.copy("act", V[:, tg * 4:(tg + 1) * 4, :], PSB[:, 0:512].rearrange("p (a b) -> p a b", a=4))
                for tg in range(4):
                    cols = slice(tg * 512, (tg + 1) * 512)
                    ps = PS[tg % 2]
                    proj_f(ps[:, :], lambda c: W[:, c, 3, :], tg)
                    sc.act(T2[:, cols], ps[:, :], AF.Silu)
                sc.memset("pool", ST[0], 0.0)
                sc.memset("pool", STB[0], 0.0)
                gcol = vcol(layer, V_HGG, h)

                def stF(tt_):
                    tcols = slice(tt_ * 128, (tt_ + 1) * 128)
                    b = tt_ % 2
                    trp = PSB[:, b * 128:b * 128 + 128]
                    sc.tr(trp, Kh[:, tcols], ident_b)
                    for c4 in range(4):
                        cm = CSTF[:, CST_CMASK + c4:CST_CMASK + c4 + 1]
                        sc.act(KM[b][:, c4, :], trp, AF.Copy, scale=cm)
                    sc.mm(PS[2][:, 0:128], Kt[:, tcols], Qt[:, tcols])
                    sc.copy("act", SCF[b], PS[2][:, 0:128])
                    sc.tt("dve", SC_[b], SCF[b], hgmask_f, ALU.mult)
                    kvb = PS[5 + b]
                    for c4 in range(4):
                        sc.mm(kvb[:, c4 * 128:(c4 + 1) * 128], KM[b][:, c4, :], V[:, tt_, :])

                def stR(tt_):
                    b = tt_ % 2
                    op_ = PS[3 + b]
                    kvb = PS[5 + b]
                    sc.mm(op_[:, 0:128], V[:, tt_, :], SC_[b], start=True, stop=False)
                    for c4 in range(4):
                        ch = tt_ * 4 + c4
                        sc.mm(op_[:, c4 * 32:(c4 + 1) * 32], STB[ch % 4], Qt[:, ch * 32:(ch + 1) * 32],
                              start=False, stop=(c4 == 3))
                        sc.stt(ST[(ch + 1) % 4], ST[ch % 4], EGL[:, ch:ch + 1], kvb[:, c4 * 128:(c4 + 1) * 128],
                               ALU.mult, ALU.add)
                        sc.copy("pool", STB[(ch + 1) % 4], ST[(ch + 1) % 4])

                def stN1(tt_):
                    b = tt_ % 2
                    op_ = PS[3 + b]
                    sc.act(TSQ, op_[:, 0:128], AF.Square)
                    sc.copy("act", OSB[b], op_[:, 0:128])
                    sc.mm(PS[2][:, 128:256], ones_f, TSQ)
                    sc.act(TRS, PS[2][:, 128:256], AF.Ln, scale=1.0 / 128.0, bias=EPS)
                    sc.act(TRIB[b], TRS, AF.Exp, scale=-0.5)

                def stN2(tt_):
                    tcols = slice(tt_ * 128, (tt_ + 1) * 128)
                    b = tt_ % 2
                    sc.stt(TO, OSB[b], gcol, TRIB[b], ALU.mult, ALU.mult)
                    sc.tt("dve", OUT[:, tcols], TO, T2[:, tcols], ALU.mult)

                stF(0)
                stF(1)
                for tt_ in range(16):
                    stR(tt_)
                    if tt_ + 2 < 16:
                        stF(tt_ + 2)
                    if tt_ >= 2:
                        stN2(tt_ - 2)
                    if tt_ >= 1:
                        stN1(tt_ - 1)
                stN2(14)
                stN1(15)
                stN2(15)
                sc.dma("sp", mixd[h * 128:(h + 1) * 128, :], OUT)

            job(load, compute)

    def sb_stage(layer):
        st = {}
        scale = 128.0 ** -0.5
        PSTAT = PSB[:, :].bitcast(F32)

        def alloc_all():
            att.reset()
            st["QT"] = [att.alloc([128, S], BF16) for _ in range(2)]
            st["KT"] = [att.alloc([128, S], BF16) for _ in range(2)]
            st["V"] = [att.alloc([128, 16, 128], BF16) for _ in range(2)]
            st["OUT"] = [att.alloc([128, S], BF16) for _ in range(2)]
            st["E"] = [att.alloc([128, 512], F32) for _ in range(3)]
            st["AX"] = [att.alloc([128, 512], F32) for _ in range(2)]
            st["SPb"] = [att.alloc([128, 512], BF16) for _ in range(3)]
            st["LL"] = [[att.alloc([128, 512], BF16) for _ in range(3)] for _ in range(2)]
            st["A"] = [att.alloc([128, 512], BF16) for _ in range(2)]
            st["AD"] = [None] + [att.alloc([128, 512], BF16) for _ in range(3)]
            st["TSQ"] = att.alloc([128, 512], F32)
            st["TRS"] = att.alloc([128, 512], F32)
            st["TRI"] = att.alloc([128, 512], F32)
            st["VT"] = [att.alloc([128, 512], BF16) for _ in range(2)]
            for i in range(1, 4):
                sc.memset("pool", st["AD"][i][:, 0:i * 128], 0.0)

        def proj_units(slot, hs):
            W = wview(slot, [16, 3, 128])
            QT, KT, V = st["QT"][hs], st["KT"][hs], st["V"][hs]
            units = []

            def piece(ps, j, tg, q, fin):
                def f():
                    for c in range(4 * q, 4 * q + 4):
                        sc.mm(ps[:, :], W[:, c, j, :], HT[:, c, tg * 512:(tg + 1) * 512], start=(c == 0), stop=(c == 15))
                    if q == 3:
                        fin()
                return f

            for tg in range(4):
                cols = slice(tg * 512, (tg + 1) * 512)

                def fq(cols=cols):
                    sc.ts("dve", QT[:, cols], PS[1][:, :], scale, None, ALU.mult)

                def fk(cols=cols):
                    sc.copy("dve", KT[:, cols], PS[1][:, :])

                def fv(tg=tg):
                    vt = st["VT"][tg % 2]
                    sc.copy("dve", vt, PS[0][:, :])
                    for j in range(4):
                        sc.tr(PSB[:, j * 128:(j + 1) * 128], vt[:, j * 128:(j + 1) * 128], ident_b)
                    sc.copy("dve", V[:, tg * 4:(tg + 1) * 4, :], PSB[:, 0:512].rearrange("p (a b) -> p a b", a=4))

                for q in range(4):
                    units.append(piece(PS[1], 0, tg, q, fq))
                for q in range(4):
                    units.append(piece(PS[1], 1, tg, q, fk))
                for q in range(4):
                    units.append(piece(PS[0], 2, tg, q, fv))
            return units

        def attn(h, units):
            hs = h % 2
            QT, KT, V, OUT = st["QT"][hs], st["KT"][hs], st["V"][hs], st["OUT"][hs]
            E, SPb, LL, A, AD = st["E"], st["SPb"], st["LL"], st["A"], st["AD"]
            pairs = []
            for qg in range(4):
                nkb = 4 * qg + 4
                for n, kb in enumerate(range(nkb - 1, -1, -1)):
                    i = kb - 4 * qg
                    pairs.append(dict(qg=qg, n=n, kb=kb, i=i, c0=(i * 128 if i > 0 else 0), diag=(i >= 0),
                                      nkb=nkb, idx=len(pairs)))
            ot = PS[6]

            def stA(p):
                idx, c0, qg, n, kb = p["idx"], p["c0"], p["qg"], p["n"], p["kb"]
                if n == 0:
                    for b_ in LL[qg % 2]:
                        sc.memset("pool", b_, 0.0)
                zp = PS[2 + idx % 2]
                e_ = E[idx % 3]
                sp = SPb[idx % 3]
                Lcur = LL[qg % 2][n % 3]
                Lnext = LL[qg % 2][(n + 1) % 3]
                q0 = qg * 512 + c0
                q1 = (qg + 1) * 512
                kcols = slice(kb * 128, (kb + 1) * 128)
                sc.mm(zp[:, c0:512], KT[:, kcols], QT[:, q0:q1])
                sc.act(e_[:, c0:512], zp[:, c0:512], AF.Exp)
                sc.act(sp[:, c0:512], e_[:, c0:512], AF.Ln, bias=1.0)
                if p["diag"]:
                    sc.tt("dve", sp[:, c0:c0 + 128], sp[:, c0:c0 + 128], strict_b, ALU.mult)
                if n + 1 < p["nkb"]:
                    sc.tt("pool", Lnext[:, c0:512], Lcur[:, c0:512], sp[:, c0:512], ALU.add)

            def stB(p):
                idx, c0, qg, n, kb, i = p["idx"], p["c0"], p["qg"], p["n"], p["kb"], p["i"]
                cp = PS[4 + idx % 2]
                sp = SPb[idx % 3]
                Lcur = LL[qg % 2][n % 3]
                a_ = AD[i] if i > 0 else A[idx % 2]
                q0 = qg * 512 + c0
                q1 = (qg + 1) * 512
                kcols = slice(kb * 128, (kb + 1) * 128)
                sc.mm(cp[:, c0:512], negtri_b, sp[:, c0:512], start=True, stop=(n == 0))
                if n > 0:
                    sc.mm(cp[:, c0:512], negones_b, Lcur[:, c0:512], start=False, stop=True)
                ax = st["AX"][idx % 2]
                sc.act(ax[:, c0:512], cp[:, c0:512], AF.Exp)
                sc.tt("dve", a_[:, c0:512], ax[:, c0:512], E[idx % 3][:, c0:512], ALU.mult)
                if p["diag"]:
                    sc.tt("dve", a_[:, c0:c0 + 128], a_[:, c0:c0 + 128], strict_b, ALU.mult)

            def stC(p):
                idx, qg, n, kb, i = p["idx"], p["qg"], p["n"], p["kb"], p["i"]
                a_ = AD[i] if i > 0 else A[idx % 2]
                sc.mm(ot[:, :], V[:, kb, :], a_[:, :], start=(n == 0), stop=(n == p["nkb"] - 1))
                if n == p["nkb"] - 1:
                    group_norm_out(ot[:, :], 512, vcol(layer, V_SBG, h), OUT[:, qg * 512:(qg + 1) * 512],
                                   st["TSQ"], st["TRS"], st["TRI"], PSTAT)

            units = list(units)
            nunits0 = len(units)
            NP_ = len(pairs)
            stA(pairs[0])
            stA(pairs[1])
            stB(pairs[0])
            for k_ in range(NP_):
                if k_ + 2 < NP_:
                    stA(pairs[k_ + 2])
                if k_ + 1 < NP_:
                    stB(pairs[k_ + 1])
                stC(pairs[k_])
                want = (k_ + 1) * nunits0 / 36.0
                while units and (nunits0 - len(units)) < want:
                    units.pop(0)()
            while units:
                units.pop(0)()
            sc.dma("sp", mixd[(4 + h) * 128:(5 + h) * 128, :], OUT)

        def wload(hh):
            def load(slot):
                W = wview(slot, [16, 3, 128])
                for j in range(3):
                    a = 2048 + j * 768 + hh * 128
                    sc.dma("pool", W[:, :, j, :], w_in_cols(layer, a, a + 128))
            return load

        def pre(slot):
            alloc_all()
            for u in proj_units(slot, 0):
                u()

        job(wload(0), pre)
        for h in range(6):
            if h < 5:
                job(wload(h + 1), lambda slot, h=h: attn(h, proj_units(slot, (h + 1) % 2)))
            else:
                job(None, lambda slot, h=h: attn(h, []))

    def mla_stage(layer):
        att.reset()
        CQN = att.alloc([128, 4, S], BF16)
        KVN = att.alloc([128, 2, S], BF16)
        KR2 = att.alloc([128, S], BF16)
        mla_base = att.off
        uq_v = w_uq[layer].rearrange("(c p) (h e) -> p c h e", p=128, e=192)
        ukv_v = w_ukv[layer].rearrange("(c p) (h e) -> p c h e", p=128, e=256)

        def load_q(slot):
            sc.dma("pool", wview(slot, [16, 512]), w_in_cols(layer, 4352, 4864))

        import os as _os

        def comp_q(slot):
            att.off = mla_base
            W = wview(slot, [16, 512])
            CQ = att.alloc([128, 4, 512], F32)
            TSQ = att.alloc([128, 512], F32)
            TRS = att.alloc([128, 512], F32)
            TRI = att.alloc([128, 512], F32)
            CUT = int(_os.environ.get("MLA_CUT", "9"))
            for tg in range(4):
                cols = slice(tg * 512, (tg + 1) * 512)
                for j in range(4):
                    ps = PS[j % 2]
                    proj_f(ps[:, :], lambda c: W[:, c, j * 128:(j + 1) * 128], tg)
                    sc.copy("dve", CQ[:, j, :], ps[:, :])
                    if CUT >= 2:
                        sc.act(TSQ, CQ[:, j, :], AF.Square)
                        sc.mm(PS[6][:, :], ones_f, TSQ, start=(j == 0), stop=(j == 3))
                if CUT >= 3:
                    sc.act(TRS, PS[6][:, :], AF.Sqrt, scale=1.0 / 512.0, bias=EPS)
                    sc.recip(TRI, TRS)
                if CUT >= 4:
                    for j in range(4):
                        sc.stt(CQN[:, j, cols], CQ[:, j, :], vcol(layer, V_MLAQG, j), TRI, ALU.mult, ALU.mult)

        import os as _os
        if 'q' in _os.environ.get('MLA_PRE', 'qk'):
            job(load_q, comp_q)

        def load_kv(slot):
            W = wview(slot, [16, 384])
            sc.dma("pool", W[:, :, 0:256], w_in_cols(layer, 4864, 5120))
            sc.dma("pool", W[:, :, 256:320], w_in_cols(layer, 5120, 5184))
            sc.dma("pool", W[:, :, 320:352], w_in_cols(layer, 5152, 5184))
            sc.dma("pool", W[:, :, 352:384], w_in_cols(layer, 5120, 5152))

        def comp_kv(slot):
            att.off = mla_base
            W = wview(slot, [16, 384])
            CK = att.alloc([128, 2, 512], F32)
            TSQ = att.alloc([128, 512], F32)
            TRS = att.alloc([128, 512], F32)
            TRI = att.alloc([128, 512], F32)
            KRT = att.alloc([128, 512], F32)
            for tg in range(4):
                cols = slice(tg * 512, (tg + 1) * 512)
                for j in range(2):
                    ps = PS[j % 2]
                    proj_f(ps[:, :], lambda c: W[:, c, j * 128:(j + 1) * 128], tg)
                    sc.copy("dve", CK[:, j, :], ps[:, :])
                    sc.act(TSQ, CK[:, j, :], AF.Square)
                    sc.mm(PS[6][:, :], ones_f, TSQ, start=(j == 0), stop=(j == 1))
                sc.act(TRS, PS[6][:, :], AF.Sqrt, scale=1.0 / 256.0, bias=EPS)
                sc.recip(TRI, TRS)
                for j in range(2):
                    sc.stt(KVN[:, j, cols], CK[:, j, :], vcol(layer, V_MLAKVG, j), TRI, ALU.mult, ALU.mult)
                ps = PS[2]
                proj_f(ps[:, :], lambda c: W[:, c, 256:384], tg)
                sc.tt("dve", KRT, ps[:, :], CS[:, cols], ALU.mult)
                sc.mm(PS[3][:, :], fold_f, KRT)
                sc.copy("act", KR2[:, cols], PS[3][:, :])

        if 'k' in _os.environ.get('MLA_PRE', 'qk'):
            job(load_kv, comp_kv)

        for h in range(int(_os.environ.get("MLA_HEADS", "6"))):
            def load(slot, h=h):
                Wq = wview(slot, [4, 256])
                Wk = slot[:, 1024:1024 + 512].rearrange("p (a b) -> p a b", a=2)
                sc.dma("pool", Wq[:, :, 0:128], uq_v[:, :, h, 0:128])
                sc.dma("pool", Wq[:, :, 128:192], uq_v[:, :, h, 128:192])
                sc.dma("pool", Wq[:, :, 192:224], uq_v[:, :, h, 160:192])
                sc.dma("pool", Wq[:, :, 224:256], uq_v[:, :, h, 128:160])
                sc.dma("pool", Wk, ukv_v[:, :, h, :])

            def compute(slot, h=h):
                att.off = mla_base
                Wq = wview(slot, [4, 256])
                Wk = slot[:, 1024:1024 + 512].rearrange("p (a b) -> p a b", a=2)
                QN = att.alloc([128, S], BF16)
                QR = att.alloc([128, S], BF16)
                KN = att.alloc([128, S], BF16)
                V = att.alloc([128, 16, 128], BF16)
                OUT = att.alloc([128, S], BF16)
                A = [att.alloc([128, 512], BF16) for _ in range(3)]
                AD = [None] + [att.alloc([128, 512], BF16) for _ in range(3)]
                TO = att.alloc([128, 512], F32)
                TSQ = att.alloc([128, 512], F32)
                TRS = att.alloc([128, 512], F32)
                TRI = att.alloc([128, 512], F32)
                for i in range(1, 4):
                    sc.memset("pool", AD[i][:, 0:i * 128], 0.0)
                for tg in range(4):
                    cols = slice(tg * 512, (tg + 1) * 512)
                    ps = PS[0]
                    proj_f(ps[:, :], lambda c: Wq[:, c, 0:128], tg, nk=4, rhs_fn=lambda c: CQN[:, c, cols])
                    sc.copy("act", QN[:, cols], ps[:, :])
                    ps = PS[1]
                    proj_f(ps[:, :], lambda c: Wq[:, c, 128:256], tg, nk=4, rhs_fn=lambda c: CQN[:, c, cols])
                    sc.tt("dve", QR[:, cols], ps[:, :], CS[:, cols], ALU.mult)
                    ps = PS[2]
                    proj_f(ps[:, :], lambda c: Wk[:, c, 0:128], tg, nk=2, rhs_fn=lambda c: KVN[:, c, cols])
                    sc.copy("act", KN[:, cols], ps[:, :])
                for tq in range(4):
                    ps = PS[tq % 2]
                    for j in range(4):
                        tt_ = tq * 4 + j
                        for c in range(2):
                            sc.mm(ps[:, j * 128:(j + 1) * 128], KVN[:, c, tt_ * 128:(tt_ + 1) * 128], Wk[:, c, 128:256],
                                  start=(c == 0), stop=(c == 1))
                    sc.copy("dve", V[:, tq * 4:(tq + 1) * 4, :], ps[:, :].rearrange("p (a b) -> p a b", a=4))
                scale = 192.0 ** -0.5
                pairs = []
                for qg in range(4):
                    nkb = 4 * qg + 4
                    for kb in range(nkb):
                        i = kb - 4 * qg
                        pairs.append(dict(qg=qg, kb=kb, i=i, c0=(i * 128 if i > 0 else 0), diag=(i >= 0),
                                          nkb=nkb, idx=len(pairs)))
                ot = PS[5]
                den = PS[6]

                def abuf(p):
                    return AD[p["i"]] if p["i"] > 0 else A[p["idx"] % 3]

                SPBANK = [PS[1], PS[2], PS[3]]

                def stA(p):
                    idx, c0, qg, kb = p["idx"], p["c0"], p["qg"], p["kb"]
                    sp_ = SPBANK[idx % 3]
                    a_ = abuf(p)
                    q0 = qg * 512 + c0
                    q1 = (qg + 1) * 512
                    kcols = slice(kb * 128, (kb + 1) * 128)
                    sc.mm(sp_[:, c0:512], KN[:, kcols], QN[:, q0:q1], start=True, stop=False)
                    sc.mm(sp_[:, c0:512], KR2[:, kcols], QR[:, q0:q1], start=False, stop=True)
                    sc.act(a_[:, c0:512], sp_[:, c0:512], AF.Exp, scale=scale)
                    if p["diag"]:
                        sc.memset("pool", a_[64:128, c0:c0 + 64], 0.0)

                def stB(p):
                    qg, kb = p["qg"], p["kb"]
                    a_ = abuf(p)
                    sc.mm(ot[:, :], V[:, kb, :], a_[:, :], start=(kb == 0), stop=(kb == p["nkb"] - 1))
                    sc.mm(den[:, :], ones_b, a_[:, :], start=(kb == 0), stop=(kb == p["nkb"] - 1))
                    if kb == p["nkb"] - 1:
                        sc.recip(TRI, den[:, :])
                        sc.tt("dve", TO, ot[:, :], TRI, ALU.mult)
                        group_norm_out(TO, 512, vcol(layer, V_MLAOG, h), OUT[:, qg * 512:(qg + 1) * 512],
                                       TSQ, TRS, TRI, PS[4])

                NP_ = len(pairs)
                stA(pairs[0])
                stA(pairs[1])
                for k_ in range(NP_):
                    if k_ + 2 < NP_:
                        stA(pairs[k_ + 2])
                    stB(pairs[k_])
                sc.dma("sp", mixd[(10 + h) * 128:(11 + h) * 128, :], OUT)

            job(load, compute)

    def wo_stage(layer, src):
        state = {}
        mixv = mixd.rearrange("(c p) n -> p c n", p=128)

        for dcg in range(4):
            def load(slot, dcg=dcg):
                sc.dma("pool", wview(slot, [16, 512]),
                       w_o[layer].rearrange("(c p) n -> p c n", p=128)[:, :, dcg * 512:(dcg + 1) * 512])

            def compute(slot, dcg=dcg):
                if dcg == 0:
                    att.reset()
                    state["MX"] = [att.alloc([128, 16, 512], BF16) for _ in range(2)]
                    state["XB"] = [att.alloc([128, 512], F32) for _ in range(8)]
                    state["k"] = 0
                    sc.dma("sp", state["MX"][0], mixv[:, :, 0:512])
                MX = state["MX"]
                XB = state["XB"]
                W = wview(slot, [16, 512])
                for tg in range(4):
                    cols = slice(tg * 512, (tg + 1) * 512)
                    n = dcg * 4 + tg
                    mx = MX[n % 2]
                    xbs = []
                    for j in range(4):
                        dc = dcg * 4 + j
                        xb = XB[state["k"] % 8]
                        state["k"] += 1
                        sc.dma("sp", xb, src[dc * 128:(dc + 1) * 128, cols])
                        xbs.append(xb)
                    if n + 1 < 16:
                        tg2 = (tg + 1) % 4
                        sc.dma("sp", MX[(n + 1) % 2], mixv[:, :, tg2 * 512:(tg2 + 1) * 512])
                    for j in range(4):
                        dc = dcg * 4 + j
                        ps = PS[j % 4]
                        for mc in range(16):
                            sc.mm(ps[:, :], W[:, mc, j * 128:(j + 1) * 128], mx[:, mc, :], start=(mc == 0), stop=(mc == 15))
                        sc.tt("dve", xbs[j], ps[:, :], xbs[j], ALU.add)
                        sc.dma("sp", xres[dc * 128:(dc + 1) * 128, cols], xbs[j])

            job(load, compute)

    def ffn_stage(layer):
        state = {}
        wg = w_gate[layer].rearrange("(c p) n -> p c n", p=128)
        wu = w_up[layer].rearrange("(c p) n -> p c n", p=128)
        wd = w_down[layer].rearrange("(c p) n -> p c n", p=128)
        for tt_ in range(4):
            cols = slice(tt_ * 512, (tt_ + 1) * 512)
            for fcp in range(22):
                def load(slot, fcp=fcp):
                    W = wview(slot, [16, 2, 256])
                    sc.dma("pool", W[:, :, 0, :], wg[:, :, fcp * 256:(fcp + 1) * 256])
                    sc.dma("pool", W[:, :, 1, :], wu[:, :, fcp * 256:(fcp + 1) * 256])

                def compute(slot, fcp=fcp, tt_=tt_, cols=cols):
                    if fcp == 0 and tt_ == 0:
                        att.reset()
                        state["ACT"] = att.alloc([128, NFC, 512], BF16)
                        state["SG"] = [att.alloc([128, 512], F32) for _ in range(2)]
                        state["XB"] = [att.alloc([128, 512], F32) for _ in range(8)]
                        state["k"] = 0
                    W = wview(slot, [16, 2, 256])
                    for j in range(2):
                        fc = fcp * 2 + j
                        pg = PS[j * 2]
                        pu = PS[j * 2 + 1]
                        proj_f(pg[:, :], lambda c: W[:, c, 0, j * 128:(j + 1) * 128], tt_)
                        proj_f(pu[:, :], lambda c: W[:, c, 1, j * 128:(j + 1) * 128], tt_)
                        sg = state["SG"][fc % 2]
                        sc.act(sg, pg[:, :], AF.Silu)
                        sc.tt("dve", state["ACT"][:, fc, :], pu[:, :], sg, ALU.mult)

                job(load, compute)
            for dcp in range(8):
                for half in range(2):
                    def load(slot, dcp=dcp, half=half):
                        sc.dma("pool", wview(slot, [22, 256]), wd[:, half * 22:(half + 1) * 22, dcp * 256:(dcp + 1) * 256])

                    def compute(slot, dcp=dcp, half=half, cols=cols):
                        W = wview(slot, [22, 256])
                        if half == 0:
                            state["xbs"] = []
                            for j in range(2):
                                dc = dcp * 2 + j
                                xb = state["XB"][state["k"] % 8]
                                state["k"] += 1
                                sc.dma("sp", xb, xres[dc * 128:(dc + 1) * 128, cols])
                                state["xbs"].append(xb)
                        for j in range(2):
                            ps = PS[4 + j]
                            for f in range(22):
                                fc = half * 22 + f
                                sc.mm(ps[:, :], W[:, f, j * 128:(j + 1) * 128], state["ACT"][:, fc, :],
                                      start=(fc == 0), stop=(fc == NFC - 1))
                        if half == 1:
                            for j in range(2):
                                dc = dcp * 2 + j
                                ps = PS[4 + j]
                                xb = state["xbs"][j]
                                sc.tt("dve", xb, ps[:, :], xb, ALU.add)
                                sc.dma("sp", xres[dc * 128:(dc + 1) * 128, cols], xb)

                    job(load, compute)

    def plain(fn):
        job(None, lambda slot: fn())

    stages = stages or ("norm", "hg", "sb", "mla", "wo", "ffn", "final")
    for layer in range(nlayers):
        src = xT_in if layer == 0 else xres
        if "norm" in stages:
            plain(lambda layer=layer, src=src: norm_stage(src, lambda c: vcol(layer, V_ATTN, c)))
            if debug and layer == 0 and "hT" in dbg:
                plain(lambda: sc.dma("sp", dbg["hT"].rearrange("(c p) n -> p c n", p=128), HT[:, :, :], is_out=True))
        if "hg" in stages:
            hg_stage(layer)
        if "sb" in stages:
            sb_stage(layer)
        if "mla" in stages:
            mla_stage(layer)
        if debug and layer == 0 and "mix" in dbg:
            def dump_mix():
                att.reset()
                t = att.alloc([128, 16, 512], BF16)
                for tg in range(4):
                    sc.dma("sp", t, mixd.rearrange("(c p) n -> p c n", p=128)[:, :, tg * 512:(tg + 1) * 512])
                    sc.dma("sp", dbg["mix"].rearrange("(c p) n -> p c n", p=128)[:, :, tg * 512:(tg + 1) * 512], t, is_out=True)
            plain(dump_mix)
        if "wo" in stages:
            wo_stage(layer, src)

        def dump_x(key):
            att.reset()
            t = att.alloc([128, 2048], F32)
            for c in range(16):
                sc.dma("sp", t, xres[c * 128:(c + 1) * 128, :])
                sc.dma("sp", dbg[key][c * 128:(c + 1) * 128, :], t, is_out=True)
        if debug and layer == 0 and "x1" in dbg:
            plain(lambda: dump_x("x1"))
        if "ffn" in stages:
            plain(lambda layer=layer: norm_stage(xres, lambda c: vcol(layer, V_FFN, c)))
            ffn_stage(layer)
        if debug and layer == 0 and "x2" in dbg:
            plain(lambda: dump_x("x2"))
    if "final" in stages:
        plain(lambda: norm_stage(xres, lambda c: VEC[:, V_FINAL + c:V_FINAL + c + 1], dst_dram=outT))

    wjobs = [k for k, (ld, _) in enumerate(jobs) if ld is not None]
    slot_of = {k: WS[n % NSLOT][:, :] for n, k in enumerate(wjobs)}
    issued = [0]

    def issue_upto(n):
        while issued[0] < min(n, len(wjobs)):
            k = wjobs[issued[0]]
            jobs[k][0](slot_of[k])
            issued[0] += 1

    nw = 0
    for k, (ld, comp) in enumerate(jobs):
        if ld is not None:
            issue_upto(nw + NSLOT)
            nw += 1
        else:
            issue_upto(nw + NSLOT - 1)
        comp(slot_of.get(k))

    sc.emit_all()
    nc._in_names = {"xT", "pos", "vec", "cst"} | ({"w_in"} if w_in is not None else set()) | \
        ({"mla_w_uq", "mla_w_ukv"} if w_uq is not None else set()) | ({"w_o"} if w_o is not None else set()) | \
        ({"w_gate", "w_up", "w_down"} if w_gate is not None else set())
    nc._nops = len(sc.ops)
    return nc


_PROG_CACHE = {}


def _get_prog(debug=False, nlayers=DEPTH, stages=None):
    key = (debug, nlayers, stages)
    if key not in _PROG_CACHE:
        _PROG_CACHE[key] = build_program(debug=debug, nlayers=nlayers, stages=stages)
    return _PROG_CACHE[key]


def _pack_vec(inp):
    v = np.zeros((128, V_N), np.float32)

    def cols(a):
        a = np.asarray(a, np.float32)
        return a.reshape(-1, 128).T

    for l in range(DEPTH):
        b = l * VPL
        v[:, b + V_ATTN:b + V_ATTN + 16] = cols(inp["attn_norm_g"][l])
        v[:, b + V_FFN:b + V_FFN + 16] = cols(inp["ffn_norm_g"][l])
        v[:, b + V_HGG:b + V_HGG + 4] = cols(inp["hg_norm_g"][l])
        v[:, b + V_SBG:b + V_SBG + 6] = cols(inp["sb_norm_g"][l])
        v[:, b + V_MLAOG:b + V_MLAOG + 6] = cols(inp["mla_out_norm_g"][l])
        v[:, b + V_MLAQG:b + V_MLAQG + 4] = cols(inp["mla_q_norm_g"][l])
        v[:, b + V_MLAKVG:b + V_MLAKVG + 2] = cols(inp["mla_kv_norm_g"][l])
        v[:, b + V_LB:b + V_LB + 4] = cols(inp["hg_lower_bounds"][l])
    v[:, V_FINAL:V_FINAL + 16] = cols(inp["final_norm_g"])
    return v


def _in_maps(inp, names=None, ncores=NCORES):
    x = np.asarray(inp["x"], np.float32)
    pos = np.asarray(inp["positions"], np.int32)
    vec = _pack_vec(inp)
    cst = _make_cst()
    shared = {
        "vec": vec, "cst": cst,
        "w_in": np.ascontiguousarray(inp["w_in"], dtype=np.float32),
        "mla_w_uq": np.ascontiguousarray(inp["mla_w_uq"], dtype=np.float32),
        "mla_w_ukv": np.ascontiguousarray(inp["mla_w_ukv"], dtype=np.float32),
        "w_o": np.ascontiguousarray(inp["w_o"], dtype=np.float32),
        "w_gate": np.ascontiguousarray(inp["w_gate"], dtype=np.float32),
        "w_up": np.ascontiguousarray(inp["w_up"], dtype=np.float32),
        "w_down": np.ascontiguousarray(inp["w_down"], dtype=np.float32),
    }
    if names is not None:
        shared = {k: v for k, v in shared.items() if k in names}
    maps = []
    for b in range(ncores):
        m = dict(shared)
        m["xT"] = np.ascontiguousarray(x[b].T)
        m["pos"] = np.ascontiguousarray(pos[b:b + 1])
        maps.append(m)
    return maps


def kernel(**inputs):
    nc = _get_prog()
    res = run_bass_kernel_spmd(nc, _in_maps(inputs), core_ids=list(range(NCORES)))
    out = np.stack([np.ascontiguousarray(np.asarray(r["outT"], np.float32).T) for r in res.results], axis=0)
    return out
```
